# Optimizing a Trainium2 kernel written in Bass

```python
import math
import jax, jax.numpy as jnp
from jax import lax
import numpy as np

D_MODEL = 1024
BATCH = 8
SEQ = 2048
DEPTH = 2

PLE_DIM = 256
HEAD_DIM = 64
N_HEADS_A = 8
N_HEADS_B = 8
N_KV_B = 2
DILATED_PATTERNS = ((128, 1), (512, 4), (2048, 16))
WINDOW_B = 128
NUM_BUCKETS = 32
MAX_DISTANCE = 1024
D_FF = ((8 * D_MODEL + 3 * 256 - 1) // (3 * 256)) * 256
RMS_EPS = 1e-6
NEG_INF = -1e30
WIDTH_A = N_HEADS_A * HEAD_DIM
WIDTH_BQ = N_HEADS_B * HEAD_DIM
WIDTH_BKV = N_KV_B * HEAD_DIM
IN_SPLITS = (WIDTH_A, WIDTH_A, WIDTH_A, WIDTH_BQ, WIDTH_BKV, WIDTH_BKV, D_MODEL, D_MODEL)
D_IN = sum(IN_SPLITS)

kernel_name = "hybrid_dilated_window_gqa_encoder"


def rmsnorm(x, g):
    xf = x.astype(jnp.float32)
    y = xf * lax.rsqrt(jnp.mean(xf * xf, axis=-1, keepdims=True) + RMS_EPS)
    return (y * g.astype(jnp.float32)).astype(x.dtype)


def t5_bucket(rel):
    half_b = NUM_BUCKETS // 2
    max_exact = half_b // 2
    sign = jnp.where(rel > 0, half_b, 0)
    n = jnp.abs(rel)
    nf = jnp.maximum(n, 1).astype(jnp.float32)
    large = max_exact + (jnp.log(nf / max_exact) / math.log(MAX_DISTANCE / max_exact)
                         * (half_b - max_exact)).astype(jnp.int32)
    large = jnp.minimum(large, half_b - 1)
    return sign + jnp.where(n < max_exact, n, large)


def band_rel(blk):
    i = jnp.arange(blk, dtype=jnp.int32)[:, None]
    j = jnp.arange(3 * blk, dtype=jnp.int32)[None, :]
    return j - blk - i


def rel_bias(table, blk, dilation):
    buckets = t5_bucket(band_rel(blk) * dilation)
    return jnp.transpose(table[buckets], (2, 0, 1))


def banded_attention(q, k, v, bias, half, blk, sink=None):
    N, L, Hq, hd = q.shape
    Hk = k.shape[2]
    G = Hq // Hk
    nb = -(-L // blk)
    Lp = nb * blk
    qp = jnp.pad(q, ((0, 0), (0, Lp - L), (0, 0), (0, 0)))
    kp = jnp.pad(k, ((0, 0), (blk, Lp - L + blk), (0, 0), (0, 0))).reshape(N, nb + 2, blk, Hk, hd)
    vp = jnp.pad(v, ((0, 0), (blk, Lp - L + blk), (0, 0), (0, 0))).reshape(N, nb + 2, blk, Hk, hd)
    kw = jnp.concatenate([kp[:, :-2], kp[:, 1:-1], kp[:, 2:]], axis=2)
    vw = jnp.concatenate([vp[:, :-2], vp[:, 1:-1], vp[:, 2:]], axis=2)
    qb = qp.reshape(N, nb, blk, Hk, G, hd)
    s = jnp.einsum('nbqkgd,nbjkd->nbkgqj', qb, kw).astype(jnp.float32) * (hd ** -0.5)
    s = s + bias.reshape(Hk, G, blk, 3 * blk).astype(jnp.float32)[None, None]
    rel = band_rel(blk)
    key_pos = (jnp.arange(nb, dtype=jnp.int32)[:, None, None] * blk
               + (jnp.arange(3 * blk, dtype=jnp.int32) - blk)[None, None, :])
    mask = (jnp.abs(rel) <= half)[None] & (key_pos >= 0) & (key_pos < L)
    s = jnp.where(mask[None, :, None, None], s, NEG_INF)
    m = jnp.max(s, axis=-1)
    if sink is not None:
        sk = sink.astype(jnp.float32).reshape(Hk, G)[None, None, :, :, None]
        m = jnp.maximum(m, sk)
    pr = jnp.exp(s - m[..., None])
    den = jnp.sum(pr, axis=-1)
    if sink is not None:
        den = den + jnp.exp(sk - m)
    pr = (pr / den[..., None]).astype(v.dtype)
    o = jnp.einsum('nbkgqj,nbjkd->nbqkgd', pr, vw).reshape(N, Lp, Hq, hd)[:, :L]
    lse = jnp.transpose(m + jnp.log(den), (0, 1, 4, 2, 3)).reshape(N, Lp, Hq)[:, :L]
    return o, lse


def dilated_mixer(q, k, v, biases):
    B_, S_, H, hd = q.shape
    outs, lses = [], []
    for (w, d), bias in zip(DILATED_PATTERNS, biases):
        half = w // (2 * d)

        def to_sub(t):
            return t.reshape(B_, S_ // d, d, H, hd).transpose(0, 2, 1, 3, 4).reshape(B_ * d, S_ // d, H, hd)

        o, lse = banded_attention(to_sub(q), to_sub(k), to_sub(v), bias, half, half)
        outs.append(o.reshape(B_, d, S_ // d, H, hd).transpose(0, 2, 1, 3, 4).reshape(B_, S_, H, hd))
        lses.append(lse.reshape(B_, d, S_ // d, H).transpose(0, 2, 1, 3).reshape(B_, S_, H))
    wts = jax.nn.softmax(jnp.stack(lses, axis=0), axis=0)
    return jnp.einsum('pbsh,pbshd->bshd', wts.astype(q.dtype), jnp.stack(outs, axis=0))


def setup_inputs(seed: int = 0) -> dict:
    key = jax.random.key(seed)
    ks = jax.random.split(key, 24)
    f32 = jnp.float32

    def nrm(k_, shape, scale):
        return jax.random.normal(k_, shape, f32) * scale

    def gain(k_, shape):
        return 1.0 + 0.05 * jax.random.normal(k_, shape, f32)

    return {
        "x": nrm(ks[0], (BATCH, SEQ, D_MODEL), 1.0),
        "p": nrm(ks[1], (DEPTH, BATCH, SEQ, PLE_DIM), 1.0),
        "rel_table": nrm(ks[2], (NUM_BUCKETS, N_HEADS_A + N_HEADS_B), 0.1),
        "norm_mix_g": gain(ks[3], (DEPTH, D_MODEL)),
        "w_in": nrm(ks[4], (DEPTH, D_MODEL, D_IN), D_MODEL ** -0.5),
        "qnorm_a_g": gain(ks[5], (DEPTH, HEAD_DIM)),
        "knorm_a_g": gain(ks[6], (DEPTH, HEAD_DIM)),
        "qnorm_b_g": gain(ks[7], (DEPTH, HEAD_DIM)),
        "knorm_b_g": gain(ks[8], (DEPTH, HEAD_DIM)),
        "sink_b": nrm(ks[9], (DEPTH, N_HEADS_B), 0.5),
        "w_branch_a": nrm(ks[10], (DEPTH, WIDTH_A, D_MODEL), WIDTH_A ** -0.5),
        "w_branch_b": nrm(ks[11], (DEPTH, WIDTH_BQ, D_MODEL), WIDTH_BQ ** -0.5),
        "w_out": nrm(ks[12], (DEPTH, D_MODEL, D_MODEL), D_MODEL ** -0.5),
        "norm_ffn_g": gain(ks[13], (DEPTH, D_MODEL)),
        "w_ffn_gate": nrm(ks[14], (DEPTH, D_MODEL, D_FF), D_MODEL ** -0.5),
        "w_ffn_up": nrm(ks[15], (DEPTH, D_MODEL, D_FF), D_MODEL ** -0.5),
        "w_ffn_down": nrm(ks[16], (DEPTH, D_FF, D_MODEL), D_FF ** -0.5),
        "norm_ple_g": gain(ks[17], (DEPTH, D_MODEL)),
        "w_ple_gate": nrm(ks[18], (DEPTH, D_MODEL, D_MODEL), D_MODEL ** -0.5),
        "w_ple_proj": nrm(ks[19], (DEPTH, PLE_DIM, D_MODEL), PLE_DIM ** -0.5),
    }


def reference(x, p, rel_table, norm_mix_g, w_in, qnorm_a_g, knorm_a_g, qnorm_b_g, knorm_b_g,
              sink_b, w_branch_a, w_branch_b, w_out, norm_ffn_g, w_ffn_gate, w_ffn_up, w_ffn_down,
              norm_ple_g, w_ple_gate, w_ple_proj):
    B_, S_, _ = x.shape
    table_a = rel_table[:, :N_HEADS_A]
    table_b = rel_table[:, N_HEADS_A:]
    biases_a = [rel_bias(table_a, w // (2 * d), d) for (w, d) in DILATED_PATTERNS]
    bias_b = rel_bias(table_b, WINDOW_B, 1)
    split_points = [int(v) for v in np.cumsum(IN_SPLITS)[:-1]]

    for l in range(DEPTH):
        h = rmsnorm(x, norm_mix_g[l])
        proj = jnp.einsum('bsd,de->bse', h, w_in[l])
        qa, ka, va, qb, kb, vb, ga, gb = jnp.split(proj, split_points, axis=-1)
        qa = rmsnorm(qa.reshape(B_, S_, N_HEADS_A, HEAD_DIM), qnorm_a_g[l])
        ka = rmsnorm(ka.reshape(B_, S_, N_HEADS_A, HEAD_DIM), knorm_a_g[l])
        va = va.reshape(B_, S_, N_HEADS_A, HEAD_DIM)
        qb = rmsnorm(qb.reshape(B_, S_, N_HEADS_B, HEAD_DIM), qnorm_b_g[l])
        kb = rmsnorm(kb.reshape(B_, S_, N_KV_B, HEAD_DIM), knorm_b_g[l])
        vb = vb.reshape(B_, S_, N_KV_B, HEAD_DIM)

        ya = dilated_mixer(qa, ka, va, biases_a).reshape(B_, S_, WIDTH_A)
        yb, _ = banded_attention(qb, kb, vb, bias_b, WINDOW_B, WINDOW_B, sink=sink_b[l])
        yb = yb.reshape(B_, S_, WIDTH_BQ)

        merged = (jax.nn.sigmoid(ga) * jnp.einsum('bsc,cd->bsd', ya, w_branch_a[l])
                  + jax.nn.sigmoid(gb) * jnp.einsum('bsc,cd->bsd', yb, w_branch_b[l]))
        x = x + jnp.einsum('bsd,de->bse', merged, w_out[l])

        h = rmsnorm(x, norm_ffn_g[l])
        hid = jax.nn.silu(jnp.einsum('bsd,df->bsf', h, w_ffn_gate[l])) * jnp.einsum('bsd,df->bsf', h, w_ffn_up[l])
        x = x + jnp.einsum('bsf,fd->bsd', hid, w_ffn_down[l])

        h = rmsnorm(x, norm_ple_g[l])
        x = x + jax.nn.sigmoid(jnp.einsum('bsd,de->bse', h, w_ple_gate[l])) * jnp.einsum('bsq,qd->bsd', p[l], w_ple_proj[l])
    return x
```

```python
import numpy as np
from contextlib import ExitStack
import concourse.bass as bass
import concourse.mybir as mybir
from concourse.bass_utils import run_bass_kernel_spmd

F32 = mybir.dt.float32
BF16 = mybir.dt.bfloat16
AF = mybir.ActivationFunctionType
ALU = mybir.AluOpType

T = 2048
D = 1024
NCH = 8
TC = 512
NTC = 4
DFF = 2816
EPS = 1e-6
NEG = -30000.0
NVL = 36
STRIPS = [(1, 64, 128, 320), (4, 64, 128, 320), (16, 64, 0, 128), (1, 128, 128, 384)]
SBASE = [0, 320, 640, 768]
EW = 1152


class _Rec:
    def __init__(self):
        self.call = None

    def __getattr__(self, name):
        def f(*a, **k):
            self.call = (name, a, k)
            return self
        return f


def _capture(fn):
    r = _Rec()
    fn(r)
    assert r.call is not None
    return r.call


class Sched:
    ENG = ("pe", "act", "dve", "pool", "sp")

    def __init__(self, nc, stack):
        self.nc = nc
        self.stack = stack
        self.prog = {e: [] for e in self.ENG}
        self.sems = {}
        self.count = {}
        self.waited = {e: {} for e in self.ENG}
        self.res = {}
        self.nwait = 0

    def _sem(self, key):
        if key not in self.sems:
            name = "s_" + "_".join(str(k) for k in (key if isinstance(key, tuple) else (key,)))
            self.sems[key] = self.stack.enter_context(self.nc.semaphore(name))
            self.count[key] = 0
        return self.sems[key]

    def _deps(self, reads, writes):
        toks = []
        for r in reads:
            st = self.res.get(r)
            if st and st[0]:
                toks.append(st[0])
        for w in writes:
            st = self.res.get(w)
            if st:
                if st[0]:
                    toks.append(st[0])
                toks.extend(st[1])
        return toks

    def _waits(self, eng, toks, rawkeys):
        need = {}
        for (k, v) in toks:
            if v > need.get(k, 0):
                need[k] = v
        for k, v in need.items():
            if k == eng and (eng == "pe" or k not in rawkeys):
                continue
            if self.waited[eng].get(k, 0) >= v:
                continue
            self.waited[eng][k] = v
            s = self._sem(k)
            self.prog[eng].append(lambda e, s=s, v=v: e.wait_ge(s, v))
            self.nwait += 1

    def _record(self, tok, reads, writes):
        for r in reads:
            st = self.res.setdefault(r, [None, []])
            st[1].append(tok)
        for w in writes:
            self.res[w] = [tok, []]

    def op(self, eng, fn, reads=(), writes=(), inc=True):
        toks = self._deps(reads, writes)
        raw = set()
        for r in reads:
            st = self.res.get(r)
            if st and st[0]:
                raw.add(st[0][0])
        self._waits(eng, toks, raw)
        s = self._sem(eng)
        call = _capture(fn)
        if inc:
            self.count[eng] += 1
            tok = (eng, self.count[eng])
            self.prog[eng].append(lambda e, c=call, s=s: getattr(e, c[0])(*c[1], **c[2]).then_inc(s, 1))
        else:
            tok = (eng, self.count[eng] + 1)
            self.prog[eng].append(lambda e, c=call: getattr(e, c[0])(*c[1], **c[2]))
        self._record(tok, reads, writes)

    def dma(self, eng, fn, chan, reads=(), writes=()):
        toks = [t for t in self._deps(reads, writes) if t[0] != chan]
        self._waits(eng, toks, set(k for k, _ in toks))
        s = self._sem(chan)
        self.count[chan] += 16
        tok = (chan, self.count[chan])
        call = _capture(fn)
        self.prog[eng].append(lambda e, c=call, s=s: getattr(e, c[0])(*c[1], **c[2]).then_inc(s, 16))
        self._record(tok, reads, writes)

    def alias(self, new_keys, old_keys):
        toks = []
        nk = set(new_keys)
        for k in old_keys:
            if k in nk:
                continue
            st = self.res.get(k)
            if st:
                if st[0]:
                    toks.append(st[0])
                toks.extend(st[1])
        red = {}
        for (k, v) in toks:
            if v > red.get(k, 0):
                red[k] = v
        toks = list(red.items())
        for k in new_keys:
            st = self.res.setdefault(k, [None, []])
            st[1].extend(toks)

    def wait_all(self, eng, keys):
        toks = []
        for k in keys:
            st = self.res.get(k)
            if st:
                if st[0]:
                    toks.append(st[0])
                toks.extend(st[1])
        self._waits(eng, toks, set(k for k, _ in toks))


def _t5_bucket_np(rel):
    rel = np.asarray(rel, dtype=np.int64)
    half_b, max_exact = 16, 8
    sign = np.where(rel > 0, half_b, 0)
    n = np.abs(rel)
    nf = np.maximum(n, 1).astype(np.float32)
    large = max_exact + (np.log(nf / np.float32(max_exact)) / np.float32(np.log(1024 / max_exact))
                         * np.float32(half_b - max_exact)).astype(np.int32)
    large = np.minimum(large, half_b - 1)
    return sign + np.where(n < max_exact, n, large)


def _emat():
    E = np.zeros((33, len(STRIPS), 512), np.float32)
    for si, (d, half, c0, width) in enumerate(STRIPS):
        m = np.arange(512)
        rel = m - (width - 1 - c0)
        code = np.where(np.abs(rel) <= half, _t5_bucket_np(rel * d), 32)
        E[code, si, m] = 1.0
    return E.reshape(33, len(STRIPS) * 512)


def _cst():
    c = np.zeros((128, 512), np.float32)
    c[:, 384:512] = np.roll(np.eye(128, dtype=np.float32), 64, axis=1)
    c[:, 0:128] = np.eye(128, dtype=np.float32)
    c[:, 128:256] = 1.0
    c[0:64, 256:320] = 1.0
    c[64:128, 320:384] = 1.0
    return c


def build(n_layers=2, stage=99, dbg=0):
    nc = bass.Bass("TRN2", target_bir_lowering=False)
    dram = {}

    def din(name, shape):
        dram[name] = nc.dram_tensor(name, list(shape), F32, kind="ExternalInput").ap()
        return dram[name]

    x_d = din("x", [T, D])
    p_d = din("p", [2, T, 256])
    w_in_d = din("w_in", [2, D, 4352])
    w_a_d = din("w_branch_a", [2, 512, D])
    w_b_d = din("w_branch_b", [2, 512, D])
    w_out_d = din("w_out", [2, D, D])
    w_g_d = din("w_ffn_gate", [2, D, DFF])
    w_u_d = din("w_ffn_up", [2, D, DFF])
    w_d_d = din("w_ffn_down", [2, DFF, D])
    w_pg_d = din("w_ple_gate", [2, D, D])
    w_pp_d = din("w_ple_proj", [2, 256, D])
    vecs_d = din("vecs", [128, 2 * NVL])
    tab_d = din("rel_table", [32, 16])
    cst_d = din("cst", [128, 512])
    emat_d = din("emat", [33, 4 * 512])
    y_d = nc.dram_tensor("y", [T, D], F32, kind="ExternalOutput").ap()
    dbg_d = None
    if dbg:
        dbg_d = nc.dram_tensor("dbg", [128, 4 * T], BF16, kind="ExternalOutput").ap()

    with ExitStack() as stack:
        def sb(name, shape, dt):
            return stack.enter_context(nc.sbuf_tensor(name, list(shape), dt))

        xT = sb("xT", [128, NCH, T], F32)
        hT = sb("hT", [128, NCH, T], BF16)
        ring = [sb(f"ring{i}", [128, 4096], BF16) for i in range(3)]
        Ebuf = sb("Ebuf", [128, 8, EW], BF16)
        R = sb("R", [128, 26624], BF16)
        scr = [sb(f"scr{i}", [128, 512], F32) for i in range(2)]
        sq_all = sb("sq_all", [128, 4 * 512], BF16)
        sqb = [sq_all[:, i * 512:(i + 1) * 512] for i in range(4)]
        pTs = [sb(f"pT{i}", [128, 512], BF16) for i in range(6)]
        cst_f = sb("cst_f", [128, 256], F32)
        cst_b = sb("cst_b", [128, 384], BF16)
        vecs = sb("vecs_s", [128, 2 * NVL], F32)
        vd = sb("vd", [128, 2 * NVL], F32)
        tab = sb("tab", [33, 16], BF16)
        tabf = sb("tabf", [33, 16], F32)
        psb = [stack.enter_context(nc.psum_tensor(f"ps{i}", [128, 512], F32)) for i in range(8)]

        S = Sched(nc, stack)
        ident_f = cst_f[:, 0:128]
        swap_f = cst_f[:, 128:256]
        ident_b = cst_b[:, 0:128]
        ones_b = cst_b[:, 128:256]
        bones_b = cst_b[:, 256:384]

        def Rv(off_bytes, nbytes, dt):
            a = R[:, off_bytes // 2:(off_bytes + nbytes) // 2]
            return a if dt == BF16 else a.bitcast(dt)

        KB = 1024
        yaT = Rv(0, 16 * KB, BF16).rearrange("p (c t) -> p c t", c=4)
        qa_v = Rv(16 * KB, 4 * KB, BF16)
        ka_v = Rv(20 * KB, 4 * KB, BF16)
        vt = Rv(24 * KB, 53 * 384, BF16).rearrange("p (t f) -> p t f", f=192)
        acc = Rv(44 * KB, 8 * KB, F32)
        qbT = Rv(16 * KB, 16 * KB, BF16).rearrange("p (c t) -> p c t", c=4)
        kbT = Rv(32 * KB, 4 * KB, BF16)
        vbt = Rv(36 * KB, 16 * 384, BF16).rearrange("p (t f) -> p t f", f=192)
        merged = Rv(32 * KB, 16 * KB, BF16).rearrange("p (c t) -> p c t", c=4)
        hid = Rv(0, 44 * KB, BF16).rearrange("p (c t) -> p c t", c=11)
        ptok = Rv(0, 16 * KB, F32).rearrange("p (t f) -> p t f", f=256)
        ppT = Rv(16 * KB, 8 * KB, BF16).rearrange("p (c t) -> p c t", c=2)
        xtmp = Rv(0, 16 * KB, F32).rearrange("p (s f) -> p s f", s=4)
        Emat = Rv(16 * KB, 4 * KB, BF16)
        otile = Rv(0, 8 * KB, F32).rearrange("p (s f) -> p s f", s=2)
        K_SETUP = [("xtmp", i) for i in range(4)] + ["Emat"]
        K_YA = [("ya", i) for i in range(4)]
        K_A = ["qa", "ka", "vt", "acc"]
        K_B = [("qb", i, j) for i in range(4) for j in range(4)] + ["kb", "vbt"]
        K_G = [("mg", n) for n in range(4)]
        K_F = [("hid", n) for n in range(4)]
        K_P = [("ptok", 0), ("ptok", 1)] + [("ppT", n) for n in range(4)]
        K_O = [("ot", 0), ("ot", 1)]
        ALLR = K_SETUP + K_YA + K_A + K_B + K_G + K_F + K_P + K_O

        bank_ctr = [0]

        def bank():
            b = bank_ctr[0] % 8
            bank_ctr[0] += 1
            return b

        ring_ctr = [0]

        def wload(parts):
            s = ring_ctr[0] % 3
            ring_ctr[0] += 1
            for dst_fn, src in parts:
                dst = dst_fn(ring[s])
                S.dma("pool", lambda e, dst=dst, src=src: e.dma_start(out=dst, in_=src),
                      ("wch", s), writes=[("w", s)])
            return s

        def wview(s, k, n):
            return ring[s][:, 0:k * n].rearrange("p (k n) -> p k n", k=k)

        evac_ctr = [0]

        def load_pair_w(l, c):
            s_ = ring_ctr[0] % 3
            ring_ctr[0] += 1
            W_ = wview(s_, 8, 384)
            src_ = w_in_d[l].rearrange("(k p) n -> p k n", p=128)
            for part, base in enumerate((0, 512, 1024)):
                S.dma("pool", lambda e, W_=W_, part=part, base=base, src_=src_: e.dma_start(
                    out=W_[:, :, part * 128:(part + 1) * 128], in_=src_[:, :, base + c * 128: base + (c + 1) * 128]),
                    ("wch", s_), writes=[("w", s_)])
            return s_

        def load_gu(l, fb, cb, ncols):
            s_ = ring_ctr[0] % 3
            ring_ctr[0] += 1
            Wgu = ring[s_][:, 0:16 * ncols].rearrange("p (w k n) -> p w k n", w=2, k=8)
            WG_, WU_ = Wgu[:, 0], Wgu[:, 1]
            f0_ = fb * 1408 + cb
            wg3_ = w_g_d[l].rearrange("(k p) n -> p k n", p=128)
            wu3_ = w_u_d[l].rearrange("(k p) n -> p k n", p=128)
            S.dma("pool", lambda e: e.dma_start(out=WG_, in_=wg3_[:, :, f0_:f0_ + ncols]), ("wch", s_), writes=[("w", s_)])
            S.dma("pool", lambda e: e.dma_start(out=WU_, in_=wu3_[:, :, f0_:f0_ + ncols]), ("wch", s_), writes=[("w", s_)])
            return s_, WG_, WU_

        preloaded = {}
        S.dma("sp", lambda e: e.dma_start(out=cst_f[:, 0:128], in_=cst_d[:, 0:128]), "c_cstf", writes=["cst_f"])
        S.dma("sp", lambda e: e.dma_start(out=cst_f[:, 128:256], in_=cst_d[:, 384:512]), "c_cstf2", writes=["cst_f2"])
        S.dma("pool", lambda e: e.dma_start(out=cst_b[:], in_=cst_d[:, 0:384]), "c_cstb", writes=["cst_b"])
        S.dma("sp", lambda e: e.dma_start(out=vecs[:], in_=vecs_d[:]), "c_vecs", writes=["vecs"])
        S.dma("sp", lambda e: e.dma_start(out=tabf[0:32, :], in_=tab_d[:]), "c_tab", writes=["tabf"])
        S.op("dve", lambda e: e.memset(tabf[32:33, :], NEG), reads=[], writes=["tabf32"])
        S.op("dve", lambda e: e.tensor_copy(out=tab[0:32, :], in_=tabf[0:32, :]), reads=["tabf"], writes=["tab0"])
        S.op("dve", lambda e: e.tensor_copy(out=tab[32:33, :], in_=tabf[32:33, :]), reads=["tabf32"], writes=["tab1"])
        S.dma("pool", lambda e: e.dma_start(out=Emat[0:33, :], in_=emat_d[:]), "c_emat", writes=["Emat"])
        if n_layers >= 1 and stage >= 0.1:
            preloaded[(0, 0)] = load_pair_w(0, 0)
        for l in range(n_layers):
            b0 = l * NVL
            S.op("dve", lambda e, b0=b0: e.tensor_scalar_mul(out=vd[:, b0:b0 + 24], in0=vecs[:, b0:b0 + 24], scalar1=32.0),
                 reads=["vecs"], writes=[("vd", l, 0)])
            S.op("dve", lambda e, b0=b0: e.tensor_copy(out=vd[:, b0 + 24:b0 + 25], in_=vecs[:, b0 + 24:b0 + 25]),
                 reads=["vecs"], writes=[("vd", l, 1)])
            S.op("dve", lambda e, b0=b0: e.tensor_scalar_mul(out=vd[:, b0 + 25:b0 + 26], in0=vecs[:, b0 + 25:b0 + 26], scalar1=8.0),
                 reads=["vecs"], writes=[("vd", l, 2)])
            S.op("dve", lambda e, b0=b0: e.tensor_copy(out=vd[:, b0 + 26:b0 + 27], in_=vecs[:, b0 + 26:b0 + 27]),
                 reads=["vecs"], writes=[("vd", l, 3)])
            S.op("dve", lambda e, b0=b0: e.tensor_scalar_mul(out=vd[:, b0 + 27:b0 + 28], in0=vecs[:, b0 + 27:b0 + 28], scalar1=8.0),
                 reads=["vecs"], writes=[("vd", l, 4)])
            S.op("act", lambda e, b0=b0: e.activation(out=vd[:, b0 + 28:b0 + 36], in_=vecs[:, b0 + 28:b0 + 36], func=AF.Exp),
                 reads=["vecs"], writes=[("vd", l, 5)])
        VD = [("vd", l, i) for l in range(n_layers) for i in range(6)]

        def gen_strip(si_):
            d_, half_, c0_, width_ = STRIPS[si_]
            hs = 8 if si_ == 3 else 0
            for j0 in range(0, width_, 64):
                bk = bank()
                for jj in range(64):
                    jp = j0 + jj
                    e0 = si_ * 512 + (width_ - 1 - jp)
                    S.op("pe", lambda e, bk=bk, jj=jj, hs=hs, e0=e0: e.matmul(
                        psb[bk][:, jj * 8:(jj + 1) * 8], lhsT=Emat[0:33, e0:e0 + 128],
                        rhs=tab[0:33, hs:hs + 8], start=True, stop=True),
                        reads=["Emat", "tab0", "tab1"], writes=[("ps", bk)], inc=(jj == 63))
                S.op("act", lambda e, bk=bk, si_=si_, j0=j0: e.activation(
                    out=Ebuf[:, :, SBASE[si_] + j0: SBASE[si_] + j0 + 64],
                    in_=psb[bk][:, :].rearrange("p (j h) -> p h j", h=8), func=AF.Exp),
                    reads=[("ps", bk)], writes=["Ebuf"])

        for n in range(NTC):
            for i in range(4):
                t = 4 * n + i
                S.dma("sp", lambda e, i=i, t=t: e.dma_start(out=xtmp[:, i, :], in_=x_d[t * 128:(t + 1) * 128, :]),
                      ("xl", i), writes=[("xtmp", i)])
            for c in range(NCH):
                bk = bank()
                for i in range(4):
                    S.op("pe", lambda e, bk=bk, i=i, c=c: e.transpose(
                        out=psb[bk][:, i * 128:(i + 1) * 128], in_=xtmp[:, i, c * 128:(c + 1) * 128], identity=ident_f),
                        reads=[("xtmp", i), "cst_f"], writes=[("ps", bk)], inc=(i == 3))
                eng = "act" if c % 2 == 0 else "dve"
                if eng == "act":
                    S.op("act", lambda e, bk=bk, c=c, n=n: e.copy(out=xT[:, c, n * TC:(n + 1) * TC], in_=psb[bk][:, :]),
                         reads=[("ps", bk)], writes=[("x", c, n)])
                else:
                    S.op("dve", lambda e, bk=bk, c=c, n=n: e.tensor_copy(out=xT[:, c, n * TC:(n + 1) * TC], in_=psb[bk][:, :]),
                         reads=[("ps", bk)], writes=[("x", c, n)])

        sq_ctr = [0]
        scr_ctr = [0]

        def norm_chunk(l, gcol, n, mid=None):
            bk = bank()
            for half in range(2):
                sis = {}
                for c in range(4 * half, 4 * half + 4):
                    si = sq_ctr[0] % 4
                    sq_ctr[0] += 1
                    sis[c] = si
                    xin = xT[:, c, n * TC:(n + 1) * TC]
                    if c % 2 == 0:
                        S.op("act", lambda e, si=si, xin=xin: e.activation(out=sqb[si], in_=xin, func=AF.Square),
                             reads=[("x", c, n)], writes=[("sq", si)])
                    else:
                        S.op("pool", lambda e, si=si, xin=xin: e.tensor_tensor(out=sqb[si], in0=xin, in1=xin, op=ALU.mult),
                             reads=[("x", c, n)], writes=[("sq", si)])
                if half == 0 and mid is not None:
                    mid()
                for c in range(4 * half, 4 * half + 4):
                    si = sis[c]
                    S.op("pe", lambda e, bk=bk, si=si, c=c: e.matmul(psb[bk][:, :], lhsT=ones_b, rhs=sqb[si],
                                                                    start=(c == 0), stop=(c == NCH - 1)),
                         reads=[("sq", si), "cst_b"], writes=[("ps", bk)], inc=True)
            ri = scr_ctr[0] % 2
            scr_ctr[0] += 1
            S.op("act", lambda e, bk=bk, ri=ri: e.activation(out=scr[ri][:, :], in_=psb[bk][:, :], func=AF.Ln, bias=1024.0 * EPS),
                 reads=[("ps", bk)], writes=[("scr", ri)])
            S.op("act", lambda e, ri=ri: e.activation(out=scr[ri][:, :], in_=scr[ri][:, :], func=AF.Exp, scale=-0.5),
                 reads=[("scr", ri)], writes=[("scr", ri)])
            for c in range(NCH):
                eng = "dve"
                S.op(eng, lambda e, c=c, n=n, ri=ri: e.scalar_tensor_tensor(
                    out=hT[:, c, n * TC:(n + 1) * TC], in0=xT[:, c, n * TC:(n + 1) * TC],
                    scalar=vd[:, l * NVL + gcol + c: l * NVL + gcol + c + 1], in1=scr[ri][:, :],
                    op0=ALU.mult, op1=ALU.mult),
                    reads=[("x", c, n), ("scr", ri)] + VD, writes=[("h", n)])

        def rmsnorm_to_hT(l, gcol, chunks=range(NTC)):
            for n in chunks:
                norm_chunk(l, gcol, n)

        qk_pend = []

        def qk_finish(l, bk, si, gcolidx, dst, dkeys):
            b2 = bank()
            S.op("pe", lambda e, b2=b2, si=si: e.matmul(psb[b2][:, :], lhsT=bones_b, rhs=sqb[si], start=True, stop=True),
                 reads=[("sq", si), "cst_b"], writes=[("ps", b2)])
            ri = scr_ctr[0] % 2
            scr_ctr[0] += 1
            S.op("act", lambda e, b2=b2, ri=ri: e.activation(out=scr[ri][:, :], in_=psb[b2][:, :], func=AF.Ln, bias=64.0 * EPS),
                 reads=[("ps", b2)], writes=[("scr", ri)])
            S.op("act", lambda e, ri=ri: e.activation(out=scr[ri][:, :], in_=scr[ri][:, :], func=AF.Exp, scale=-0.5),
                 reads=[("scr", ri)], writes=[("scr", ri)])
            S.op("dve", lambda e, bk=bk, ri=ri, dst=dst: e.scalar_tensor_tensor(
                out=dst, in0=psb[bk][:, :], scalar=vd[:, l * NVL + gcolidx: l * NVL + gcolidx + 1],
                in1=scr[ri][:, :], op0=ALU.mult, op1=ALU.mult),
                reads=[("ps", bk), ("scr", ri)] + VD, writes=dkeys)

        def qk_flush():
            while qk_pend:
                qk_finish(*qk_pend.pop(0))

        def qk_proj(l, slot, wk, wcol, gcolidx, dst_fn, dst_keys_fn, flush=True):
            W = wview(slot, 8, wk)
            for n in range(NTC):
                bk = bank()
                for k in range(NCH):
                    S.op("pe", lambda e, k=k, n=n, bk=bk: e.matmul(
                        psb[bk][:, :], lhsT=W[:, k, wcol:wcol + 128], rhs=hT[:, k, n * TC:(n + 1) * TC],
                        start=(k == 0), stop=(k == NCH - 1)),
                        reads=[("w", slot), ("h", n)], writes=[("ps", bk)], inc=(k == NCH - 1))
                si = sq_ctr[0] % 4
                sq_ctr[0] += 1
                S.op("act", lambda e, si=si, bk=bk: e.activation(out=sqb[si], in_=psb[bk][:, :], func=AF.Square),
                     reads=[("ps", bk)], writes=[("sq", si)])
                qk_pend.append((l, bk, si, gcolidx, dst_fn(n), dst_keys_fn(n)))
                if len(qk_pend) > 1:
                    qk_finish(*qk_pend.pop(0))
            if flush:
                qk_flush()

        def attention(pairs_by_otile, evac_fn, look=5, hook=None):
            groups = []
            for oi, ot in enumerate(pairs_by_otile):
                flat = [(bi, pi, p) for bi, blk in enumerate(ot["blocks"]) for pi, p in enumerate(blk)]
                gsz = ot.get("gsz", 4)
                chunks = [flat[i:i + gsz] for i in range(0, len(flat), gsz)]
                for ci, ch in enumerate(chunks):
                    groups.append((oi, ch, ci == len(chunks) - 1))
            state = {}
            obank = {}
            mul_ctr = [0]

            def stage1(gi):
                oi, ch, _ = groups[gi]
                bk = bank()
                pslot = gi % 6
                for j, (bi, pi, p) in enumerate(ch):
                    M = p["M"]
                    cols = slice(j * 128, (j + 1) * 128)
                    S.op("pe", lambda e, bk=bk, p=p, M=M, cols=cols: e.matmul(
                        psb[bk][0:M, cols], lhsT=p["k"], rhs=p["q"], start=True, stop=True),
                        reads=p["reads"], writes=[("ps", bk)], inc=(j == len(ch) - 1))
                j0 = 0
                while j0 < len(ch):
                    j1 = j0
                    while j1 < len(ch) and ch[j1][2]["M"] == ch[j0][2]["M"]:
                        j1 += 1
                    M = ch[j0][2]["M"]
                    S.op("act", lambda e, bk=bk, pslot=pslot, j0=j0, j1=j1, M=M: e.activation(
                        out=pTs[pslot][0:M, j0 * 128:j1 * 128], in_=psb[bk][0:M, j0 * 128:j1 * 128], func=AF.Exp),
                        reads=[("ps", bk)], writes=[("pT", pslot)])
                    j0 = j1
                ecs = [p["ecol"] for (_, _, p) in ch]
                full = all(p["M"] == 128 for (_, _, p) in ch)
                rep = None
                if full and len(ch) == 4 and ecs[0] == ecs[2] and ecs[1] == ecs[3] and ecs[1] == ecs[0] + 128:
                    rep = (2, 256)
                elif full and len(ch) == 4 and ecs[0] == ecs[1] == ecs[2] == ecs[3]:
                    rep = (4, 128)
                if rep is not None:
                    a_, b_ = rep
                    eh = ch[0][2]["eh"]
                    ec = ecs[0]
                    pv = pTs[pslot][:, 0:512].rearrange("p (a b) -> p a b", a=a_)
                    ev = Ebuf[:, eh, ec:ec + b_].unsqueeze(1).broadcast_to([128, a_, b_])
                    S.op("dve", lambda e, pv=pv, ev=ev: e.tensor_tensor(out=pv, in0=pv, in1=ev, op=ALU.mult),
                         reads=[("pT", pslot), "Ebuf"], writes=[("pT", pslot)])
                j0 = 0 if rep is None else len(ch)
                while j0 < len(ch):
                    j1 = j0 + 1
                    while (j1 < len(ch) and ch[j1][2]["M"] == ch[j0][2]["M"]
                           and ch[j1][2]["ecol"] == ch[j1 - 1][2]["ecol"] + 128):
                        j1 += 1
                    M = ch[j0][2]["M"]
                    ec = ch[j0][2]["ecol"]
                    eh = ch[j0][2]["eh"]
                    eng = "dve"
                    mul_ctr[0] += 1
                    S.op(eng, lambda e, pslot=pslot, j0=j0, j1=j1, M=M, ec=ec, eh=eh: e.tensor_tensor(
                        out=pTs[pslot][0:M, j0 * 128:j1 * 128], in0=pTs[pslot][0:M, j0 * 128:j1 * 128],
                        in1=Ebuf[0:M, eh, ec:ec + (j1 - j0) * 128], op=ALU.mult),
                        reads=[("pT", pslot), "Ebuf"], writes=[("pT", pslot)])
                    j0 = j1
                state[gi] = pslot

            def stage2(gi):
                oi, ch, last = groups[gi]
                ot = pairs_by_otile[oi]
                first_in_bank = oi not in obank
                if first_in_bank:
                    obank[oi] = bank()
                ob = obank[oi]
                pslot = state[gi]
                order = sorted(range(len(ch)), key=lambda j: (ch[j][1], ch[j][0]))
                for oi_, j in enumerate(order):
                    bi, pi, p = ch[j]
                    M = p["M"]
                    nb = len(ot["blocks"][bi])
                    S.op("pe", lambda e, ob=ob, p=p, M=M, bi=bi, pi=pi, nb=nb, pslot=pslot, j=j, st=(first_in_bank and oi_ == 0): e.matmul(
                        psb[ob][:, bi * 128:(bi + 1) * 128], lhsT=p["v"], rhs=pTs[pslot][0:M, j * 128:(j + 1) * 128],
                        start=st, stop=(pi == nb - 1), skip_group_check=True),
                        reads=[("pT", pslot)] + p["vreads"], writes=[("ps", ob)], inc=(oi_ == len(ch) - 1))
                if last:
                    (ot.get("evac") or evac_fn)(oi, ob, ot)
                    if ot.get("post") is not None:
                        ot["post"]()

            G = len(groups)
            for i in range(G + look):
                if i < G:
                    stage1(i)
                if hook is not None and i == min(2, G - 1):
                    hook()
                    hook = None
                if i - look >= 0:
                    stage2(i - look)

        out_done = set()

        def out_tile(t):
            so = t % 2
            b2 = [bank(), bank()]
            for c in range(NCH):
                bk = b2[c // 4]
                S.op("pe", lambda e, bk=bk, c=c, t=t: e.transpose(
                    out=psb[bk][:, (c % 4) * 128:(c % 4 + 1) * 128], in_=xT[:, c, t * 128:(t + 1) * 128], identity=ident_f),
                    reads=[("x", c, t // 4), "cst_f"], writes=[("ps", bk)], inc=(c % 4 == 3))
            S.op("act", lambda e, bk=b2[0], so=so: e.copy(out=otile[:, so, 0:512], in_=psb[bk][:, :]),
                 reads=[("ps", b2[0])], writes=[("ot", so)])
            S.op("dve", lambda e, bk=b2[1], so=so: e.tensor_copy(out=otile[:, so, 512:1024], in_=psb[bk][:, :]),
                 reads=[("ps", b2[1])], writes=[("ot", so)])
            S.dma("sp", lambda e, t=t, so=so: e.dma_start(out=y_d[t * 128:(t + 1) * 128, :], in_=otile[:, so, :]),
                  ("och", so), reads=[("ot", so)], writes=[("yout", so)])

        for l in range(n_layers):
            vb0 = l * NVL
            if stage < 0.1:
                break
            if l == 0:
                for n in range(NTC):
                    norm_chunk(l, 0, n, mid=(lambda n=n: gen_strip(n)))
            if stage < 1 and stage <= 0.2:
                if dbg == 3:
                    S.dma("sp", lambda e: e.dma_start(out=dbg_d[:, :], in_=hT[:, 0:4, :].rearrange("p c t -> p (c t)")), "dbgc", reads=[("h", n) for n in range(4)], writes=["dbgout"])
                break

            def a_tiles():
                tiles = []
                idx = {}
                for j in range(17):
                    s0, s1 = max(128 * j - 64, 0), min(128 * j + 64, T)
                    idx[(0, 0, j)] = len(tiles)
                    tiles.append((s0, 1, s1 - s0, 64 if j == 0 else 0))
                for r in range(4):
                    for j in range(5):
                        s0, s1 = max(128 * j - 64, 0), min(128 * j + 64, 512)
                        idx[(1, r, j)] = len(tiles)
                        tiles.append((r + 4 * s0, 4, s1 - s0, 64 if j == 0 else 0))
                for r in range(16):
                    idx[(2, r, 0)] = len(tiles)
                    tiles.append((r, 16, 128, 0))
                return tiles, idx

            tiles, tidx = a_tiles()
            pending_fin = [None]

            def tok_ap(buf2d, rows, tok0, step, cnt):
                if step == 1:
                    return buf2d[rows, tok0:tok0 + cnt]
                v = buf2d.rearrange("p (s r) -> p r s", r=step)
                return v[rows, tok0 % step, tok0 // step: tok0 // step + cnt]

            for c in range(4):
                src3 = w_in_d[l].rearrange("(k p) n -> p k n", p=128)
                if (l, c) in preloaded:
                    s = preloaded.pop((l, c))
                else:
                    s = load_pair_w(l, c)
                W = wview(s, 8, 384)
                akeys = ["qa", "ka", "vt", "acc"]
                if c == 0:
                    S.alias(K_A + K_YA, ALLR)
                qk_proj(l, s, 384, 0, 24, lambda n: qa_v[:, n * TC:(n + 1) * TC], lambda n: ["qa"], flush=False)
                qk_proj(l, s, 384, 128, 25, lambda n: ka_v[:, n * TC:(n + 1) * TC], lambda n: ["ka"], flush=True)
                if stage <= 0.4:
                    if dbg == 2:
                        S.dma("sp", lambda e: e.dma_start(out=dbg_d[:, 0:T], in_=qa_v), "dbgc", reads=["qa"], writes=["dbgout"])
                        S.dma("sp", lambda e: e.dma_start(out=dbg_d[:, T:2 * T], in_=ka_v), "dbgc2", reads=["ka"], writes=["dbgout2"])
                    break
                S.op("dve", lambda e: e.memset(vt[:, :, 64:128], 1.0), reads=[], writes=["vt"])
                vT = sq_all
                for n in range(NTC):
                    bk = bank()
                    for k in range(NCH):
                        S.op("pe", lambda e, k=k, n=n, bk=bk, W=W: e.matmul(
                            psb[bk][:, :], lhsT=W[:, k, 256:384], rhs=hT[:, k, n * TC:(n + 1) * TC],
                            start=(k == 0), stop=(k == NCH - 1)),
                            reads=[("w", s), ("h", n)], writes=[("ps", bk)], inc=(k == NCH - 1))
                    S.op("act", lambda e, bk=bk, n=n: e.copy(out=vT[:, n * TC:(n + 1) * TC], in_=psb[bk][:, :]),
                         reads=[("ps", bk)], writes=[("sq", n)])
                for t0 in range(0, len(tiles), 4):
                    grp = tiles[t0:t0 + 4]
                    bk = bank()
                    for gi, (tok0, step, M, i0) in enumerate(grp):
                        lhs = tok_ap(vT, slice(0, 128), tok0, step, M)
                        S.op("pe", lambda e, bk=bk, gi=gi, M=M, lhs=lhs: e.matmul(
                            psb[bk][0:M, gi * 128:(gi + 1) * 128], lhsT=lhs, rhs=ident_b,
                            start=True, stop=True),
                            reads=["cst_b"] + [("sq", n) for n in range(NTC)], writes=[("ps", bk)],
                            inc=(gi == len(grp) - 1))
                    ng = len(grp)
                    vo = vt[:, t0:t0 + ng, :].rearrange("p t (a b) -> p t a b", a=3)[:, :, 0:3:2, :]
                    vi = psb[bk][:, 0:ng * 128].rearrange("p (t a b) -> p t a b", a=2, b=64)
                    if (t0 // 4) % 2 == 0:
                        S.op("act", lambda e, vo=vo, vi=vi: e.copy(out=vo, in_=vi), reads=[("ps", bk)], writes=["vt"])
                    else:
                        S.op("dve", lambda e, vo=vo, vi=vi: e.tensor_copy(out=vo, in_=vi), reads=[("ps", bk)], writes=["vt"])
                if stage <= 0.6:
                    break
                pair_otiles = []
                for u in range(2):
                    h = 2 * c + u
                    rows = slice(u * 64, (u + 1) * 64)
                    vcols = slice(0, 128) if u == 0 else slice(64, 192)
                    orow = slice(0, 64) if u == 0 else slice(64, 128)
                    drow = slice(64, 128) if u == 0 else slice(0, 64)

                    def mkpair(pat, r, j, qtok0, step, kind):
                        tok0, st, M, i0 = tiles[tidx[(pat, r, j)]]
                        ti = tidx[(pat, r, j)]
                        if pat == 2:
                            shift = 0
                        elif kind == "t1":
                            shift = 64
                        else:
                            shift = 128 if i0 == 64 else 192
                        return dict(k=tok_ap(ka_v, rows, tok0, st, M), q=tok_ap(qa_v, rows, qtok0, step, 128), M=M,
                                    ecol=SBASE[pat] + shift, eh=h, v=vt[0:M, ti, vcols],
                                    reads=["qa", "ka"], vreads=["vt"])

                    otiles = []
                    for Q in range(4):
                        blocks = []
                        for b in range(4 * Q, 4 * Q + 4):
                            blocks.append([mkpair(0, 0, b + 1, 128 * b, 1, "t1"), mkpair(0, 0, b, 128 * b, 1, "t0")])
                        otiles.append(dict(blocks=blocks, kind=("p1", Q)))
                    for r in range(4):
                        blocks = []
                        for b in range(4):
                            blocks.append([mkpair(1, r, b + 1, r + 4 * 128 * b, 4, "t1"), mkpair(1, r, b, r + 4 * 128 * b, 4, "t0")])
                        otiles.append(dict(blocks=blocks, kind=("p2", r)))
                    for r0 in range(0, 16, 4):
                        blocks = [[mkpair(2, r, 0, r, 16, "t")] for r in range(r0, r0 + 4)]
                        otiles.append(dict(blocks=blocks, kind=("p3", r0)))

                    def evacA(oi, ob, ot):
                        kind, v = ot["kind"]
                        if kind == "p1":
                            S.op("dve", lambda e, ob=ob, v=v: e.tensor_copy(out=acc[:, v * 512:(v + 1) * 512], in_=psb[ob][:, :]),
                                 reads=[("ps", ob)], writes=["acc"])
                        elif kind == "p2":
                            av = acc.rearrange("p (s r) -> p r s", r=4)[:, v, :]
                            S.op("dve", lambda e, ob=ob, av=av: e.tensor_tensor(out=av, in0=psb[ob][:, :], in1=av, op=ALU.add),
                                 reads=[("ps", ob), "acc"], writes=["acc"])
                        else:
                            av = acc.rearrange("p (s r) -> p r s", r=16)[:, v:v + 4, :]
                            S.op("dve", lambda e, ob=ob, av=av: e.tensor_tensor(
                                out=av, in0=psb[ob][:, :].rearrange("p (r s) -> p r s", r=4), in1=av, op=ALU.add),
                                reads=[("ps", ob), "acc"], writes=["acc"])


                    def make_fin(c=c, orow=orow, drow=drow):
                        def fin():
                            for n in range(NTC):
                                bk = bank()
                                S.op("act", lambda e, bk=bk, n=n: e.activation(out=psb[bk][drow, :], in_=acc[drow, n * TC:(n + 1) * TC], func=AF.Ln),
                                     reads=["acc"], writes=[("ps", bk)])
                                S.op("act", lambda e, bk=bk: e.activation(out=psb[bk][drow, :], in_=psb[bk][drow, :], func=AF.Exp, scale=-1.0),
                                     reads=[("ps", bk)], writes=[("ps", bk)])
                                S.op("dve", lambda e, bk=bk, n=n: e.tensor_tensor(
                                    out=yaT[orow, c, n * TC:(n + 1) * TC], in0=acc[orow, n * TC:(n + 1) * TC], in1=psb[bk][drow, :], op=ALU.mult),
                                    reads=["acc", ("ps", bk)], writes=[("ya", c)])
                        return fin
                    for ot_ in otiles:
                        ot_["evac"] = evacA
                    otiles[-1]["post"] = make_fin()
                    pair_otiles.extend(otiles)
                attention(pair_otiles, None)
            if pending_fin[0] is not None:
                pending_fin[0]()
                pending_fin[0] = None
            if stage <= 0.6:
                break
            if stage < 2:
                if dbg == 1:
                    S.dma("sp", lambda e: e.dma_start(out=dbg_d[:, :], in_=Rv(0, 16 * KB, BF16)), "dbgc", reads=K_YA, writes=["dbgout"])
                elif dbg == 2:
                    S.dma("sp", lambda e: e.dma_start(out=dbg_d[:, 0:T], in_=qa_v), "dbgc", reads=["qa"], writes=["dbgout"])
                    S.dma("sp", lambda e: e.dma_start(out=dbg_d[:, T:2 * T], in_=ka_v), "dbgc2", reads=["ka"], writes=["dbgout2"])
                elif dbg == 3:
                    S.dma("sp", lambda e: e.dma_start(out=dbg_d[:, :], in_=hT[:, 0:4, :].rearrange("p c t -> p (c t)")), "dbgc", reads=[("h", n) for n in range(4)], writes=["dbgout"])
                break

            s1 = ring_ctr[0] % 3
            ring_ctr[0] += 1
            Wq = wview(s1, 8, 512)
            src3 = w_in_d[l].rearrange("(k p) n -> p k n", p=128)
            Wq5 = ring[s1][:, 0:4096].rearrange("p (k c u e) -> p k c u e", k=8, c=4, u=2)
            for u in range(2):
                for cq in range(4):
                    S.dma("pool", lambda e, u=u, cq=cq, Wq5=Wq5, src3=src3: e.dma_start(
                        out=Wq5[:, :, cq, u, :],
                        in_=src3[:, :, 1536 + u * 256 + cq * 64:1536 + u * 256 + (cq + 1) * 64]),
                        ("wch", s1), writes=[("w", s1)])
            s2 = ring_ctr[0] % 3
            ring_ctr[0] += 1
            Wkv = wview(s2, 8, 256)
            S.dma("pool", lambda e: e.dma_start(out=Wkv[:, :, :], in_=src3[:, :, 2048:2304]), ("wch", s2), writes=[("w", s2)])
            S.alias(K_B, ALLR)
            for cc in range(4):
                qk_proj(l, s1, 512, cc * 128, 26, lambda n, cc=cc: qbT[:, cc, n * TC:(n + 1) * TC],
                        lambda n, cc=cc: [("qb", cc, n)], flush=False)
            qk_proj(l, s2, 256, 0, 27, lambda n: kbT[:, n * TC:(n + 1) * TC], lambda n: ["kb"])
            S.op("dve", lambda e: e.memset(vbt[:, :, 64:128], 1.0), reads=[], writes=["vbt"])
            for t0 in range(0, 16, 4):
                bk = bank()
                for gi in range(4):
                    t = t0 + gi
                    for k in range(NCH):
                        S.op("pe", lambda e, bk=bk, gi=gi, t=t, k=k: e.matmul(
                            psb[bk][:, gi * 128:(gi + 1) * 128], lhsT=hT[:, k, t * 128:(t + 1) * 128], rhs=Wkv[:, k, 128:256],
                            start=(k == 0), stop=(k == NCH - 1)),
                            reads=[("w", s2), ("h", t // 4)], writes=[("ps", bk)], inc=(k == NCH - 1 and gi == 3))
                S.op("act", lambda e, bk=bk, t0=t0: e.copy(
                    out=vbt[:, t0:t0 + 4, 0:64], in_=psb[bk][:, :].rearrange("p (t f) -> p t f", f=128)[:, :, 0:64]),
                    reads=[("ps", bk)], writes=["vbt"])
                S.op("dve", lambda e, bk=bk, t0=t0: e.tensor_copy(
                    out=vbt[:, t0:t0 + 4, 128:192], in_=psb[bk][:, :].rearrange("p (t f) -> p t f", f=128)[:, :, 64:128]),
                    reads=[("ps", bk)], writes=["vbt"])

            b_otiles = []
            for cc in range(4):
                for u in range(2):
                    hq = u * 4 + cc
                    rows = slice(u * 64, (u + 1) * 64)
                    vcols = slice(0, 128) if u == 0 else slice(64, 192)
                    orow = slice(0, 64) if u == 0 else slice(64, 128)
                    drow = slice(64, 128) if u == 0 else slice(0, 64)
                    otiles = []
                    for Q in range(4):
                        blocks = []
                        for b in range(4 * Q, 4 * Q + 4):
                            blk = []
                            for dj, shift in ((1, 0), (0, 128), (-1, 256)):
                                kt = b + dj
                                if kt < 0 or kt > 15:
                                    continue
                                blk.append(dict(k=kbT[rows, kt * 128:(kt + 1) * 128], q=qbT[rows, cc, b * 128:(b + 1) * 128], M=128,
                                                ecol=SBASE[3] + shift, eh=hq, v=vbt[:, kt, vcols],
                                                reads=[("qb", cc, Q), "kb"], vreads=["vbt"]))
                            blocks.append(blk)
                        otiles.append(dict(blocks=blocks, kind=Q, gsz=4))

                    def evacB(oi, ob, ot, cc=cc, hq=hq, orow=orow, drow=drow):
                        Q = ot["kind"]
                        ri = scr_ctr[0] % 2
                        scr_ctr[0] += 1
                        S.op("act", lambda e, ob=ob, ri=ri: e.activation(
                            out=scr[ri][drow, :], in_=psb[ob][drow, :], func=AF.Ln, bias=vd[drow, vb0 + 28 + hq: vb0 + 29 + hq]),
                            reads=[("ps", ob)] + VD, writes=[("scr", ri)])
                        S.op("act", lambda e, ri=ri: e.activation(out=scr[ri][drow, :], in_=scr[ri][drow, :], func=AF.Exp, scale=-1.0),
                            reads=[("scr", ri)], writes=[("scr", ri)])
                        S.op("dve", lambda e, ob=ob, ri=ri: e.tensor_tensor(
                            out=qbT[orow, cc, Q * 512:(Q + 1) * 512], in0=psb[ob][orow, :], in1=scr[ri][drow, :], op=ALU.mult),
                            reads=[("ps", ob), ("scr", ri)], writes=[("qb", cc, Q)])

                    for ot_ in otiles:
                        ot_["evac"] = evacB
                    b_otiles.extend(otiles)
            attention(b_otiles, None)
            if stage < 3:
                if dbg == 4:
                    S.dma("sp", lambda e: e.dma_start(out=dbg_d[:, :], in_=Rv(16 * KB, 16 * KB, BF16)), "dbgc", reads=K_B, writes=["dbgout"])
                break

            S.alias(K_G, ALLR)
            ybT = qbT
            for mb in range(2):
                sA = ring_ctr[0] % 3
                ring_ctr[0] += 1
                WA = wview(sA, 8, 512)
                S.dma("pool", lambda e, WA=WA, mb=mb: e.dma_start(
                    out=WA[:, 0:4, :], in_=w_a_d[l].rearrange("(k p) n -> p k n", p=128)[:, :, mb * 512:(mb + 1) * 512]),
                    ("wch", sA), writes=[("w", sA)])
                for u in range(2):
                    S.dma("pool", lambda e, WA=WA, mb=mb, u=u: e.dma_start(
                        out=WA[u * 64:(u + 1) * 64, 4:8, :],
                        in_=w_b_d[l][u * 256:(u + 1) * 256, mb * 512:(mb + 1) * 512].rearrange("(c e) n -> e c n", e=64)),
                        ("wch", sA), writes=[("w", sA)])
                for mi in range(4):
                    if mi % 2 == 0:
                        sGa = ring_ctr[0] % 3
                        ring_ctr[0] += 1
                        sGb = sGa
                        Wg2 = ring[sGa][:, 0:4096].rearrange("p (w k n) -> p w k n", w=2, k=8)
                        WGa = Wg2[:, 0]
                        WGb = Wg2[:, 1]
                        c0g = mb * 512 + (mi // 2) * 256
                        S.dma("pool", lambda e, WGa=WGa, c0g=c0g: e.dma_start(
                            out=WGa, in_=src3[:, :, 2304 + c0g:2304 + c0g + 256]), ("wch", sGa), writes=[("w", sGa)])
                        S.dma("pool", lambda e, WGb=WGb, c0g=c0g: e.dma_start(
                            out=WGb, in_=src3[:, :, 3328 + c0g:3328 + c0g + 256]), ("wch", sGb), writes=[("w", sGb)])
                    mg_ = mi % 2
                    for n in range(NTC):
                        nsl = slice(n * TC, (n + 1) * TC)
                        bA, bB, bGa, bGb = bank(), bank(), bank(), bank()
                        for k in range(4):
                            S.op("pe", lambda e, k=k, bA=bA, mi=mi, nsl=nsl, WA=WA: e.matmul(
                                psb[bA][:, :], lhsT=WA[:, k, mi * 128:(mi + 1) * 128], rhs=yaT[:, k, nsl], start=(k == 0), stop=(k == 3)),
                                reads=[("w", sA), ("ya", k)], writes=[("ps", bA)], inc=(k == 3))
                        for k in range(4):
                            S.op("pe", lambda e, k=k, bB=bB, mi=mi, nsl=nsl, WA=WA: e.matmul(
                                psb[bB][:, :], lhsT=WA[:, 4 + k, mi * 128:(mi + 1) * 128], rhs=ybT[:, k, nsl], start=(k == 0), stop=(k == 3)),
                                reads=[("w", sA), ("qb", k, n)], writes=[("ps", bB)], inc=(k == 3))
                        for k in range(NCH):
                            S.op("pe", lambda e, k=k, bGa=bGa, mg_=mg_, nsl=nsl, WGa=WGa: e.matmul(
                                psb[bGa][:, :], lhsT=WGa[:, k, mg_ * 128:(mg_ + 1) * 128], rhs=hT[:, k, nsl], start=(k == 0), stop=(k == 7)),
                                reads=[("w", sGa), ("h", n)], writes=[("ps", bGa)], inc=(k == 7))
                        for k in range(NCH):
                            S.op("pe", lambda e, k=k, bGb=bGb, mg_=mg_, nsl=nsl, WGb=WGb: e.matmul(
                                psb[bGb][:, :], lhsT=WGb[:, k, mg_ * 128:(mg_ + 1) * 128], rhs=hT[:, k, nsl], start=(k == 0), stop=(k == 7)),
                                reads=[("w", sGb), ("h", n)], writes=[("ps", bGb)], inc=(k == 7))
                        r1 = scr_ctr[0] % 2
                        r2 = (scr_ctr[0] + 1) % 2
                        scr_ctr[0] += 2
                        S.op("act", lambda e, bGa=bGa, r1=r1: e.activation(out=scr[r1][:, :], in_=psb[bGa][:, :], func=AF.Sigmoid),
                             reads=[("ps", bGa)], writes=[("scr", r1)])
                        S.op("act", lambda e, bGb=bGb, r2=r2: e.activation(out=scr[r2][:, :], in_=psb[bGb][:, :], func=AF.Sigmoid),
                             reads=[("ps", bGb)], writes=[("scr", r2)])
                        S.op("dve", lambda e, bA=bA, r1=r1: e.tensor_tensor(out=scr[r1][:, :], in0=psb[bA][:, :], in1=scr[r1][:, :], op=ALU.mult),
                             reads=[("ps", bA), ("scr", r1)], writes=[("scr", r1)])
                        S.op("dve", lambda e, bB=bB, r2=r2: e.tensor_tensor(out=scr[r2][:, :], in0=psb[bB][:, :], in1=scr[r2][:, :], op=ALU.mult),
                             reads=[("ps", bB), ("scr", r2)], writes=[("scr", r2)])
                        S.op("dve", lambda e, r1=r1, r2=r2, mi=mi, nsl=nsl: e.tensor_tensor(
                            out=merged[:, mi, nsl], in0=scr[r1][:, :], in1=scr[r2][:, :], op=ALU.add),
                            reads=[("scr", r1), ("scr", r2)], writes=[("mg", n)])
                sO = ring_ctr[0] % 3
                ring_ctr[0] += 1
                WO = wview(sO, 4, 1024)
                S.dma("pool", lambda e, WO=WO, mb=mb: e.dma_start(
                    out=WO[:, :, :], in_=w_out_d[l][mb * 512:(mb + 1) * 512, :].rearrange("(k p) n -> p k n", p=128)),
                    ("wch", sO), writes=[("w", sO)])
                if mb == 1 and stage >= 4:
                    preloaded[("gu", l, 0, 0)] = load_gu(l, 0, 0, 256)
                    preloaded[("gu", l, 0, 256)] = load_gu(l, 0, 256, 256)
                for n in range(NTC):
                    for m2 in range(NCH):
                        bk = bank()
                        for k in range(4):
                            S.op("pe", lambda e, k=k, n=n, bk=bk, m2=m2, WO=WO: e.matmul(
                                psb[bk][:, :], lhsT=WO[:, k, m2 * 128:(m2 + 1) * 128], rhs=merged[:, k, n * TC:(n + 1) * TC],
                                start=(k == 0), stop=(k == 3)),
                                reads=[("w", sO), ("mg", n)], writes=[("ps", bk)], inc=(k == 3))
                        S.op("dve", lambda e, bk=bk, m2=m2, n=n: e.tensor_tensor(
                            out=xT[:, m2, n * TC:(n + 1) * TC], in0=psb[bk][:, :], in1=xT[:, m2, n * TC:(n + 1) * TC], op=ALU.add),
                            reads=[("ps", bk), ("x", m2, n)], writes=[("x", m2, n)])
                    if mb == 1 and n >= 1 and stage >= 4:
                        norm_chunk(l, 8, n - 1)
                if mb == 1 and stage >= 4:
                    norm_chunk(l, 8, NTC - 1)
            if stage < 4:
                break

            S.alias(K_F, ALLR)
            wg3 = w_g_d[l].rearrange("(k p) n -> p k n", p=128)
            wu3 = w_u_d[l].rearrange("(k p) n -> p k n", p=128)
            for fb in range(2):
                f0 = fb * 1408
                fi = 0
                for cb, ncols in ((0, 256), (256, 256), (512, 256), (768, 256), (1024, 256), (1280, 128)):
                    if ("gu", l, fb, cb) in preloaded:
                        sG, WG, WU = preloaded.pop(("gu", l, fb, cb))
                    else:
                        sG, WG, WU = load_gu(l, fb, cb, ncols)
                    sU = sG
                    for mi in range(ncols // 128):
                        for half in range(2):
                            ns = (2 * half, 2 * half + 1)
                            bG = {n: bank() for n in ns}
                            bU = {n: bank() for n in ns}
                            for k in range(NCH):
                                for n in ns:
                                    S.op("pe", lambda e, k=k, n=n, bk=bG[n], mi=mi, WG=WG: e.matmul(
                                        psb[bk][:, :], lhsT=WG[:, k, mi * 128:(mi + 1) * 128], rhs=hT[:, k, n * TC:(n + 1) * TC],
                                        start=(k == 0), stop=(k == 7)),
                                        reads=[("w", sG), ("h", n)], writes=[("ps", bG[n])], inc=(k == 7))
                            for k in range(NCH):
                                for n in ns:
                                    S.op("pe", lambda e, k=k, n=n, bk=bU[n], mi=mi, WU=WU: e.matmul(
                                        psb[bk][:, :], lhsT=WU[:, k, mi * 128:(mi + 1) * 128], rhs=hT[:, k, n * TC:(n + 1) * TC],
                                        start=(k == 0), stop=(k == 7)),
                                        reads=[("w", sU), ("h", n)], writes=[("ps", bU[n])], inc=(k == 7))
                            for n in ns:
                                ri = scr_ctr[0] % 2
                                scr_ctr[0] += 1
                                S.op("act", lambda e, bk=bG[n], ri=ri: e.activation(out=scr[ri][:, :], in_=psb[bk][:, :], func=AF.Silu),
                                     reads=[("ps", bG[n])], writes=[("scr", ri)])
                                S.op("dve", lambda e, bk=bU[n], ri=ri, fi=fi, n=n: e.tensor_tensor(
                                    out=hid[:, fi, n * TC:(n + 1) * TC], in0=psb[bk][:, :], in1=scr[ri][:, :], op=ALU.mult),
                                    reads=[("ps", bU[n]), ("scr", ri)], writes=[("hid", n)])
                        fi += 1
                for mq in range(4):
                    sD = ring_ctr[0] % 3
                    ring_ctr[0] += 1
                    WD = wview(sD, 11, 256)
                    S.dma("pool", lambda e, WD=WD, f0=f0, mq=mq: e.dma_start(
                        out=WD[:, :, :], in_=w_d_d[l][f0:f0 + 1408, mq * 256:(mq + 1) * 256].rearrange("(k p) n -> p k n", p=128)),
                        ("wch", sD), writes=[("w", sD)])
                    for ml in range(2):
                        m2 = mq * 2 + ml
                        bks = [bank() for _ in range(NTC)]
                        for k in range(11):
                            for n in range(NTC):
                                S.op("pe", lambda e, k=k, n=n, bk=bks[n], ml=ml, WD=WD: e.matmul(
                                    psb[bk][:, :], lhsT=WD[:, k, ml * 128:(ml + 1) * 128], rhs=hid[:, k, n * TC:(n + 1) * TC],
                                    start=(k == 0), stop=(k == 10)),
                                    reads=[("w", sD), ("hid", n)], writes=[("ps", bks[n])], inc=(k == 10))
                        for n in range(NTC):
                            S.op("dve", lambda e, bk=bks[n], m2=m2, n=n: e.tensor_tensor(
                                out=xT[:, m2, n * TC:(n + 1) * TC], in0=psb[bk][:, :], in1=xT[:, m2, n * TC:(n + 1) * TC], op=ALU.add),
                                reads=[("ps", bks[n]), ("x", m2, n)], writes=[("x", m2, n)])
            if stage < 5:
                break

            sP = ring_ctr[0] % 3
            ring_ctr[0] += 1
            WP = wview(sP, 2, 1024)
            S.dma("pool", lambda e, WP=WP: e.dma_start(out=WP[:, :, :], in_=w_pp_d[l].rearrange("(k p) n -> p k n", p=128)),
                  ("wch", sP), writes=[("w", sP)])
            wpg3 = w_pg_d[l].rearrange("(k p) n -> p k n", p=128)
            WGs = []
            for mb in range(2):
                sG = ring_ctr[0] % 3
                ring_ctr[0] += 1
                WG = wview(sG, 8, 512)
                S.dma("pool", lambda e, WG=WG, mb=mb: e.dma_start(out=WG[:, :, :], in_=wpg3[:, :, mb * 512:(mb + 1) * 512]),
                      ("wch", sG), writes=[("w", sG)])
                WGs.append((sG, WG))
            rmsnorm_to_hT(l, 16)
            S.alias(K_P, ALLR)
            for half in range(2):
                S.dma("sp", lambda e, half=half: e.dma_start(
                    out=ptok[:, half * 8:(half + 1) * 8, :],
                    in_=p_d[l][half * 1024:(half + 1) * 1024, :].rearrange("(t p) f -> p t f", p=128)),
                    ("pl", half), writes=[("ptok", half)])
            for fc in range(2):
                for n in range(NTC):
                    bk = bank()
                    for i in range(4):
                        t = 4 * n + i
                        S.op("pe", lambda e, bk=bk, i=i, t=t, fc=fc: e.transpose(
                            out=psb[bk][:, i * 128:(i + 1) * 128], in_=ptok[:, t, fc * 128:(fc + 1) * 128], identity=ident_f),
                            reads=[("ptok", t // 8), "cst_f"], writes=[("ps", bk)], inc=(i == 3))
                    S.op("act", lambda e, bk=bk, fc=fc, n=n: e.copy(out=ppT[:, fc, n * TC:(n + 1) * TC], in_=psb[bk][:, :]),
                         reads=[("ps", bk)], writes=[("ppT", n)])
            last_layer = (l == n_layers - 1)
            if last_layer:
                S.alias(K_O, ALLR)

            def after_chunk(n):
                if not last_layer:
                    norm_chunk(l + 1, 0, n)
                else:
                    for t in range(4 * n, 4 * n + 4):
                        out_tile(t)
                    out_done.add(n)

            for n in range(NTC):
                for m2 in range(NCH):
                    mb, mi = divmod(m2, 4)
                    sG, WG = WGs[mb]
                    bG = bank()
                    for k in range(NCH):
                        S.op("pe", lambda e, k=k, n=n, bk=bG, mi=mi, WG=WG: e.matmul(
                            psb[bk][:, :], lhsT=WG[:, k, mi * 128:(mi + 1) * 128], rhs=hT[:, k, n * TC:(n + 1) * TC],
                            start=(k == 0), stop=(k == 7)),
                            reads=[("w", sG), ("h", n)], writes=[("ps", bG)], inc=(k == 7))
                    bP = bank()
                    for k in range(2):
                        S.op("pe", lambda e, k=k, n=n, bk=bP, m2=m2, WP=WP: e.matmul(
                            psb[bk][:, :], lhsT=WP[:, k, m2 * 128:(m2 + 1) * 128], rhs=ppT[:, k, n * TC:(n + 1) * TC],
                            start=(k == 0), stop=(k == 1)),
                            reads=[("w", sP), ("ppT", n)], writes=[("ps", bP)], inc=(k == 1))
                    ri = scr_ctr[0] % 2
                    scr_ctr[0] += 1
                    S.op("act", lambda e, bk=bG, ri=ri: e.activation(out=scr[ri][:, :], in_=psb[bk][:, :], func=AF.Sigmoid),
                         reads=[("ps", bG)], writes=[("scr", ri)])
                    S.op("dve", lambda e, bk=bP, ri=ri: e.tensor_tensor(out=scr[ri][:, :], in0=psb[bk][:, :], in1=scr[ri][:, :], op=ALU.mult),
                         reads=[("ps", bP), ("scr", ri)], writes=[("scr", ri)])
                    S.op("dve", lambda e, ri=ri, m2=m2, n=n: e.tensor_tensor(
                        out=xT[:, m2, n * TC:(n + 1) * TC], in0=scr[ri][:, :], in1=xT[:, m2, n * TC:(n + 1) * TC], op=ALU.add),
                        reads=[("scr", ri), ("x", m2, n)], writes=[("x", m2, n)])
                if n >= 1:
                    after_chunk(n - 1)
            after_chunk(NTC - 1)

        if len(out_done) < NTC:
            S.alias(K_O, ALLR)
            for t in range(16):
                if t // 4 not in out_done:
                    out_tile(t)
        S.wait_all("sp", [("yout", 0), ("yout", 1), "dbgout", "dbgout2"])

        with nc.Block() as block:
            @block.tensor
            def _(e):
                for f in S.prog["pe"]:
                    f(e)

            @block.scalar
            def _(e):
                for f in S.prog["act"]:
                    f(e)

            @block.vector
            def _(e):
                for f in S.prog["dve"]:
                    f(e)

            @block.gpsimd
            def _(e):
                for f in S.prog["pool"]:
                    f(e)

            @block.sync
            def _(e):
                for f in S.prog["sp"]:
                    f(e)
        build.stats = {k: len(v) for k, v in S.prog.items()}
        build.stats["waits"] = S.nwait
        build.stats["sems"] = len(S.sems)
    return nc


def wview_of(rg, k, n):
    return rg[:, 0:k * n].rearrange("p (k n) -> p k n", k=k)


def _pack_vecs(inp):
    v = np.zeros((128, 2 * NVL), np.float32)
    for l in range(2):
        b = l * NVL
        v[:, b + 0:b + 8] = np.asarray(inp["norm_mix_g"][l]).reshape(8, 128).T
        v[:, b + 8:b + 16] = np.asarray(inp["norm_ffn_g"][l]).reshape(8, 128).T
        v[:, b + 16:b + 24] = np.asarray(inp["norm_ple_g"][l]).reshape(8, 128).T
        v[:, b + 24] = np.tile(np.asarray(inp["qnorm_a_g"][l]), 2)
        v[:, b + 25] = np.tile(np.asarray(inp["knorm_a_g"][l]), 2)
        v[:, b + 26] = np.tile(np.asarray(inp["qnorm_b_g"][l]), 2)
        v[:, b + 27] = np.tile(np.asarray(inp["knorm_b_g"][l]), 2)
        v[:, b + 28:b + 36] = np.broadcast_to(np.asarray(inp["sink_b"][l])[None, :], (128, 8))
    return v


_NC_CACHE = {}


def make_in_maps(inp, n_cores=8):
    f = lambda a: np.ascontiguousarray(np.asarray(a, dtype=np.float32))
    shared = {k: f(inp[k]) for k in ("w_in", "w_branch_a", "w_branch_b", "w_out", "w_ffn_gate", "w_ffn_up",
                                     "w_ffn_down", "w_ple_gate", "w_ple_proj", "rel_table")}
    shared["vecs"] = _pack_vecs(inp)
    shared["cst"] = _cst()
    shared["emat"] = _emat()
    x = f(inp["x"])
    p = f(inp["p"])
    maps = []
    for b in range(n_cores):
        m = dict(shared)
        m["x"] = np.ascontiguousarray(x[b])
        m["p"] = np.ascontiguousarray(p[:, b])
        maps.append(m)
    return maps


def kernel(**inputs):
    if "nc" not in _NC_CACHE:
        _NC_CACHE["nc"] = build(2)
    nc = _NC_CACHE["nc"]
    in_maps = make_in_maps(inputs, 8)
    res = run_bass_kernel_spmd(nc, in_maps, core_ids=list(range(8)))
    out = np.stack([np.asarray(r["y"], dtype=np.float32) for r in res.results], axis=0)
    return out
```

```python
import numpy as np
from contextlib import ExitStack
import concourse.bass as bass
import concourse.mybir as mybir
from concourse.bass_utils import run_bass_kernel_spmd

F32 = mybir.dt.float32
BF16 = mybir.dt.bfloat16
AF = mybir.ActivationFunctionType
ALU = mybir.AluOpType

T = 2048
D = 1024
NCH = 8
TC = 512
NTC = 4
DFF = 2816
EPS = 1e-6
NEG = -30000.0
NVL = 36
STRIPS = [(1, 64, 128, 320), (4, 64, 128, 320), (16, 64, 0, 128), (1, 128, 128, 384)]
SBASE = [0, 320, 640, 768]
EW = 1152


class _Rec:
    def __init__(self):
        self.call = None

    def __getattr__(self, name):
        def f(*a, **k):
            self.call = (name, a, k)
            return self
        return f


def _capture(fn):
    r = _Rec()
    fn(r)
    assert r.call is not None
    return r.call


class Sched:
    ENG = ("pe", "act", "dve", "pool", "sp")

    def __init__(self, nc, stack):
        self.nc = nc
        self.stack = stack
        self.prog = {e: [] for e in self.ENG}
        self.sems = {}
        self.count = {}
        self.waited = {e: {} for e in self.ENG}
        self.res = {}
        self.nwait = 0

    def _sem(self, key):
        if key not in self.sems:
            name = "s_" + "_".join(str(k) for k in (key if isinstance(key, tuple) else (key,)))
            self.sems[key] = self.stack.enter_context(self.nc.semaphore(name))
            self.count[key] = 0
        return self.sems[key]

    def _deps(self, reads, writes):
        toks = []
        for r in reads:
            st = self.res.get(r)
            if st and st[0]:
                toks.append(st[0])
        for w in writes:
            st = self.res.get(w)
            if st:
                if st[0]:
                    toks.append(st[0])
                toks.extend(st[1])
        return toks

    def _waits(self, eng, toks, rawkeys):
        need = {}
        for (k, v) in toks:
            if v > need.get(k, 0):
                need[k] = v
        for k, v in need.items():
            if k == eng and (eng == "pe" or k not in rawkeys):
                continue
            if self.waited[eng].get(k, 0) >= v:
                continue
            self.waited[eng][k] = v
            s = self._sem(k)
            self.prog[eng].append(lambda e, s=s, v=v: e.wait_ge(s, v))
            self.nwait += 1

    def _record(self, tok, reads, writes):
        for r in reads:
            st = self.res.setdefault(r, [None, []])
            st[1].append(tok)
        for w in writes:
            self.res[w] = [tok, []]

    def op(self, eng, fn, reads=(), writes=(), inc=True):
        toks = self._deps(reads, writes)
        raw = set()
        for r in reads:
            st = self.res.get(r)
            if st and st[0]:
                raw.add(st[0][0])
        self._waits(eng, toks, raw)
        s = self._sem(eng)
        call = _capture(fn)
        if inc:
            self.count[eng] += 1
            tok = (eng, self.count[eng])
            self.prog[eng].append(lambda e, c=call, s=s: getattr(e, c[0])(*c[1], **c[2]).then_inc(s, 1))
        else:
            tok = (eng, self.count[eng] + 1)
            self.prog[eng].append(lambda e, c=call: getattr(e, c[0])(*c[1], **c[2]))
        self._record(tok, reads, writes)

    def dma(self, eng, fn, chan, reads=(), writes=()):
        toks = [t for t in self._deps(reads, writes) if t[0] != chan]
        self._waits(eng, toks, set(k for k, _ in toks))
        s = self._sem(chan)
        self.count[chan] += 16
        tok = (chan, self.count[chan])
        call = _capture(fn)
        self.prog[eng].append(lambda e, c=call, s=s: getattr(e, c[0])(*c[1], **c[2]).then_inc(s, 16))
        self._record(tok, reads, writes)

    def alias(self, new_keys, old_keys):
        toks = []
        nk = set(new_keys)
        for k in old_keys:
            if k in nk:
                continue
            st = self.res.get(k)
            if st:
                if st[0]:
                    toks.append(st[0])
                toks.extend(st[1])
        red = {}
        for (k, v) in toks:
            if v > red.get(k, 0):
                red[k] = v
        toks = list(red.items())
        for k in new_keys:
            st = self.res.setdefault(k, [None, []])
            st[1].extend(toks)

    def wait_all(self, eng, keys):
        toks = []
        for k in keys:
            st = self.res.get(k)
            if st:
                if st[0]:
                    toks.append(st[0])
                toks.extend(st[1])
        self._waits(eng, toks, set(k for k, _ in toks))


def _t5_bucket_np(rel):
    rel = np.asarray(rel, dtype=np.int64)
    half_b, max_exact = 16, 8
    sign = np.where(rel > 0, half_b, 0)
    n = np.abs(rel)
    nf = np.maximum(n, 1).astype(np.float32)
    large = max_exact + (np.log(nf / np.float32(max_exact)) / np.float32(np.log(1024 / max_exact))
                         * np.float32(half_b - max_exact)).astype(np.int32)
    large = np.minimum(large, half_b - 1)
    return sign + np.where(n < max_exact, n, large)


def _emat():
    E = np.zeros((33, len(STRIPS), 512), np.float32)
    for si, (d, half, c0, width) in enumerate(STRIPS):
        m = np.arange(512)
        rel = m - (width - 1 - c0)
        code = np.where(np.abs(rel) <= half, _t5_bucket_np(rel * d), 32)
        E[code, si, m] = 1.0
    return E.reshape(33, len(STRIPS) * 512)


def _cst():
    c = np.zeros((128, 512), np.float32)
    c[:, 384:512] = np.roll(np.eye(128, dtype=np.float32), 64, axis=1)
    c[:, 0:128] = np.eye(128, dtype=np.float32)
    c[:, 128:256] = 1.0
    c[0:64, 256:320] = 1.0
    c[64:128, 320:384] = 1.0
    return c


def build(n_layers=2, stage=99, dbg=0):
    nc = bass.Bass("TRN2", target_bir_lowering=False)
    dram = {}

    def din(name, shape):
        dram[name] = nc.dram_tensor(name, list(shape), F32, kind="ExternalInput").ap()
        return dram[name]

    x_d = din("x", [T, D])
    p_d = din("p", [2, T, 256])
    w_in_d = din("w_in", [2, D, 4352])
    w_a_d = din("w_branch_a", [2, 512, D])
    w_b_d = din("w_branch_b", [2, 512, D])
    w_out_d = din("w_out", [2, D, D])
    w_g_d = din("w_ffn_gate", [2, D, DFF])
    w_u_d = din("w_ffn_up", [2, D, DFF])
    w_d_d = din("w_ffn_down", [2, DFF, D])
    w_pg_d = din("w_ple_gate", [2, D, D])
    w_pp_d = din("w_ple_proj", [2, 256, D])
    vecs_d = din("vecs", [128, 2 * NVL])
    tab_d = din("rel_table", [32, 16])
    cst_d = din("cst", [128, 512])
    emat_d = din("emat", [33, 4 * 512])
    y_d = nc.dram_tensor("y", [T, D], F32, kind="ExternalOutput").ap()
    dbg_d = None
    if dbg:
        dbg_d = nc.dram_tensor("dbg", [128, 4 * T], BF16, kind="ExternalOutput").ap()

    with ExitStack() as stack:
        def sb(name, shape, dt):
            return stack.enter_context(nc.sbuf_tensor(name, list(shape), dt))

        xT = sb("xT", [128, NCH, T], F32)
        hT = sb("hT", [128, NCH, T], BF16)
        ring = [sb(f"ring{i}", [128, 4096], BF16) for i in range(3)]
        Ebuf = sb("Ebuf", [128, 8, EW], BF16)
        R = sb("R", [128, 26624], BF16)
        scr = [sb(f"scr{i}", [128, 512], F32) for i in range(2)]
        sq_all = sb("sq_all", [128, 4 * 512], BF16)
        sqb = [sq_all[:, i * 512:(i + 1) * 512] for i in range(4)]
        pTs = [sb(f"pT{i}", [128, 512], BF16) for i in range(6)]
        cst_f = sb("cst_f", [128, 256], F32)
        cst_b = sb("cst_b", [128, 384], BF16)
        vecs = sb("vecs_s", [128, 2 * NVL], F32)
        vd = sb("vd", [128, 2 * NVL], F32)
        tab = sb("tab", [33, 16], BF16)
        tabf = sb("tabf", [33, 16], F32)
        psb = [stack.enter_context(nc.psum_tensor(f"ps{i}", [128, 512], F32)) for i in range(8)]

        S = Sched(nc, stack)
        ident_f = cst_f[:, 0:128]
        swap_f = cst_f[:, 128:256]
        ident_b = cst_b[:, 0:128]
        ones_b = cst_b[:, 128:256]
        bones_b = cst_b[:, 256:384]

        def Rv(off_bytes, nbytes, dt):
            a = R[:, off_bytes // 2:(off_bytes + nbytes) // 2]
            return a if dt == BF16 else a.bitcast(dt)

        KB = 1024
        yaT = Rv(0, 16 * KB, BF16).rearrange("p (c t) -> p c t", c=4)
        qa_v = Rv(16 * KB, 4 * KB, BF16)
        ka_v = Rv(20 * KB, 4 * KB, BF16)
        vt = Rv(24 * KB, 53 * 384, BF16).rearrange("p (t f) -> p t f", f=192)
        acc = Rv(44 * KB, 8 * KB, F32)
        qbT = Rv(16 * KB, 16 * KB, BF16).rearrange("p (c t) -> p c t", c=4)
        kbT = Rv(32 * KB, 4 * KB, BF16)
        vbt = Rv(36 * KB, 16 * 384, BF16).rearrange("p (t f) -> p t f", f=192)
        merged = Rv(32 * KB, 16 * KB, BF16).rearrange("p (c t) -> p c t", c=4)
        hid = Rv(0, 44 * KB, BF16).rearrange("p (c t) -> p c t", c=11)
        ptok = Rv(0, 16 * KB, F32).rearrange("p (t f) -> p t f", f=256)
        ppT = Rv(16 * KB, 8 * KB, BF16).rearrange("p (c t) -> p c t", c=2)
        xtmp = Rv(0, 16 * KB, F32).rearrange("p (s f) -> p s f", s=4)
        Emat = Rv(16 * KB, 4 * KB, BF16)
        otile = Rv(0, 8 * KB, F32).rearrange("p (s f) -> p s f", s=2)
        K_SETUP = [("xtmp", i) for i in range(4)] + ["Emat"]
        K_YA = [("ya", i) for i in range(4)]
        K_A = ["qa", "ka", "vt", "acc"]
        K_B = [("qb", i, j) for i in range(4) for j in range(4)] + ["kb", "vbt"]
        K_G = [("mg", n) for n in range(4)]
        K_F = [("hid", n) for n in range(4)]
        K_P = [("ptok", 0), ("ptok", 1)] + [("ppT", n) for n in range(4)]
        K_O = [("ot", 0), ("ot", 1)]
        ALLR = K_SETUP + K_YA + K_A + K_B + K_G + K_F + K_P + K_O

        bank_ctr = [0]

        def bank():
            b = bank_ctr[0] % 8
            bank_ctr[0] += 1
            return b

        ring_ctr = [0]

        def wload(parts):
            s = ring_ctr[0] % 3
            ring_ctr[0] += 1
            for dst_fn, src in parts:
                dst = dst_fn(ring[s])
                S.dma("pool", lambda e, dst=dst, src=src: e.dma_start(out=dst, in_=src),
                      ("wch", s), writes=[("w", s)])
            return s

        def wview(s, k, n):
            return ring[s][:, 0:k * n].rearrange("p (k n) -> p k n", k=k)

        evac_ctr = [0]

        def load_pair_w(l, c):
            s_ = ring_ctr[0] % 3
            ring_ctr[0] += 1
            W_ = wview(s_, 8, 384)
            src_ = w_in_d[l].rearrange("(k p) n -> p k n", p=128)
            for part, base in enumerate((0, 512, 1024)):
                S.dma("pool", lambda e, W_=W_, part=part, base=base, src_=src_: e.dma_start(
                    out=W_[:, :, part * 128:(part + 1) * 128], in_=src_[:, :, base + c * 128: base + (c + 1) * 128]),
                    ("wch", s_), writes=[("w", s_)])
            return s_

        def load_gu(l, fb, cb, ncols):
            s_ = ring_ctr[0] % 3
            ring_ctr[0] += 1
            Wgu = ring[s_][:, 0:16 * ncols].rearrange("p (w k n) -> p w k n", w=2, k=8)
            WG_, WU_ = Wgu[:, 0], Wgu[:, 1]
            f0_ = fb * 1408 + cb
            wg3_ = w_g_d[l].rearrange("(k p) n -> p k n", p=128)
            wu3_ = w_u_d[l].rearrange("(k p) n -> p k n", p=128)
            S.dma("pool", lambda e: e.dma_start(out=WG_, in_=wg3_[:, :, f0_:f0_ + ncols]), ("wch", s_), writes=[("w", s_)])
            S.dma("pool", lambda e: e.dma_start(out=WU_, in_=wu3_[:, :, f0_:f0_ + ncols]), ("wch", s_), writes=[("w", s_)])
            return s_, WG_, WU_

        preloaded = {}
        S.dma("sp", lambda e: e.dma_start(out=cst_f[:, 0:128], in_=cst_d[:, 0:128]), "c_cstf", writes=["cst_f"])
        S.dma("sp", lambda e: e.dma_start(out=cst_f[:, 128:256], in_=cst_d[:, 384:512]), "c_cstf2", writes=["cst_f2"])
        S.dma("pool", lambda e: e.dma_start(out=cst_b[:], in_=cst_d[:, 0:384]), "c_cstb", writes=["cst_b"])
        S.dma("sp", lambda e: e.dma_start(out=vecs[:], in_=vecs_d[:]), "c_vecs", writes=["vecs"])
        S.dma("sp", lambda e: e.dma_start(out=tabf[0:32, :], in_=tab_d[:]), "c_tab", writes=["tabf"])
        S.op("dve", lambda e: e.memset(tabf[32:33, :], NEG), reads=[], writes=["tabf32"])
        S.op("dve", lambda e: e.tensor_copy(out=tab[0:32, :], in_=tabf[0:32, :]), reads=["tabf"], writes=["tab0"])
        S.op("dve", lambda e: e.tensor_copy(out=tab[32:33, :], in_=tabf[32:33, :]), reads=["tabf32"], writes=["tab1"])
        S.dma("pool", lambda e: e.dma_start(out=Emat[0:33, :], in_=emat_d[:]), "c_emat", writes=["Emat"])
        if n_layers >= 1 and stage >= 0.1:
            preloaded[(0, 0)] = load_pair_w(0, 0)
        for l in range(n_layers):
            b0 = l * NVL
            S.op("dve", lambda e, b0=b0: e.tensor_scalar_mul(out=vd[:, b0:b0 + 24], in0=vecs[:, b0:b0 + 24], scalar1=32.0),
                 reads=["vecs"], writes=[("vd", l, 0)])
            S.op("dve", lambda e, b0=b0: e.tensor_copy(out=vd[:, b0 + 24:b0 + 25], in_=vecs[:, b0 + 24:b0 + 25]),
                 reads=["vecs"], writes=[("vd", l, 1)])
            S.op("dve", lambda e, b0=b0: e.tensor_scalar_mul(out=vd[:, b0 + 25:b0 + 26], in0=vecs[:, b0 + 25:b0 + 26], scalar1=8.0),
                 reads=["vecs"], writes=[("vd", l, 2)])
            S.op("dve", lambda e, b0=b0: e.tensor_copy(out=vd[:, b0 + 26:b0 + 27], in_=vecs[:, b0 + 26:b0 + 27]),
                 reads=["vecs"], writes=[("vd", l, 3)])
            S.op("dve", lambda e, b0=b0: e.tensor_scalar_mul(out=vd[:, b0 + 27:b0 + 28], in0=vecs[:, b0 + 27:b0 + 28], scalar1=8.0),
                 reads=["vecs"], writes=[("vd", l, 4)])
            S.op("act", lambda e, b0=b0: e.activation(out=vd[:, b0 + 28:b0 + 36], in_=vecs[:, b0 + 28:b0 + 36], func=AF.Exp),
                 reads=["vecs"], writes=[("vd", l, 5)])
        VD = [("vd", l, i) for l in range(n_layers) for i in range(6)]

        def gen_strip(si_):
            d_, half_, c0_, width_ = STRIPS[si_]
            hs = 8 if si_ == 3 else 0
            for j0 in range(0, width_, 64):
                bk = bank()
                for jj in range(64):
                    jp = j0 + jj
                    e0 = si_ * 512 + (width_ - 1 - jp)
                    S.op("pe", lambda e, bk=bk, jj=jj, hs=hs, e0=e0: e.matmul(
                        psb[bk][:, jj * 8:(jj + 1) * 8], lhsT=Emat[0:33, e0:e0 + 128],
                        rhs=tab[0:33, hs:hs + 8], start=True, stop=True),
                        reads=["Emat", "tab0", "tab1"], writes=[("ps", bk)], inc=(jj == 63))
                S.op("act", lambda e, bk=bk, si_=si_, j0=j0: e.activation(
                    out=Ebuf[:, :, SBASE[si_] + j0: SBASE[si_] + j0 + 64],
                    in_=psb[bk][:, :].rearrange("p (j h) -> p h j", h=8), func=AF.Exp),
                    reads=[("ps", bk)], writes=["Ebuf"])

        for si_ in range(len(STRIPS)):
            gen_strip(si_)

        for n in range(NTC):
            for i in range(4):
                t = 4 * n + i
                S.dma("sp", lambda e, i=i, t=t: e.dma_start(out=xtmp[:, i, :], in_=x_d[t * 128:(t + 1) * 128, :]),
                      ("xl", i), writes=[("xtmp", i)])
            for c in range(NCH):
                bk = bank()
                for i in range(4):
                    S.op("pe", lambda e, bk=bk, i=i, c=c: e.transpose(
                        out=psb[bk][:, i * 128:(i + 1) * 128], in_=xtmp[:, i, c * 128:(c + 1) * 128], identity=ident_f),
                        reads=[("xtmp", i), "cst_f"], writes=[("ps", bk)], inc=(i == 3))
                eng = "act" if c % 2 == 0 else "dve"
                if eng == "act":
                    S.op("act", lambda e, bk=bk, c=c, n=n: e.copy(out=xT[:, c, n * TC:(n + 1) * TC], in_=psb[bk][:, :]),
                         reads=[("ps", bk)], writes=[("x", c, n)])
                else:
                    S.op("dve", lambda e, bk=bk, c=c, n=n: e.tensor_copy(out=xT[:, c, n * TC:(n + 1) * TC], in_=psb[bk][:, :]),
                         reads=[("ps", bk)], writes=[("x", c, n)])

        sq_ctr = [0]
        scr_ctr = [0]

        def norm_chunk(l, gcol, n, mid=None):
            bk = bank()
            for half in range(2):
                sis = {}
                for c in range(4 * half, 4 * half + 4):
                    si = sq_ctr[0] % 4
                    sq_ctr[0] += 1
                    sis[c] = si
                    xin = xT[:, c, n * TC:(n + 1) * TC]
                    if c % 2 == 0:
                        S.op("act", lambda e, si=si, xin=xin: e.activation(out=sqb[si], in_=xin, func=AF.Square),
                             reads=[("x", c, n)], writes=[("sq", si)])
                    else:
                        S.op("pool", lambda e, si=si, xin=xin: e.tensor_tensor(out=sqb[si], in0=xin, in1=xin, op=ALU.mult),
                             reads=[("x", c, n)], writes=[("sq", si)])
                if half == 0 and mid is not None:
                    mid()
                for c in range(4 * half, 4 * half + 4):
                    si = sis[c]
                    S.op("pe", lambda e, bk=bk, si=si, c=c: e.matmul(psb[bk][:, :], lhsT=ones_b, rhs=sqb[si],
                                                                    start=(c == 0), stop=(c == NCH - 1)),
                         reads=[("sq", si), "cst_b"], writes=[("ps", bk)], inc=True)
            ri = scr_ctr[0] % 2
            scr_ctr[0] += 1
            S.op("act", lambda e, bk=bk, ri=ri: e.activation(out=scr[ri][:, :], in_=psb[bk][:, :], func=AF.Ln, bias=1024.0 * EPS),
                 reads=[("ps", bk)], writes=[("scr", ri)])
            S.op("act", lambda e, ri=ri: e.activation(out=scr[ri][:, :], in_=scr[ri][:, :], func=AF.Exp, scale=-0.5),
                 reads=[("scr", ri)], writes=[("scr", ri)])
            for c in range(NCH):
                eng = "dve"
                S.op(eng, lambda e, c=c, n=n, ri=ri: e.scalar_tensor_tensor(
                    out=hT[:, c, n * TC:(n + 1) * TC], in0=xT[:, c, n * TC:(n + 1) * TC],
                    scalar=vd[:, l * NVL + gcol + c: l * NVL + gcol + c + 1], in1=scr[ri][:, :],
                    op0=ALU.mult, op1=ALU.mult),
                    reads=[("x", c, n), ("scr", ri)] + VD, writes=[("h", n)])

        def rmsnorm_to_hT(l, gcol, chunks=range(NTC)):
            for n in chunks:
                norm_chunk(l, gcol, n)

        qk_pend = []

        def qk_finish(l, bk, si, gcolidx, dst, dkeys):
            b2 = bank()
            S.op("pe", lambda e, b2=b2, si=si: e.matmul(psb[b2][:, :], lhsT=bones_b, rhs=sqb[si], start=True, stop=True),
                 reads=[("sq", si), "cst_b"], writes=[("ps", b2)])
            ri = scr_ctr[0] % 2
            scr_ctr[0] += 1
            S.op("act", lambda e, b2=b2, ri=ri: e.activation(out=scr[ri][:, :], in_=psb[b2][:, :], func=AF.Ln, bias=64.0 * EPS),
                 reads=[("ps", b2)], writes=[("scr", ri)])
            S.op("act", lambda e, ri=ri: e.activation(out=scr[ri][:, :], in_=scr[ri][:, :], func=AF.Exp, scale=-0.5),
                 reads=[("scr", ri)], writes=[("scr", ri)])
            S.op("dve", lambda e, bk=bk, ri=ri, dst=dst: e.scalar_tensor_tensor(
                out=dst, in0=psb[bk][:, :], scalar=vd[:, l * NVL + gcolidx: l * NVL + gcolidx + 1],
                in1=scr[ri][:, :], op0=ALU.mult, op1=ALU.mult),
                reads=[("ps", bk), ("scr", ri)] + VD, writes=dkeys)

        def qk_flush():
            while qk_pend:
                qk_finish(*qk_pend.pop(0))

        def qk_proj(l, slot, wk, wcol, gcolidx, dst_fn, dst_keys_fn, flush=True):
            W = wview(slot, 8, wk)
            for n in range(NTC):
                bk = bank()
                for k in range(NCH):
                    S.op("pe", lambda e, k=k, n=n, bk=bk: e.matmul(
                        psb[bk][:, :], lhsT=W[:, k, wcol:wcol + 128], rhs=hT[:, k, n * TC:(n + 1) * TC],
                        start=(k == 0), stop=(k == NCH - 1)),
                        reads=[("w", slot), ("h", n)], writes=[("ps", bk)], inc=(k == NCH - 1))
                si = sq_ctr[0] % 4
                sq_ctr[0] += 1
                S.op("act", lambda e, si=si, bk=bk: e.activation(out=sqb[si], in_=psb[bk][:, :], func=AF.Square),
                     reads=[("ps", bk)], writes=[("sq", si)])
                qk_pend.append((l, bk, si, gcolidx, dst_fn(n), dst_keys_fn(n)))
                if len(qk_pend) > 1:
                    qk_finish(*qk_pend.pop(0))
            if flush:
                qk_flush()

        def attention(pairs_by_otile, evac_fn, look=5, hook=None):
            groups = []
            for oi, ot in enumerate(pairs_by_otile):
                flat = [(bi, pi, p) for bi, blk in enumerate(ot["blocks"]) for pi, p in enumerate(blk)]
                gsz = ot.get("gsz", 4)
                chunks = [flat[i:i + gsz] for i in range(0, len(flat), gsz)]
                for ci, ch in enumerate(chunks):
                    groups.append((oi, ch, ci == len(chunks) - 1))
            state = {}
            obank = {}
            mul_ctr = [0]

            def stage1(gi):
                oi, ch, _ = groups[gi]
                bk = bank()
                pslot = gi % 6
                for j, (bi, pi, p) in enumerate(ch):
                    M = p["M"]
                    cols = slice(j * 128, (j + 1) * 128)
                    S.op("pe", lambda e, bk=bk, p=p, M=M, cols=cols: e.matmul(
                        psb[bk][0:M, cols], lhsT=p["k"], rhs=p["q"], start=True, stop=True),
                        reads=p["reads"], writes=[("ps", bk)], inc=(j == len(ch) - 1))
                j0 = 0
                while j0 < len(ch):
                    j1 = j0
                    while j1 < len(ch) and ch[j1][2]["M"] == ch[j0][2]["M"]:
                        j1 += 1
                    M = ch[j0][2]["M"]
                    S.op("act", lambda e, bk=bk, pslot=pslot, j0=j0, j1=j1, M=M: e.activation(
                        out=pTs[pslot][0:M, j0 * 128:j1 * 128], in_=psb[bk][0:M, j0 * 128:j1 * 128], func=AF.Exp),
                        reads=[("ps", bk)], writes=[("pT", pslot)])
                    j0 = j1
                ecs = [p["ecol"] for (_, _, p) in ch]
                full = all(p["M"] == 128 for (_, _, p) in ch)
                rep = None
                if full and len(ch) == 4 and ecs[0] == ecs[2] and ecs[1] == ecs[3] and ecs[1] == ecs[0] + 128:
                    rep = (2, 256)
                elif full and len(ch) == 4 and ecs[0] == ecs[1] == ecs[2] == ecs[3]:
                    rep = (4, 128)
                if rep is not None:
                    a_, b_ = rep
                    eh = ch[0][2]["eh"]
                    ec = ecs[0]
                    pv = pTs[pslot][:, 0:512].rearrange("p (a b) -> p a b", a=a_)
                    ev = Ebuf[:, eh, ec:ec + b_].unsqueeze(1).broadcast_to([128, a_, b_])
                    S.op("dve", lambda e, pv=pv, ev=ev: e.tensor_tensor(out=pv, in0=pv, in1=ev, op=ALU.mult),
                         reads=[("pT", pslot), "Ebuf"], writes=[("pT", pslot)])
                j0 = 0 if rep is None else len(ch)
                while j0 < len(ch):
                    j1 = j0 + 1
                    while (j1 < len(ch) and ch[j1][2]["M"] == ch[j0][2]["M"]
                           and ch[j1][2]["ecol"] == ch[j1 - 1][2]["ecol"] + 128):
                        j1 += 1
                    M = ch[j0][2]["M"]
                    ec = ch[j0][2]["ecol"]
                    eh = ch[j0][2]["eh"]
                    eng = "dve"
                    mul_ctr[0] += 1
                    S.op(eng, lambda e, pslot=pslot, j0=j0, j1=j1, M=M, ec=ec, eh=eh: e.tensor_tensor(
                        out=pTs[pslot][0:M, j0 * 128:j1 * 128], in0=pTs[pslot][0:M, j0 * 128:j1 * 128],
                        in1=Ebuf[0:M, eh, ec:ec + (j1 - j0) * 128], op=ALU.mult),
                        reads=[("pT", pslot), "Ebuf"], writes=[("pT", pslot)])
                    j0 = j1
                state[gi] = pslot

            def stage2(gi):
                oi, ch, last = groups[gi]
                ot = pairs_by_otile[oi]
                first_in_bank = oi not in obank
                if first_in_bank:
                    obank[oi] = bank()
                ob = obank[oi]
                pslot = state[gi]
                order = sorted(range(len(ch)), key=lambda j: (ch[j][1], ch[j][0]))
                for oi_, j in enumerate(order):
                    bi, pi, p = ch[j]
                    M = p["M"]
                    nb = len(ot["blocks"][bi])
                    S.op("pe", lambda e, ob=ob, p=p, M=M, bi=bi, pi=pi, nb=nb, pslot=pslot, j=j, st=(first_in_bank and oi_ == 0): e.matmul(
                        psb[ob][:, bi * 128:(bi + 1) * 128], lhsT=p["v"], rhs=pTs[pslot][0:M, j * 128:(j + 1) * 128],
                        start=st, stop=(pi == nb - 1), skip_group_check=True),
                        reads=[("pT", pslot)] + p["vreads"], writes=[("ps", ob)], inc=(oi_ == len(ch) - 1))
                if last:
                    (ot.get("evac") or evac_fn)(oi, ob, ot)
                    if ot.get("post") is not None:
                        ot["post"]()

            G = len(groups)
            for i in range(G + look):
                if i < G:
                    stage1(i)
                if hook is not None and i == min(2, G - 1):
                    hook()
                    hook = None
                if i - look >= 0:
                    stage2(i - look)

        out_done = set()

        def out_tile(t):
            so = t % 2
            b2 = [bank(), bank()]
            for c in range(NCH):
                bk = b2[c // 4]
                S.op("pe", lambda e, bk=bk, c=c, t=t: e.transpose(
                    out=psb[bk][:, (c % 4) * 128:(c % 4 + 1) * 128], in_=xT[:, c, t * 128:(t + 1) * 128], identity=ident_f),
                    reads=[("x", c, t // 4), "cst_f"], writes=[("ps", bk)], inc=(c % 4 == 3))
            S.op("act", lambda e, bk=b2[0], so=so: e.copy(out=otile[:, so, 0:512], in_=psb[bk][:, :]),
                 reads=[("ps", b2[0])], writes=[("ot", so)])
            S.op("dve", lambda e, bk=b2[1], so=so: e.tensor_copy(out=otile[:, so, 512:1024], in_=psb[bk][:, :]),
                 reads=[("ps", b2[1])], writes=[("ot", so)])
            S.dma("sp", lambda e, t=t, so=so: e.dma_start(out=y_d[t * 128:(t + 1) * 128, :], in_=otile[:, so, :]),
                  ("och", so), reads=[("ot", so)], writes=[("yout", so)])

        for l in range(n_layers):
            vb0 = l * NVL
            if stage < 0.1:
                break
            if l == 0:
                for n in range(NTC):
                    norm_chunk(l, 0, n)
            if stage < 1 and stage <= 0.2:
                if dbg == 3:
                    S.dma("sp", lambda e: e.dma_start(out=dbg_d[:, :], in_=hT[:, 0:4, :].rearrange("p c t -> p (c t)")), "dbgc", reads=[("h", n) for n in range(4)], writes=["dbgout"])
                break

            def a_tiles():
                tiles = []
                idx = {}
                for j in range(17):
                    s0, s1 = max(128 * j - 64, 0), min(128 * j + 64, T)
                    idx[(0, 0, j)] = len(tiles)
                    tiles.append((s0, 1, s1 - s0, 64 if j == 0 else 0))
                for r in range(4):
                    for j in range(5):
                        s0, s1 = max(128 * j - 64, 0), min(128 * j + 64, 512)
                        idx[(1, r, j)] = len(tiles)
                        tiles.append((r + 4 * s0, 4, s1 - s0, 64 if j == 0 else 0))
                for r in range(16):
                    idx[(2, r, 0)] = len(tiles)
                    tiles.append((r, 16, 128, 0))
                return tiles, idx

            tiles, tidx = a_tiles()
            pending_fin = [None]

            def tok_ap(buf2d, rows, tok0, step, cnt):
                if step == 1:
                    return buf2d[rows, tok0:tok0 + cnt]
                v = buf2d.rearrange("p (s r) -> p r s", r=step)
                return v[rows, tok0 % step, tok0 // step: tok0 // step + cnt]

            for c in range(4):
                src3 = w_in_d[l].rearrange("(k p) n -> p k n", p=128)
                if (l, c) in preloaded:
                    s = preloaded.pop((l, c))
                else:
                    s = load_pair_w(l, c)
                W = wview(s, 8, 384)
                akeys = ["qa", "ka", "vt", "acc"]
                if c == 0:
                    S.alias(K_A + K_YA, ALLR)
                qk_proj(l, s, 384, 0, 24, lambda n: qa_v[:, n * TC:(n + 1) * TC], lambda n: ["qa"], flush=False)
                qk_proj(l, s, 384, 128, 25, lambda n: ka_v[:, n * TC:(n + 1) * TC], lambda n: ["ka"], flush=True)
                if pending_fin[0] is not None:
                    pending_fin[0]()
                    pending_fin[0] = None
                if stage <= 0.4:
                    if dbg == 2:
                        S.dma("sp", lambda e: e.dma_start(out=dbg_d[:, 0:T], in_=qa_v), "dbgc", reads=["qa"], writes=["dbgout"])
                        S.dma("sp", lambda e: e.dma_start(out=dbg_d[:, T:2 * T], in_=ka_v), "dbgc2", reads=["ka"], writes=["dbgout2"])
                    break
                S.op("dve", lambda e: e.memset(vt[:, :, 64:128], 1.0), reads=[], writes=["vt"])
                vT = sq_all
                for n in range(NTC):
                    bk = bank()
                    for k in range(NCH):
                        S.op("pe", lambda e, k=k, n=n, bk=bk, W=W: e.matmul(
                            psb[bk][:, :], lhsT=W[:, k, 256:384], rhs=hT[:, k, n * TC:(n + 1) * TC],
                            start=(k == 0), stop=(k == NCH - 1)),
                            reads=[("w", s), ("h", n)], writes=[("ps", bk)], inc=(k == NCH - 1))
                    S.op("act", lambda e, bk=bk, n=n: e.copy(out=vT[:, n * TC:(n + 1) * TC], in_=psb[bk][:, :]),
                         reads=[("ps", bk)], writes=[("sq", n)])
                for t0 in range(0, len(tiles), 4):
                    grp = tiles[t0:t0 + 4]
                    bk = bank()
                    for gi, (tok0, step, M, i0) in enumerate(grp):
                        lhs = tok_ap(vT, slice(0, 128), tok0, step, M)
                        S.op("pe", lambda e, bk=bk, gi=gi, M=M, lhs=lhs: e.matmul(
                            psb[bk][0:M, gi * 128:(gi + 1) * 128], lhsT=lhs, rhs=ident_b,
                            start=True, stop=True),
                            reads=["cst_b"] + [("sq", n) for n in range(NTC)], writes=[("ps", bk)],
                            inc=(gi == len(grp) - 1))
                    ng = len(grp)
                    vo = vt[:, t0:t0 + ng, :].rearrange("p t (a b) -> p t a b", a=3)[:, :, 0:3:2, :]
                    vi = psb[bk][:, 0:ng * 128].rearrange("p (t a b) -> p t a b", a=2, b=64)
                    if (t0 // 4) % 2 == 0:
                        S.op("act", lambda e, vo=vo, vi=vi: e.copy(out=vo, in_=vi), reads=[("ps", bk)], writes=["vt"])
                    else:
                        S.op("dve", lambda e, vo=vo, vi=vi: e.tensor_copy(out=vo, in_=vi), reads=[("ps", bk)], writes=["vt"])
                if stage <= 0.6:
                    break
                pair_otiles = []
                for u in range(2):
                    h = 2 * c + u
                    rows = slice(u * 64, (u + 1) * 64)
                    vcols = slice(0, 128) if u == 0 else slice(64, 192)
                    orow = slice(0, 64) if u == 0 else slice(64, 128)
                    drow = slice(64, 128) if u == 0 else slice(0, 64)

                    def mkpair(pat, r, j, qtok0, step, kind):
                        tok0, st, M, i0 = tiles[tidx[(pat, r, j)]]
                        ti = tidx[(pat, r, j)]
                        if pat == 2:
                            shift = 0
                        elif kind == "t1":
                            shift = 64
                        else:
                            shift = 128 if i0 == 64 else 192
                        return dict(k=tok_ap(ka_v, rows, tok0, st, M), q=tok_ap(qa_v, rows, qtok0, step, 128), M=M,
                                    ecol=SBASE[pat] + shift, eh=h, v=vt[0:M, ti, vcols],
                                    reads=["qa", "ka"], vreads=["vt"])

                    otiles = []
                    for Q in range(4):
                        blocks = []
                        for b in range(4 * Q, 4 * Q + 4):
                            blocks.append([mkpair(0, 0, b + 1, 128 * b, 1, "t1"), mkpair(0, 0, b, 128 * b, 1, "t0")])
                        otiles.append(dict(blocks=blocks, kind=("p1", Q)))
                    for r in range(4):
                        blocks = []
                        for b in range(4):
                            blocks.append([mkpair(1, r, b + 1, r + 4 * 128 * b, 4, "t1"), mkpair(1, r, b, r + 4 * 128 * b, 4, "t0")])
                        otiles.append(dict(blocks=blocks, kind=("p2", r)))
                    for r0 in range(0, 16, 4):
                        blocks = [[mkpair(2, r, 0, r, 16, "t")] for r in range(r0, r0 + 4)]
                        otiles.append(dict(blocks=blocks, kind=("p3", r0)))

                    def evacA(oi, ob, ot):
                        kind, v = ot["kind"]
                        if kind == "p1":
                            S.op("dve", lambda e, ob=ob, v=v: e.tensor_copy(out=acc[:, v * 512:(v + 1) * 512], in_=psb[ob][:, :]),
                                 reads=[("ps", ob)], writes=["acc"])
                        elif kind == "p2":
                            av = acc.rearrange("p (s r) -> p r s", r=4)[:, v, :]
                            S.op("dve", lambda e, ob=ob, av=av: e.tensor_tensor(out=av, in0=psb[ob][:, :], in1=av, op=ALU.add),
                                 reads=[("ps", ob), "acc"], writes=["acc"])
                        else:
                            av = acc.rearrange("p (s r) -> p r s", r=16)[:, v:v + 4, :]
                            S.op("dve", lambda e, ob=ob, av=av: e.tensor_tensor(
                                out=av, in0=psb[ob][:, :].rearrange("p (r s) -> p r s", r=4), in1=av, op=ALU.add),
                                reads=[("ps", ob), "acc"], writes=["acc"])


                    def make_fin(c=c, orow=orow, drow=drow):
                        def fin():
                            for n in range(NTC):
                                bk = bank()
                                S.op("act", lambda e, bk=bk, n=n: e.activation(out=psb[bk][drow, :], in_=acc[drow, n * TC:(n + 1) * TC], func=AF.Ln),
                                     reads=["acc"], writes=[("ps", bk)])
                                S.op("act", lambda e, bk=bk: e.activation(out=psb[bk][drow, :], in_=psb[bk][drow, :], func=AF.Exp, scale=-1.0),
                                     reads=[("ps", bk)], writes=[("ps", bk)])
                                S.op("dve", lambda e, bk=bk, n=n: e.tensor_tensor(
                                    out=yaT[orow, c, n * TC:(n + 1) * TC], in0=acc[orow, n * TC:(n + 1) * TC], in1=psb[bk][drow, :], op=ALU.mult),
                                    reads=["acc", ("ps", bk)], writes=[("ya", c)])
                        return fin
                    for ot_ in otiles:
                        ot_["evac"] = evacA
                    if u == 0:
                        otiles[-1]["post"] = make_fin()
                    else:
                        pending_fin[0] = make_fin()
                    pair_otiles.extend(otiles)
                attention(pair_otiles, None)
            if pending_fin[0] is not None:
                pending_fin[0]()
                pending_fin[0] = None
            if stage <= 0.6:
                break
            if stage < 2:
                if dbg == 1:
                    S.dma("sp", lambda e: e.dma_start(out=dbg_d[:, :], in_=Rv(0, 16 * KB, BF16)), "dbgc", reads=K_YA, writes=["dbgout"])
                elif dbg == 2:
                    S.dma("sp", lambda e: e.dma_start(out=dbg_d[:, 0:T], in_=qa_v), "dbgc", reads=["qa"], writes=["dbgout"])
                    S.dma("sp", lambda e: e.dma_start(out=dbg_d[:, T:2 * T], in_=ka_v), "dbgc2", reads=["ka"], writes=["dbgout2"])
                elif dbg == 3:
                    S.dma("sp", lambda e: e.dma_start(out=dbg_d[:, :], in_=hT[:, 0:4, :].rearrange("p c t -> p (c t)")), "dbgc", reads=[("h", n) for n in range(4)], writes=["dbgout"])
                break

            s1 = ring_ctr[0] % 3
            ring_ctr[0] += 1
            Wq = wview(s1, 8, 512)
            src3 = w_in_d[l].rearrange("(k p) n -> p k n", p=128)
            Wq5 = ring[s1][:, 0:4096].rearrange("p (k c u e) -> p k c u e", k=8, c=4, u=2)
            for u in range(2):
                for cq in range(4):
                    S.dma("pool", lambda e, u=u, cq=cq, Wq5=Wq5, src3=src3: e.dma_start(
                        out=Wq5[:, :, cq, u, :],
                        in_=src3[:, :, 1536 + u * 256 + cq * 64:1536 + u * 256 + (cq + 1) * 64]),
                        ("wch", s1), writes=[("w", s1)])
            s2 = ring_ctr[0] % 3
            ring_ctr[0] += 1
            Wkv = wview(s2, 8, 256)
            S.dma("pool", lambda e: e.dma_start(out=Wkv[:, :, :], in_=src3[:, :, 2048:2304]), ("wch", s2), writes=[("w", s2)])
            S.alias(K_B, ALLR)
            for cc in range(4):
                qk_proj(l, s1, 512, cc * 128, 26, lambda n, cc=cc: qbT[:, cc, n * TC:(n + 1) * TC],
                        lambda n, cc=cc: [("qb", cc, n)], flush=False)
            qk_proj(l, s2, 256, 0, 27, lambda n: kbT[:, n * TC:(n + 1) * TC], lambda n: ["kb"])
            S.op("dve", lambda e: e.memset(vbt[:, :, 64:128], 1.0), reads=[], writes=["vbt"])
            for t0 in range(0, 16, 4):
                bk = bank()
                for gi in range(4):
                    t = t0 + gi
                    for k in range(NCH):
                        S.op("pe", lambda e, bk=bk, gi=gi, t=t, k=k: e.matmul(
                            psb[bk][:, gi * 128:(gi + 1) * 128], lhsT=hT[:, k, t * 128:(t + 1) * 128], rhs=Wkv[:, k, 128:256],
                            start=(k == 0), stop=(k == NCH - 1)),
                            reads=[("w", s2), ("h", t // 4)], writes=[("ps", bk)], inc=(k == NCH - 1 and gi == 3))
                S.op("act", lambda e, bk=bk, t0=t0: e.copy(
                    out=vbt[:, t0:t0 + 4, 0:64], in_=psb[bk][:, :].rearrange("p (t f) -> p t f", f=128)[:, :, 0:64]),
                    reads=[("ps", bk)], writes=["vbt"])
                S.op("dve", lambda e, bk=bk, t0=t0: e.tensor_copy(
                    out=vbt[:, t0:t0 + 4, 128:192], in_=psb[bk][:, :].rearrange("p (t f) -> p t f", f=128)[:, :, 64:128]),
                    reads=[("ps", bk)], writes=["vbt"])

            b_otiles = []
            for cc in range(4):
                for u in range(2):
                    hq = u * 4 + cc
                    rows = slice(u * 64, (u + 1) * 64)
                    vcols = slice(0, 128) if u == 0 else slice(64, 192)
                    orow = slice(0, 64) if u == 0 else slice(64, 128)
                    drow = slice(64, 128) if u == 0 else slice(0, 64)
                    otiles = []
                    for Q in range(4):
                        blocks = []
                        for b in range(4 * Q, 4 * Q + 4):
                            blk = []
                            for dj, shift in ((1, 0), (0, 128), (-1, 256)):
                                kt = b + dj
                                if kt < 0 or kt > 15:
                                    continue
                                blk.append(dict(k=kbT[rows, kt * 128:(kt + 1) * 128], q=qbT[rows, cc, b * 128:(b + 1) * 128], M=128,
                                                ecol=SBASE[3] + shift, eh=hq, v=vbt[:, kt, vcols],
                                                reads=[("qb", cc, Q), "kb"], vreads=["vbt"]))
                            blocks.append(blk)
                        otiles.append(dict(blocks=blocks, kind=Q, gsz=4))

                    def evacB(oi, ob, ot, cc=cc, hq=hq, orow=orow, drow=drow):
                        Q = ot["kind"]
                        ri = scr_ctr[0] % 2
                        scr_ctr[0] += 1
                        S.op("act", lambda e, ob=ob, ri=ri: e.activation(
                            out=scr[ri][drow, :], in_=psb[ob][drow, :], func=AF.Ln, bias=vd[drow, vb0 + 28 + hq: vb0 + 29 + hq]),
                            reads=[("ps", ob)] + VD, writes=[("scr", ri)])
                        S.op("act", lambda e, ri=ri: e.activation(out=scr[ri][drow, :], in_=scr[ri][drow, :], func=AF.Exp, scale=-1.0),
                            reads=[("scr", ri)], writes=[("scr", ri)])
                        S.op("dve", lambda e, ob=ob, ri=ri: e.tensor_tensor(
                            out=qbT[orow, cc, Q * 512:(Q + 1) * 512], in0=psb[ob][orow, :], in1=scr[ri][drow, :], op=ALU.mult),
                            reads=[("ps", ob), ("scr", ri)], writes=[("qb", cc, Q)])

                    for ot_ in otiles:
                        ot_["evac"] = evacB
                    b_otiles.extend(otiles)
            attention(b_otiles, None)
            if stage < 3:
                if dbg == 4:
                    S.dma("sp", lambda e: e.dma_start(out=dbg_d[:, :], in_=Rv(16 * KB, 16 * KB, BF16)), "dbgc", reads=K_B, writes=["dbgout"])
                break

            S.alias(K_G, ALLR)
            ybT = qbT
            for mb in range(2):
                sA = ring_ctr[0] % 3
                ring_ctr[0] += 1
                WA = wview(sA, 8, 512)
                S.dma("pool", lambda e, WA=WA, mb=mb: e.dma_start(
                    out=WA[:, 0:4, :], in_=w_a_d[l].rearrange("(k p) n -> p k n", p=128)[:, :, mb * 512:(mb + 1) * 512]),
                    ("wch", sA), writes=[("w", sA)])
                for u in range(2):
                    S.dma("pool", lambda e, WA=WA, mb=mb, u=u: e.dma_start(
                        out=WA[u * 64:(u + 1) * 64, 4:8, :],
                        in_=w_b_d[l][u * 256:(u + 1) * 256, mb * 512:(mb + 1) * 512].rearrange("(c e) n -> e c n", e=64)),
                        ("wch", sA), writes=[("w", sA)])
                for mi in range(4):
                    if mi % 2 == 0:
                        sGa = ring_ctr[0] % 3
                        ring_ctr[0] += 1
                        sGb = sGa
                        Wg2 = ring[sGa][:, 0:4096].rearrange("p (w k n) -> p w k n", w=2, k=8)
                        WGa = Wg2[:, 0]
                        WGb = Wg2[:, 1]
                        c0g = mb * 512 + (mi // 2) * 256
                        S.dma("pool", lambda e, WGa=WGa, c0g=c0g: e.dma_start(
                            out=WGa, in_=src3[:, :, 2304 + c0g:2304 + c0g + 256]), ("wch", sGa), writes=[("w", sGa)])
                        S.dma("pool", lambda e, WGb=WGb, c0g=c0g: e.dma_start(
                            out=WGb, in_=src3[:, :, 3328 + c0g:3328 + c0g + 256]), ("wch", sGb), writes=[("w", sGb)])
                    mg_ = mi % 2
                    for n in range(NTC):
                        nsl = slice(n * TC, (n + 1) * TC)
                        bA, bB, bGa, bGb = bank(), bank(), bank(), bank()
                        for k in range(4):
                            S.op("pe", lambda e, k=k, bA=bA, mi=mi, nsl=nsl, WA=WA: e.matmul(
                                psb[bA][:, :], lhsT=WA[:, k, mi * 128:(mi + 1) * 128], rhs=yaT[:, k, nsl], start=(k == 0), stop=(k == 3)),
                                reads=[("w", sA), ("ya", k)], writes=[("ps", bA)], inc=(k == 3))
                        for k in range(4):
                            S.op("pe", lambda e, k=k, bB=bB, mi=mi, nsl=nsl, WA=WA: e.matmul(
                                psb[bB][:, :], lhsT=WA[:, 4 + k, mi * 128:(mi + 1) * 128], rhs=ybT[:, k, nsl], start=(k == 0), stop=(k == 3)),
                                reads=[("w", sA), ("qb", k, n)], writes=[("ps", bB)], inc=(k == 3))
                        for k in range(NCH):
                            S.op("pe", lambda e, k=k, bGa=bGa, mg_=mg_, nsl=nsl, WGa=WGa: e.matmul(
                                psb[bGa][:, :], lhsT=WGa[:, k, mg_ * 128:(mg_ + 1) * 128], rhs=hT[:, k, nsl], start=(k == 0), stop=(k == 7)),
                                reads=[("w", sGa), ("h", n)], writes=[("ps", bGa)], inc=(k == 7))
                        for k in range(NCH):
                            S.op("pe", lambda e, k=k, bGb=bGb, mg_=mg_, nsl=nsl, WGb=WGb: e.matmul(
                                psb[bGb][:, :], lhsT=WGb[:, k, mg_ * 128:(mg_ + 1) * 128], rhs=hT[:, k, nsl], start=(k == 0), stop=(k == 7)),
                                reads=[("w", sGb), ("h", n)], writes=[("ps", bGb)], inc=(k == 7))
                        r1 = scr_ctr[0] % 2
                        r2 = (scr_ctr[0] + 1) % 2
                        scr_ctr[0] += 2
                        S.op("act", lambda e, bGa=bGa, r1=r1: e.activation(out=scr[r1][:, :], in_=psb[bGa][:, :], func=AF.Sigmoid),
                             reads=[("ps", bGa)], writes=[("scr", r1)])
                        S.op("act", lambda e, bGb=bGb, r2=r2: e.activation(out=scr[r2][:, :], in_=psb[bGb][:, :], func=AF.Sigmoid),
                             reads=[("ps", bGb)], writes=[("scr", r2)])
                        S.op("dve", lambda e, bA=bA, r1=r1: e.tensor_tensor(out=scr[r1][:, :], in0=psb[bA][:, :], in1=scr[r1][:, :], op=ALU.mult),
                             reads=[("ps", bA), ("scr", r1)], writes=[("scr", r1)])
                        S.op("dve", lambda e, bB=bB, r2=r2: e.tensor_tensor(out=scr[r2][:, :], in0=psb[bB][:, :], in1=scr[r2][:, :], op=ALU.mult),
                             reads=[("ps", bB), ("scr", r2)], writes=[("scr", r2)])
                        S.op("dve", lambda e, r1=r1, r2=r2, mi=mi, nsl=nsl: e.tensor_tensor(
                            out=merged[:, mi, nsl], in0=scr[r1][:, :], in1=scr[r2][:, :], op=ALU.add),
                            reads=[("scr", r1), ("scr", r2)], writes=[("mg", n)])
                sO = ring_ctr[0] % 3
                ring_ctr[0] += 1
                WO = wview(sO, 4, 1024)
                S.dma("pool", lambda e, WO=WO, mb=mb: e.dma_start(
                    out=WO[:, :, :], in_=w_out_d[l][mb * 512:(mb + 1) * 512, :].rearrange("(k p) n -> p k n", p=128)),
                    ("wch", sO), writes=[("w", sO)])
                if mb == 1 and stage >= 4:
                    preloaded[("gu", l, 0, 0)] = load_gu(l, 0, 0, 256)
                    preloaded[("gu", l, 0, 256)] = load_gu(l, 0, 256, 256)
                for n in range(NTC):
                    for m2 in range(NCH):
                        bk = bank()
                        for k in range(4):
                            S.op("pe", lambda e, k=k, n=n, bk=bk, m2=m2, WO=WO: e.matmul(
                                psb[bk][:, :], lhsT=WO[:, k, m2 * 128:(m2 + 1) * 128], rhs=merged[:, k, n * TC:(n + 1) * TC],
                                start=(k == 0), stop=(k == 3)),
                                reads=[("w", sO), ("mg", n)], writes=[("ps", bk)], inc=(k == 3))
                        S.op("dve", lambda e, bk=bk, m2=m2, n=n: e.tensor_tensor(
                            out=xT[:, m2, n * TC:(n + 1) * TC], in0=psb[bk][:, :], in1=xT[:, m2, n * TC:(n + 1) * TC], op=ALU.add),
                            reads=[("ps", bk), ("x", m2, n)], writes=[("x", m2, n)])
                    if mb == 1 and n >= 1 and stage >= 4:
                        norm_chunk(l, 8, n - 1)
                if mb == 1 and stage >= 4:
                    norm_chunk(l, 8, NTC - 1)
            if stage < 4:
                break

            S.alias(K_F, ALLR)
            wg3 = w_g_d[l].rearrange("(k p) n -> p k n", p=128)
            wu3 = w_u_d[l].rearrange("(k p) n -> p k n", p=128)
            for fb in range(2):
                f0 = fb * 1408
                fi = 0
                for cb, ncols in ((0, 256), (256, 256), (512, 256), (768, 256), (1024, 256), (1280, 128)):
                    if ("gu", l, fb, cb) in preloaded:
                        sG, WG, WU = preloaded.pop(("gu", l, fb, cb))
                    else:
                        sG, WG, WU = load_gu(l, fb, cb, ncols)
                    sU = sG
                    for mi in range(ncols // 128):
                        for half in range(2):
                            ns = (2 * half, 2 * half + 1)
                            bG = {n: bank() for n in ns}
                            bU = {n: bank() for n in ns}
                            for k in range(NCH):
                                for n in ns:
                                    S.op("pe", lambda e, k=k, n=n, bk=bG[n], mi=mi, WG=WG: e.matmul(
                                        psb[bk][:, :], lhsT=WG[:, k, mi * 128:(mi + 1) * 128], rhs=hT[:, k, n * TC:(n + 1) * TC],
                                        start=(k == 0), stop=(k == 7)),
                                        reads=[("w", sG), ("h", n)], writes=[("ps", bG[n])], inc=(k == 7))
                            for k in range(NCH):
                                for n in ns:
                                    S.op("pe", lambda e, k=k, n=n, bk=bU[n], mi=mi, WU=WU: e.matmul(
                                        psb[bk][:, :], lhsT=WU[:, k, mi * 128:(mi + 1) * 128], rhs=hT[:, k, n * TC:(n + 1) * TC],
                                        start=(k == 0), stop=(k == 7)),
                                        reads=[("w", sU), ("h", n)], writes=[("ps", bU[n])], inc=(k == 7))
                            for n in ns:
                                ri = scr_ctr[0] % 2
                                scr_ctr[0] += 1
                                S.op("act", lambda e, bk=bG[n], ri=ri: e.activation(out=scr[ri][:, :], in_=psb[bk][:, :], func=AF.Silu),
                                     reads=[("ps", bG[n])], writes=[("scr", ri)])
                                S.op("dve", lambda e, bk=bU[n], ri=ri, fi=fi, n=n: e.tensor_tensor(
                                    out=hid[:, fi, n * TC:(n + 1) * TC], in0=psb[bk][:, :], in1=scr[ri][:, :], op=ALU.mult),
                                    reads=[("ps", bU[n]), ("scr", ri)], writes=[("hid", n)])
                        fi += 1
                for mq in range(4):
                    sD = ring_ctr[0] % 3
                    ring_ctr[0] += 1
                    WD = wview(sD, 11, 256)
                    S.dma("pool", lambda e, WD=WD, f0=f0, mq=mq: e.dma_start(
                        out=WD[:, :, :], in_=w_d_d[l][f0:f0 + 1408, mq * 256:(mq + 1) * 256].rearrange("(k p) n -> p k n", p=128)),
                        ("wch", sD), writes=[("w", sD)])
                    for ml in range(2):
                        m2 = mq * 2 + ml
                        bks = [bank() for _ in range(NTC)]
                        for k in range(11):
                            for n in range(NTC):
                                S.op("pe", lambda e, k=k, n=n, bk=bks[n], ml=ml, WD=WD: e.matmul(
                                    psb[bk][:, :], lhsT=WD[:, k, ml * 128:(ml + 1) * 128], rhs=hid[:, k, n * TC:(n + 1) * TC],
                                    start=(k == 0), stop=(k == 10)),
                                    reads=[("w", sD), ("hid", n)], writes=[("ps", bks[n])], inc=(k == 10))
                        for n in range(NTC):
                            S.op("dve", lambda e, bk=bks[n], m2=m2, n=n: e.tensor_tensor(
                                out=xT[:, m2, n * TC:(n + 1) * TC], in0=psb[bk][:, :], in1=xT[:, m2, n * TC:(n + 1) * TC], op=ALU.add),
                                reads=[("ps", bks[n]), ("x", m2, n)], writes=[("x", m2, n)])
            if stage < 5:
                break

            sP = ring_ctr[0] % 3
            ring_ctr[0] += 1
            WP = wview(sP, 2, 1024)
            S.dma("pool", lambda e, WP=WP: e.dma_start(out=WP[:, :, :], in_=w_pp_d[l].rearrange("(k p) n -> p k n", p=128)),
                  ("wch", sP), writes=[("w", sP)])
            wpg3 = w_pg_d[l].rearrange("(k p) n -> p k n", p=128)
            WGs = []
            for mb in range(2):
                sG = ring_ctr[0] % 3
                ring_ctr[0] += 1
                WG = wview(sG, 8, 512)
                S.dma("pool", lambda e, WG=WG, mb=mb: e.dma_start(out=WG[:, :, :], in_=wpg3[:, :, mb * 512:(mb + 1) * 512]),
                      ("wch", sG), writes=[("w", sG)])
                WGs.append((sG, WG))
            rmsnorm_to_hT(l, 16)
            S.alias(K_P, ALLR)
            for half in range(2):
                S.dma("sp", lambda e, half=half: e.dma_start(
                    out=ptok[:, half * 8:(half + 1) * 8, :],
                    in_=p_d[l][half * 1024:(half + 1) * 1024, :].rearrange("(t p) f -> p t f", p=128)),
                    ("pl", half), writes=[("ptok", half)])
            for fc in range(2):
                for n in range(NTC):
                    bk = bank()
                    for i in range(4):
                        t = 4 * n + i
                        S.op("pe", lambda e, bk=bk, i=i, t=t, fc=fc: e.transpose(
                            out=psb[bk][:, i * 128:(i + 1) * 128], in_=ptok[:, t, fc * 128:(fc + 1) * 128], identity=ident_f),
                            reads=[("ptok", t // 8), "cst_f"], writes=[("ps", bk)], inc=(i == 3))
                    S.op("act", lambda e, bk=bk, fc=fc, n=n: e.copy(out=ppT[:, fc, n * TC:(n + 1) * TC], in_=psb[bk][:, :]),
                         reads=[("ps", bk)], writes=[("ppT", n)])
            last_layer = (l == n_layers - 1)
            if last_layer:
                S.alias(K_O, ALLR)

            def after_chunk(n):
                if not last_layer:
                    norm_chunk(l + 1, 0, n)
                else:
                    for t in range(4 * n, 4 * n + 4):
                        out_tile(t)
                    out_done.add(n)

            for n in range(NTC):
                for m2 in range(NCH):
                    mb, mi = divmod(m2, 4)
                    sG, WG = WGs[mb]
                    bG = bank()
                    for k in range(NCH):
                        S.op("pe", lambda e, k=k, n=n, bk=bG, mi=mi, WG=WG: e.matmul(
                            psb[bk][:, :], lhsT=WG[:, k, mi * 128:(mi + 1) * 128], rhs=hT[:, k, n * TC:(n + 1) * TC],
                            start=(k == 0), stop=(k == 7)),
                            reads=[("w", sG), ("h", n)], writes=[("ps", bG)], inc=(k == 7))
                    bP = bank()
                    for k in range(2):
                        S.op("pe", lambda e, k=k, n=n, bk=bP, m2=m2, WP=WP: e.matmul(
                            psb[bk][:, :], lhsT=WP[:, k, m2 * 128:(m2 + 1) * 128], rhs=ppT[:, k, n * TC:(n + 1) * TC],
                            start=(k == 0), stop=(k == 1)),
                            reads=[("w", sP), ("ppT", n)], writes=[("ps", bP)], inc=(k == 1))
                    ri = scr_ctr[0] % 2
                    scr_ctr[0] += 1
                    S.op("act", lambda e, bk=bG, ri=ri: e.activation(out=scr[ri][:, :], in_=psb[bk][:, :], func=AF.Sigmoid),
                         reads=[("ps", bG)], writes=[("scr", ri)])
                    S.op("dve", lambda e, bk=bP, ri=ri: e.tensor_tensor(out=scr[ri][:, :], in0=psb[bk][:, :], in1=scr[ri][:, :], op=ALU.mult),
                         reads=[("ps", bP), ("scr", ri)], writes=[("scr", ri)])
                    S.op("dve", lambda e, ri=ri, m2=m2, n=n: e.tensor_tensor(
                        out=xT[:, m2, n * TC:(n + 1) * TC], in0=scr[ri][:, :], in1=xT[:, m2, n * TC:(n + 1) * TC], op=ALU.add),
                        reads=[("scr", ri), ("x", m2, n)], writes=[("x", m2, n)])
                if n >= 1:
                    after_chunk(n - 1)
            after_chunk(NTC - 1)

        if len(out_done) < NTC:
            S.alias(K_O, ALLR)
            for t in range(16):
                if t // 4 not in out_done:
                    out_tile(t)
        S.wait_all("sp", [("yout", 0), ("yout", 1), "dbgout", "dbgout2"])

        with nc.Block() as block:
            @block.tensor
            def _(e):
                for f in S.prog["pe"]:
                    f(e)

            @block.scalar
            def _(e):
                for f in S.prog["act"]:
                    f(e)

            @block.vector
            def _(e):
                for f in S.prog["dve"]:
                    f(e)

            @block.gpsimd
            def _(e):
                for f in S.prog["pool"]:
                    f(e)

            @block.sync
            def _(e):
                for f in S.prog["sp"]:
                    f(e)
        build.stats = {k: len(v) for k, v in S.prog.items()}
        build.stats["waits"] = S.nwait
        build.stats["sems"] = len(S.sems)
    return nc


def wview_of(rg, k, n):
    return rg[:, 0:k * n].rearrange("p (k n) -> p k n", k=k)


def _pack_vecs(inp):
    v = np.zeros((128, 2 * NVL), np.float32)
    for l in range(2):
        b = l * NVL
        v[:, b + 0:b + 8] = np.asarray(inp["norm_mix_g"][l]).reshape(8, 128).T
        v[:, b + 8:b + 16] = np.asarray(inp["norm_ffn_g"][l]).reshape(8, 128).T
        v[:, b + 16:b + 24] = np.asarray(inp["norm_ple_g"][l]).reshape(8, 128).T
        v[:, b + 24] = np.tile(np.asarray(inp["qnorm_a_g"][l]), 2)
        v[:, b + 25] = np.tile(np.asarray(inp["knorm_a_g"][l]), 2)
        v[:, b + 26] = np.tile(np.asarray(inp["qnorm_b_g"][l]), 2)
        v[:, b + 27] = np.tile(np.asarray(inp["knorm_b_g"][l]), 2)
        v[:, b + 28:b + 36] = np.broadcast_to(np.asarray(inp["sink_b"][l])[None, :], (128, 8))
    return v


_NC_CACHE = {}


def make_in_maps(inp, n_cores=8):
    f = lambda a: np.ascontiguousarray(np.asarray(a, dtype=np.float32))
    shared = {k: f(inp[k]) for k in ("w_in", "w_branch_a", "w_branch_b", "w_out", "w_ffn_gate", "w_ffn_up",
                                     "w_ffn_down", "w_ple_gate", "w_ple_proj", "rel_table")}
    shared["vecs"] = _pack_vecs(inp)
    shared["cst"] = _cst()
    shared["emat"] = _emat()
    x = f(inp["x"])
    p = f(inp["p"])
    maps = []
    for b in range(n_cores):
        m = dict(shared)
        m["x"] = np.ascontiguousarray(x[b])
        m["p"] = np.ascontiguousarray(p[:, b])
        maps.append(m)
    return maps


def kernel(**inputs):
    if "nc" not in _NC_CACHE:
        _NC_CACHE["nc"] = build(2)
    nc = _NC_CACHE["nc"]
    in_maps = make_in_maps(inputs, 8)
    res = run_bass_kernel_spmd(nc, in_maps, core_ids=list(range(8)))
    out = np.stack([np.asarray(r["y"], dtype=np.float32) for r in res.results], axis=0)
    return out
```

```python
import numpy as np
from contextlib import ExitStack
import concourse.bass as bass
import concourse.mybir as mybir
from concourse.bass_utils import run_bass_kernel_spmd

F32 = mybir.dt.float32
BF16 = mybir.dt.bfloat16
AF = mybir.ActivationFunctionType
ALU = mybir.AluOpType

T = 2048
D = 1024
NCH = 8
TC = 512
NTC = 4
DFF = 2816
EPS = 1e-6
NEG = -30000.0
NVL = 36
STRIPS = [(1, 64, 128, 320), (4, 64, 128, 320), (16, 64, 0, 128), (1, 128, 128, 384)]
SBASE = [0, 320, 640, 768]
EW = 1152


class _Rec:
    def __init__(self):
        self.call = None

    def __getattr__(self, name):
        def f(*a, **k):
            self.call = (name, a, k)
            return self
        return f


def _capture(fn):
    r = _Rec()
    fn(r)
    assert r.call is not None
    return r.call


class Sched:
    ENG = ("pe", "act", "dve", "pool", "sp")

    def __init__(self, nc, stack):
        self.nc = nc
        self.stack = stack
        self.prog = {e: [] for e in self.ENG}
        self.sems = {}
        self.count = {}
        self.waited = {e: {} for e in self.ENG}
        self.res = {}
        self.nwait = 0

    def _sem(self, key):
        if key not in self.sems:
            name = "s_" + "_".join(str(k) for k in (key if isinstance(key, tuple) else (key,)))
            self.sems[key] = self.stack.enter_context(self.nc.semaphore(name))
            self.count[key] = 0
        return self.sems[key]

    def _deps(self, reads, writes):
        toks = []
        for r in reads:
            st = self.res.get(r)
            if st and st[0]:
                toks.append(st[0])
        for w in writes:
            st = self.res.get(w)
            if st:
                if st[0]:
                    toks.append(st[0])
                toks.extend(st[1])
        return toks

    def _waits(self, eng, toks, rawkeys):
        need = {}
        for (k, v) in toks:
            if v > need.get(k, 0):
                need[k] = v
        for k, v in need.items():
            if k == eng and (eng == "pe" or k not in rawkeys):
                continue
            if self.waited[eng].get(k, 0) >= v:
                continue
            self.waited[eng][k] = v
            s = self._sem(k)
            self.prog[eng].append(lambda e, s=s, v=v: e.wait_ge(s, v))
            self.nwait += 1

    def _record(self, tok, reads, writes):
        for r in reads:
            st = self.res.setdefault(r, [None, []])
            st[1].append(tok)
        for w in writes:
            self.res[w] = [tok, []]

    def op(self, eng, fn, reads=(), writes=(), inc=True):
        toks = self._deps(reads, writes)
        raw = set()
        for r in reads:
            st = self.res.get(r)
            if st and st[0]:
                raw.add(st[0][0])
        self._waits(eng, toks, raw)
        s = self._sem(eng)
        call = _capture(fn)
        if inc:
            self.count[eng] += 1
            tok = (eng, self.count[eng])
            self.prog[eng].append(lambda e, c=call, s=s: getattr(e, c[0])(*c[1], **c[2]).then_inc(s, 1))
        else:
            tok = (eng, self.count[eng] + 1)
            self.prog[eng].append(lambda e, c=call: getattr(e, c[0])(*c[1], **c[2]))
        self._record(tok, reads, writes)

    def dma(self, eng, fn, chan, reads=(), writes=()):
        toks = [t for t in self._deps(reads, writes) if t[0] != chan]
        self._waits(eng, toks, set(k for k, _ in toks))
        s = self._sem(chan)
        self.count[chan] += 16
        tok = (chan, self.count[chan])
        call = _capture(fn)
        self.prog[eng].append(lambda e, c=call, s=s: getattr(e, c[0])(*c[1], **c[2]).then_inc(s, 16))
        self._record(tok, reads, writes)

    def alias(self, new_keys, old_keys):
        toks = []
        nk = set(new_keys)
        for k in old_keys:
            if k in nk:
                continue
            st = self.res.get(k)
            if st:
                if st[0]:
                    toks.append(st[0])
                toks.extend(st[1])
        red = {}
        for (k, v) in toks:
            if v > red.get(k, 0):
                red[k] = v
        toks = list(red.items())
        for k in new_keys:
            st = self.res.setdefault(k, [None, []])
            st[1].extend(toks)

    def wait_all(self, eng, keys):
        toks = []
        for k in keys:
            st = self.res.get(k)
            if st:
                if st[0]:
                    toks.append(st[0])
                toks.extend(st[1])
        self._waits(eng, toks, set(k for k, _ in toks))


def _t5_bucket_np(rel):
    rel = np.asarray(rel, dtype=np.int64)
    half_b, max_exact = 16, 8
    sign = np.where(rel > 0, half_b, 0)
    n = np.abs(rel)
    nf = np.maximum(n, 1).astype(np.float32)
    large = max_exact + (np.log(nf / np.float32(max_exact)) / np.float32(np.log(1024 / max_exact))
                         * np.float32(half_b - max_exact)).astype(np.int32)
    large = np.minimum(large, half_b - 1)
    return sign + np.where(n < max_exact, n, large)


def _emat():
    E = np.zeros((33, len(STRIPS), 512), np.float32)
    for si, (d, half, c0, width) in enumerate(STRIPS):
        m = np.arange(512)
        rel = m - (width - 1 - c0)
        code = np.where(np.abs(rel) <= half, _t5_bucket_np(rel * d), 32)
        E[code, si, m] = 1.0
    return E.reshape(33, len(STRIPS) * 512)


def _cst():
    c = np.zeros((128, 512), np.float32)
    c[:, 384:512] = np.roll(np.eye(128, dtype=np.float32), 64, axis=1)
    c[:, 0:128] = np.eye(128, dtype=np.float32)
    c[:, 128:256] = 1.0
    c[0:64, 256:320] = 1.0
    c[64:128, 320:384] = 1.0
    return c


def build(n_layers=2, stage=99, dbg=0):
    nc = bass.Bass("TRN2", target_bir_lowering=False)
    dram = {}

    def din(name, shape):
        dram[name] = nc.dram_tensor(name, list(shape), F32, kind="ExternalInput").ap()
        return dram[name]

    x_d = din("x", [T, D])
    p_d = din("p", [2, T, 256])
    w_in_d = din("w_in", [2, D, 4352])
    w_a_d = din("w_branch_a", [2, 512, D])
    w_b_d = din("w_branch_b", [2, 512, D])
    w_out_d = din("w_out", [2, D, D])
    w_g_d = din("w_ffn_gate", [2, D, DFF])
    w_u_d = din("w_ffn_up", [2, D, DFF])
    w_d_d = din("w_ffn_down", [2, DFF, D])
    w_pg_d = din("w_ple_gate", [2, D, D])
    w_pp_d = din("w_ple_proj", [2, 256, D])
    vecs_d = din("vecs", [128, 2 * NVL])
    tab_d = din("rel_table", [32, 16])
    cst_d = din("cst", [128, 512])
    emat_d = din("emat", [33, 4 * 512])
    y_d = nc.dram_tensor("y", [T, D], F32, kind="ExternalOutput").ap()
    dbg_d = None
    if dbg:
        dbg_d = nc.dram_tensor("dbg", [128, 4 * T], BF16, kind="ExternalOutput").ap()

    with ExitStack() as stack:
        def sb(name, shape, dt):
            return stack.enter_context(nc.sbuf_tensor(name, list(shape), dt))

        xT = sb("xT", [128, NCH, T], F32)
        hT = sb("hT", [128, NCH, T], BF16)
        ring = [sb(f"ring{i}", [128, 4096], BF16) for i in range(3)]
        Ebuf = sb("Ebuf", [128, 8, EW], BF16)
        R = sb("R", [128, 26624], BF16)
        scr = [sb(f"scr{i}", [128, 512], F32) for i in range(2)]
        sq_all = sb("sq_all", [128, 4 * 512], BF16)
        sqb = [sq_all[:, i * 512:(i + 1) * 512] for i in range(4)]
        pTs = [sb(f"pT{i}", [128, 512], BF16) for i in range(6)]
        cst_f = sb("cst_f", [128, 256], F32)
        cst_b = sb("cst_b", [128, 384], BF16)
        vecs = sb("vecs_s", [128, 2 * NVL], F32)
        vd = sb("vd", [128, 2 * NVL], F32)
        tab = sb("tab", [33, 16], BF16)
        tabf = sb("tabf", [33, 16], F32)
        psb = [stack.enter_context(nc.psum_tensor(f"ps{i}", [128, 512], F32)) for i in range(8)]

        S = Sched(nc, stack)
        ident_f = cst_f[:, 0:128]
        swap_f = cst_f[:, 128:256]
        ident_b = cst_b[:, 0:128]
        ones_b = cst_b[:, 128:256]
        bones_b = cst_b[:, 256:384]

        def Rv(off_bytes, nbytes, dt):
            a = R[:, off_bytes // 2:(off_bytes + nbytes) // 2]
            return a if dt == BF16 else a.bitcast(dt)

        KB = 1024
        yaT = Rv(0, 16 * KB, BF16).rearrange("p (c t) -> p c t", c=4)
        qa_v = Rv(16 * KB, 4 * KB, BF16)
        ka_v = Rv(20 * KB, 4 * KB, BF16)
        vt = Rv(24 * KB, 53 * 384, BF16).rearrange("p (t f) -> p t f", f=192)
        acc = Rv(44 * KB, 8 * KB, F32)
        qbT = Rv(16 * KB, 16 * KB, BF16).rearrange("p (c t) -> p c t", c=4)
        kbT = Rv(32 * KB, 4 * KB, BF16)
        vbt = Rv(36 * KB, 16 * 384, BF16).rearrange("p (t f) -> p t f", f=192)
        merged = Rv(32 * KB, 16 * KB, BF16).rearrange("p (c t) -> p c t", c=4)
        hid = Rv(0, 44 * KB, BF16).rearrange("p (c t) -> p c t", c=11)
        ptok = Rv(0, 16 * KB, F32).rearrange("p (t f) -> p t f", f=256)
        ppT = Rv(16 * KB, 8 * KB, BF16).rearrange("p (c t) -> p c t", c=2)
        xtmp = Rv(0, 16 * KB, F32).rearrange("p (s f) -> p s f", s=4)
        Emat = Rv(16 * KB, 4 * KB, BF16)
        otile = Rv(0, 8 * KB, F32).rearrange("p (s f) -> p s f", s=2)
        K_SETUP = [("xtmp", i) for i in range(8)] + ["Emat"]
        K_YA = [("ya", i) for i in range(4)]
        K_A = ["qa", "ka", "vt", "acc"]
        K_B = [("qb", i, j) for i in range(4) for j in range(4)] + ["kb", "vbt"]
        K_G = [("mg", n) for n in range(4)]
        K_F = [("hid", n) for n in range(4)]
        K_P = [("ptok", 0), ("ptok", 1)] + [("ppT", n) for n in range(4)]
        K_O = [("ot", 0), ("ot", 1)]
        ALLR = K_SETUP + K_YA + K_A + K_B + K_G + K_F + K_P + K_O

        bank_ctr = [0]

        def bank():
            b = bank_ctr[0] % 8
            bank_ctr[0] += 1
            return b

        ring_ctr = [0]

        def wload(parts):
            s = ring_ctr[0] % 3
            ring_ctr[0] += 1
            for dst_fn, src in parts:
                dst = dst_fn(ring[s])
                S.dma("pool", lambda e, dst=dst, src=src: e.dma_start(out=dst, in_=src),
                      ("wch", s), writes=[("w", s)])
            return s

        def wview(s, k, n):
            return ring[s][:, 0:k * n].rearrange("p (k n) -> p k n", k=k)

        evac_ctr = [0]

        def load_pair_w(l, c):
            s_ = ring_ctr[0] % 3
            ring_ctr[0] += 1
            W_ = wview(s_, 8, 384)
            src_ = w_in_d[l].rearrange("(k p) n -> p k n", p=128)
            for part, base in enumerate((0, 512, 1024)):
                S.dma("pool", lambda e, W_=W_, part=part, base=base, src_=src_: e.dma_start(
                    out=W_[:, :, part * 128:(part + 1) * 128], in_=src_[:, :, base + c * 128: base + (c + 1) * 128]),
                    ("wch", s_), writes=[("w", s_)])
            return s_

        def load_gu(l, fb, cb, ncols):
            s_ = ring_ctr[0] % 3
            ring_ctr[0] += 1
            Wgu = ring[s_][:, 0:16 * ncols].rearrange("p (w k n) -> p w k n", w=2, k=8)
            WG_, WU_ = Wgu[:, 0], Wgu[:, 1]
            f0_ = fb * 1408 + cb
            wg3_ = w_g_d[l].rearrange("(k p) n -> p k n", p=128)
            wu3_ = w_u_d[l].rearrange("(k p) n -> p k n", p=128)
            S.dma("pool", lambda e: e.dma_start(out=WG_, in_=wg3_[:, :, f0_:f0_ + ncols]), ("wch", s_), writes=[("w", s_)])
            S.dma("pool", lambda e: e.dma_start(out=WU_, in_=wu3_[:, :, f0_:f0_ + ncols]), ("wch", s_), writes=[("w", s_)])
            return s_, WG_, WU_

        preloaded = {}
        S.dma("sp", lambda e: e.dma_start(out=cst_f[:, 0:128], in_=cst_d[:, 0:128]), "c_cstf", writes=["cst_f"])
        S.dma("sp", lambda e: e.dma_start(out=cst_f[:, 128:256], in_=cst_d[:, 384:512]), "c_cstf2", writes=["cst_f2"])
        S.dma("pool", lambda e: e.dma_start(out=cst_b[:], in_=cst_d[:, 0:384]), "c_cstb", writes=["cst_b"])
        S.dma("sp", lambda e: e.dma_start(out=vecs[:], in_=vecs_d[:]), "c_vecs", writes=["vecs"])
        S.dma("sp", lambda e: e.dma_start(out=tabf[0:32, :], in_=tab_d[:]), "c_tab", writes=["tabf"])
        S.op("dve", lambda e: e.memset(tabf[32:33, :], NEG), reads=[], writes=["tabf32"])
        S.op("dve", lambda e: e.tensor_copy(out=tab[0:32, :], in_=tabf[0:32, :]), reads=["tabf"], writes=["tab0"])
        S.op("dve", lambda e: e.tensor_copy(out=tab[32:33, :], in_=tabf[32:33, :]), reads=["tabf32"], writes=["tab1"])
        S.dma("pool", lambda e: e.dma_start(out=Emat[0:33, :], in_=emat_d[:]), "c_emat", writes=["Emat"])
        if n_layers >= 1 and stage >= 0.1:
            preloaded[(0, 0)] = load_pair_w(0, 0)
        for l in range(n_layers):
            b0 = l * NVL
            S.op("dve", lambda e, b0=b0: e.tensor_scalar_mul(out=vd[:, b0:b0 + 24], in0=vecs[:, b0:b0 + 24], scalar1=32.0),
                 reads=["vecs"], writes=[("vd", l, 0)])
            S.op("dve", lambda e, b0=b0: e.tensor_copy(out=vd[:, b0 + 24:b0 + 25], in_=vecs[:, b0 + 24:b0 + 25]),
                 reads=["vecs"], writes=[("vd", l, 1)])
            S.op("dve", lambda e, b0=b0: e.tensor_scalar_mul(out=vd[:, b0 + 25:b0 + 26], in0=vecs[:, b0 + 25:b0 + 26], scalar1=8.0),
                 reads=["vecs"], writes=[("vd", l, 2)])
            S.op("dve", lambda e, b0=b0: e.tensor_copy(out=vd[:, b0 + 26:b0 + 27], in_=vecs[:, b0 + 26:b0 + 27]),
                 reads=["vecs"], writes=[("vd", l, 3)])
            S.op("dve", lambda e, b0=b0: e.tensor_scalar_mul(out=vd[:, b0 + 27:b0 + 28], in0=vecs[:, b0 + 27:b0 + 28], scalar1=8.0),
                 reads=["vecs"], writes=[("vd", l, 4)])
            S.op("act", lambda e, b0=b0: e.activation(out=vd[:, b0 + 28:b0 + 36], in_=vecs[:, b0 + 28:b0 + 36], func=AF.Exp),
                 reads=["vecs"], writes=[("vd", l, 5)])
        VD = [("vd", l, i) for l in range(n_layers) for i in range(6)]

        def gen_strip(si_):
            d_, half_, c0_, width_ = STRIPS[si_]
            hs = 8 if si_ == 3 else 0
            for j0 in range(0, width_, 64):
                bk = bank()
                for jj in range(64):
                    jp = j0 + jj
                    e0 = si_ * 512 + (width_ - 1 - jp)
                    S.op("pe", lambda e, bk=bk, jj=jj, hs=hs, e0=e0: e.matmul(
                        psb[bk][:, jj * 8:(jj + 1) * 8], lhsT=Emat[0:33, e0:e0 + 128],
                        rhs=tab[0:33, hs:hs + 8], start=True, stop=True),
                        reads=["Emat", "tab0", "tab1"], writes=[("ps", bk)], inc=(jj == 63))
                S.op("act", lambda e, bk=bk, si_=si_, j0=j0: e.activation(
                    out=Ebuf[:, :, SBASE[si_] + j0: SBASE[si_] + j0 + 64],
                    in_=psb[bk][:, :].rearrange("p (j h) -> p h j", h=8), func=AF.Exp),
                    reads=[("ps", bk)], writes=["Ebuf"])

        sq_ctr = [0]
        scr_ctr = [0]

        def norm_chunk(l, gcol, n, mid=None):
            bk = bank()
            for half in range(2):
                sis = {}
                for c in range(4 * half, 4 * half + 4):
                    si = sq_ctr[0] % 4
                    sq_ctr[0] += 1
                    sis[c] = si
                    xin = xT[:, c, n * TC:(n + 1) * TC]
                    if c % 2 == 0:
                        S.op("act", lambda e, si=si, xin=xin: e.activation(out=sqb[si], in_=xin, func=AF.Square),
                             reads=[("x", c, n)], writes=[("sq", si)])
                    else:
                        S.op("pool", lambda e, si=si, xin=xin: e.tensor_tensor(out=sqb[si], in0=xin, in1=xin, op=ALU.mult),
                             reads=[("x", c, n)], writes=[("sq", si)])
                if half == 0 and mid is not None:
                    mid()
                for c in range(4 * half, 4 * half + 4):
                    si = sis[c]
                    S.op("pe", lambda e, bk=bk, si=si, c=c: e.matmul(psb[bk][:, :], lhsT=ones_b, rhs=sqb[si],
                                                                    start=(c == 0), stop=(c == NCH - 1)),
                         reads=[("sq", si), "cst_b"], writes=[("ps", bk)], inc=True)
            ri = scr_ctr[0] % 2
            scr_ctr[0] += 1
            S.op("act", lambda e, bk=bk, ri=ri: e.activation(out=scr[ri][:, :], in_=psb[bk][:, :], func=AF.Ln, bias=1024.0 * EPS),
                 reads=[("ps", bk)], writes=[("scr", ri)])
            S.op("act", lambda e, ri=ri: e.activation(out=scr[ri][:, :], in_=scr[ri][:, :], func=AF.Exp, scale=-0.5),
                 reads=[("scr", ri)], writes=[("scr", ri)])
            for c in range(NCH):
                eng = "dve"
                S.op(eng, lambda e, c=c, n=n, ri=ri: e.scalar_tensor_tensor(
                    out=hT[:, c, n * TC:(n + 1) * TC], in0=xT[:, c, n * TC:(n + 1) * TC],
                    scalar=vd[:, l * NVL + gcol + c: l * NVL + gcol + c + 1], in1=scr[ri][:, :],
                    op0=ALU.mult, op1=ALU.mult),
                    reads=[("x", c, n), ("scr", ri)] + VD, writes=[("h", n)])

        def rmsnorm_to_hT(l, gcol, chunks=range(NTC)):
            for n in chunks:
                norm_chunk(l, gcol, n)

        qk_pend = []

        def qk_finish(l, bk, si, gcolidx, dst, dkeys):
            b2 = bank()
            S.op("pe", lambda e, b2=b2, si=si: e.matmul(psb[b2][:, :], lhsT=bones_b, rhs=sqb[si], start=True, stop=True),
                 reads=[("sq", si), "cst_b"], writes=[("ps", b2)])
            ri = scr_ctr[0] % 2
            scr_ctr[0] += 1
            S.op("act", lambda e, b2=b2, ri=ri: e.activation(out=scr[ri][:, :], in_=psb[b2][:, :], func=AF.Ln, bias=64.0 * EPS),
                 reads=[("ps", b2)], writes=[("scr", ri)])
            S.op("act", lambda e, ri=ri: e.activation(out=scr[ri][:, :], in_=scr[ri][:, :], func=AF.Exp, scale=-0.5),
                 reads=[("scr", ri)], writes=[("scr", ri)])
            S.op("dve", lambda e, bk=bk, ri=ri, dst=dst: e.scalar_tensor_tensor(
                out=dst, in0=psb[bk][:, :], scalar=vd[:, l * NVL + gcolidx: l * NVL + gcolidx + 1],
                in1=scr[ri][:, :], op0=ALU.mult, op1=ALU.mult),
                reads=[("ps", bk), ("scr", ri)] + VD, writes=dkeys)

        def qk_flush():
            while qk_pend:
                qk_finish(*qk_pend.pop(0))

        def qk_proj(l, slot, wk, wcol, gcolidx, dst_fn, dst_keys_fn, flush=True):
            W = wview(slot, 8, wk)
            for n in range(NTC):
                bk = bank()
                for k in range(NCH):
                    S.op("pe", lambda e, k=k, n=n, bk=bk: e.matmul(
                        psb[bk][:, :], lhsT=W[:, k, wcol:wcol + 128], rhs=hT[:, k, n * TC:(n + 1) * TC],
                        start=(k == 0), stop=(k == NCH - 1)),
                        reads=[("w", slot), ("h", n)], writes=[("ps", bk)], inc=(k == NCH - 1))
                si = sq_ctr[0] % 4
                sq_ctr[0] += 1
                S.op("act", lambda e, si=si, bk=bk: e.activation(out=sqb[si], in_=psb[bk][:, :], func=AF.Square),
                     reads=[("ps", bk)], writes=[("sq", si)])
                qk_pend.append((l, bk, si, gcolidx, dst_fn(n), dst_keys_fn(n)))
                if len(qk_pend) > 1:
                    qk_finish(*qk_pend.pop(0))
            if flush:
                qk_flush()

        def attention(pairs_by_otile, evac_fn, look=5, hook=None):
            groups = []
            for oi, ot in enumerate(pairs_by_otile):
                flat = [(bi, pi, p) for bi, blk in enumerate(ot["blocks"]) for pi, p in enumerate(blk)]
                gsz = ot.get("gsz", 4)
                chunks = [flat[i:i + gsz] for i in range(0, len(flat), gsz)]
                for ci, ch in enumerate(chunks):
                    groups.append((oi, ch, ci == len(chunks) - 1))
            state = {}
            obank = {}
            mul_ctr = [0]

            def stage1(gi):
                oi, ch, _ = groups[gi]
                bk = bank()
                pslot = gi % 6
                for j, (bi, pi, p) in enumerate(ch):
                    M = p["M"]
                    cols = slice(j * 128, (j + 1) * 128)
                    S.op("pe", lambda e, bk=bk, p=p, M=M, cols=cols: e.matmul(
                        psb[bk][0:M, cols], lhsT=p["k"], rhs=p["q"], start=True, stop=True),
                        reads=p["reads"], writes=[("ps", bk)], inc=(j == len(ch) - 1))
                j0 = 0
                while j0 < len(ch):
                    j1 = j0
                    while j1 < len(ch) and ch[j1][2]["M"] == ch[j0][2]["M"]:
                        j1 += 1
                    M = ch[j0][2]["M"]
                    S.op("act", lambda e, bk=bk, pslot=pslot, j0=j0, j1=j1, M=M: e.activation(
                        out=pTs[pslot][0:M, j0 * 128:j1 * 128], in_=psb[bk][0:M, j0 * 128:j1 * 128], func=AF.Exp),
                        reads=[("ps", bk)], writes=[("pT", pslot)])
                    j0 = j1
                ecs = [p["ecol"] for (_, _, p) in ch]
                full = all(p["M"] == 128 for (_, _, p) in ch)
                rep = None
                if full and len(ch) == 4 and ecs[0] == ecs[2] and ecs[1] == ecs[3] and ecs[1] == ecs[0] + 128:
                    rep = (2, 256)
                elif full and len(ch) == 4 and ecs[0] == ecs[1] == ecs[2] == ecs[3]:
                    rep = (4, 128)
                if rep is not None:
                    a_, b_ = rep
                    eh = ch[0][2]["eh"]
                    ec = ecs[0]
                    pv = pTs[pslot][:, 0:512].rearrange("p (a b) -> p a b", a=a_)
                    ev = Ebuf[:, eh, ec:ec + b_].unsqueeze(1).broadcast_to([128, a_, b_])
                    S.op("dve", lambda e, pv=pv, ev=ev: e.tensor_tensor(out=pv, in0=pv, in1=ev, op=ALU.mult),
                         reads=[("pT", pslot), "Ebuf"], writes=[("pT", pslot)])
                j0 = 0 if rep is None else len(ch)
                while j0 < len(ch):
                    j1 = j0 + 1
                    while (j1 < len(ch) and ch[j1][2]["M"] == ch[j0][2]["M"]
                           and ch[j1][2]["ecol"] == ch[j1 - 1][2]["ecol"] + 128):
                        j1 += 1
                    M = ch[j0][2]["M"]
                    ec = ch[j0][2]["ecol"]
                    eh = ch[j0][2]["eh"]
                    eng = "dve"
                    mul_ctr[0] += 1
                    S.op(eng, lambda e, pslot=pslot, j0=j0, j1=j1, M=M, ec=ec, eh=eh: e.tensor_tensor(
                        out=pTs[pslot][0:M, j0 * 128:j1 * 128], in0=pTs[pslot][0:M, j0 * 128:j1 * 128],
                        in1=Ebuf[0:M, eh, ec:ec + (j1 - j0) * 128], op=ALU.mult),
                        reads=[("pT", pslot), "Ebuf"], writes=[("pT", pslot)])
                    j0 = j1
                state[gi] = pslot

            def stage2(gi):
                oi, ch, last = groups[gi]
                ot = pairs_by_otile[oi]
                first_in_bank = oi not in obank
                if first_in_bank:
                    obank[oi] = bank()
                ob = obank[oi]
                pslot = state[gi]
                order = sorted(range(len(ch)), key=lambda j: (ch[j][1], ch[j][0]))
                for oi_, j in enumerate(order):
                    bi, pi, p = ch[j]
                    M = p["M"]
                    nb = len(ot["blocks"][bi])
                    S.op("pe", lambda e, ob=ob, p=p, M=M, bi=bi, pi=pi, nb=nb, pslot=pslot, j=j, st=(first_in_bank and oi_ == 0): e.matmul(
                        psb[ob][:, bi * 128:(bi + 1) * 128], lhsT=p["v"], rhs=pTs[pslot][0:M, j * 128:(j + 1) * 128],
                        start=st, stop=(pi == nb - 1), skip_group_check=True),
                        reads=[("pT", pslot)] + p["vreads"], writes=[("ps", ob)], inc=(oi_ == len(ch) - 1))
                if last:
                    (ot.get("evac") or evac_fn)(oi, ob, ot)
                    if ot.get("post") is not None:
                        ot["post"]()

            G = len(groups)
            for i in range(G + look):
                if i < G:
                    stage1(i)
                if hook is not None and i == min(2, G - 1):
                    hook()
                    hook = None
                if i - look >= 0:
                    stage2(i - look)

        out_done = set()

        def out_tile(t):
            so = t % 2
            b2 = [bank(), bank()]
            for c in range(NCH):
                bk = b2[c // 4]
                S.op("pe", lambda e, bk=bk, c=c, t=t: e.transpose(
                    out=psb[bk][:, (c % 4) * 128:(c % 4 + 1) * 128], in_=xT[:, c, t * 128:(t + 1) * 128], identity=ident_f),
                    reads=[("x", c, t // 4), "cst_f"], writes=[("ps", bk)], inc=(c % 4 == 3))
            S.op("act", lambda e, bk=b2[0], so=so: e.copy(out=otile[:, so, 0:512], in_=psb[bk][:, :]),
                 reads=[("ps", b2[0])], writes=[("ot", so)])
            S.op("dve", lambda e, bk=b2[1], so=so: e.tensor_copy(out=otile[:, so, 512:1024], in_=psb[bk][:, :]),
                 reads=[("ps", b2[1])], writes=[("ot", so)])
            S.dma("sp", lambda e, t=t, so=so: e.dma_start(out=y_d[t * 128:(t + 1) * 128, :], in_=otile[:, so, :]),
                  ("och", so), reads=[("ot", so)], writes=[("yout", so)])

        xtmpB = Rv(24 * KB, 16 * KB, F32).rearrange("p (s f) -> p s f", s=4)
        norm0_done = (n_layers >= 1 and stage >= 0.1)
        for n in range(NTC):
            xt_ = xtmp if n % 2 == 0 else xtmpB
            for i in range(4):
                t = 4 * n + i
                S.dma("sp", lambda e, i=i, t=t, xt_=xt_: e.dma_start(out=xt_[:, i, :], in_=x_d[t * 128:(t + 1) * 128, :]),
                      ("xl", (n % 2) * 4 + i), writes=[("xtmp", (n % 2) * 4 + i)])
            for c in range(NCH):
                bk = bank()
                for i in range(4):
                    S.op("pe", lambda e, bk=bk, i=i, c=c, xt_=xt_: e.transpose(
                        out=psb[bk][:, i * 128:(i + 1) * 128], in_=xt_[:, i, c * 128:(c + 1) * 128], identity=ident_f),
                        reads=[("xtmp", (n % 2) * 4 + i), "cst_f"], writes=[("ps", bk)], inc=(i == 3))
                if c % 2 == 0:
                    S.op("act", lambda e, bk=bk, c=c, n=n: e.copy(out=xT[:, c, n * TC:(n + 1) * TC], in_=psb[bk][:, :]),
                         reads=[("ps", bk)], writes=[("x", c, n)])
                else:
                    S.op("dve", lambda e, bk=bk, c=c, n=n: e.tensor_copy(out=xT[:, c, n * TC:(n + 1) * TC], in_=psb[bk][:, :]),
                         reads=[("ps", bk)], writes=[("x", c, n)])
            if norm0_done:
                norm_chunk(0, 0, n, mid=(lambda n=n: gen_strip(n)))
        if not norm0_done:
            for si_ in range(len(STRIPS)):
                gen_strip(si_)

        for l in range(n_layers):
            vb0 = l * NVL
            if stage < 0.1:
                break
            if l == 0 and not norm0_done:
                for n in range(NTC):
                    norm_chunk(l, 0, n, mid=(lambda n=n: gen_strip(n)))
            if stage < 1 and stage <= 0.2:
                if dbg == 3:
                    S.dma("sp", lambda e: e.dma_start(out=dbg_d[:, :], in_=hT[:, 0:4, :].rearrange("p c t -> p (c t)")), "dbgc", reads=[("h", n) for n in range(4)], writes=["dbgout"])
                break

            def a_tiles():
                tiles = []
                idx = {}
                for j in range(17):
                    s0, s1 = max(128 * j - 64, 0), min(128 * j + 64, T)
                    idx[(0, 0, j)] = len(tiles)
                    tiles.append((s0, 1, s1 - s0, 64 if j == 0 else 0))
                for r in range(4):
                    for j in range(5):
                        s0, s1 = max(128 * j - 64, 0), min(128 * j + 64, 512)
                        idx[(1, r, j)] = len(tiles)
                        tiles.append((r + 4 * s0, 4, s1 - s0, 64 if j == 0 else 0))
                for r in range(16):
                    idx[(2, r, 0)] = len(tiles)
                    tiles.append((r, 16, 128, 0))
                return tiles, idx

            tiles, tidx = a_tiles()
            pending_fin = [None]

            def tok_ap(buf2d, rows, tok0, step, cnt):
                if step == 1:
                    return buf2d[rows, tok0:tok0 + cnt]
                v = buf2d.rearrange("p (s r) -> p r s", r=step)
                return v[rows, tok0 % step, tok0 // step: tok0 // step + cnt]

            for c in range(4):
                src3 = w_in_d[l].rearrange("(k p) n -> p k n", p=128)
                if (l, c) in preloaded:
                    s = preloaded.pop((l, c))
                else:
                    s = load_pair_w(l, c)
                W = wview(s, 8, 384)
                akeys = ["qa", "ka", "vt", "acc"]
                if c == 0:
                    S.alias(K_A + K_YA, ALLR)
                qk_proj(l, s, 384, 0, 24, lambda n: qa_v[:, n * TC:(n + 1) * TC], lambda n: ["qa"], flush=False)
                qk_proj(l, s, 384, 128, 25, lambda n: ka_v[:, n * TC:(n + 1) * TC], lambda n: ["ka"], flush=True)
                if pending_fin[0] is not None:
                    pending_fin[0]()
                    pending_fin[0] = None
                if stage <= 0.4:
                    if dbg == 2:
                        S.dma("sp", lambda e: e.dma_start(out=dbg_d[:, 0:T], in_=qa_v), "dbgc", reads=["qa"], writes=["dbgout"])
                        S.dma("sp", lambda e: e.dma_start(out=dbg_d[:, T:2 * T], in_=ka_v), "dbgc2", reads=["ka"], writes=["dbgout2"])
                    break
                S.op("dve", lambda e: e.memset(vt[:, :, 64:128], 1.0), reads=[], writes=["vt"])
                vT = sq_all
                for n in range(NTC):
                    bk = bank()
                    for k in range(NCH):
                        S.op("pe", lambda e, k=k, n=n, bk=bk, W=W: e.matmul(
                            psb[bk][:, :], lhsT=W[:, k, 256:384], rhs=hT[:, k, n * TC:(n + 1) * TC],
                            start=(k == 0), stop=(k == NCH - 1)),
                            reads=[("w", s), ("h", n)], writes=[("ps", bk)], inc=(k == NCH - 1))
                    S.op("act", lambda e, bk=bk, n=n: e.copy(out=vT[:, n * TC:(n + 1) * TC], in_=psb[bk][:, :]),
                         reads=[("ps", bk)], writes=[("sq", n)])
                for t0 in range(0, len(tiles), 4):
                    grp = tiles[t0:t0 + 4]
                    bk = bank()
                    for gi, (tok0, step, M, i0) in enumerate(grp):
                        lhs = tok_ap(vT, slice(0, 128), tok0, step, M)
                        S.op("pe", lambda e, bk=bk, gi=gi, M=M, lhs=lhs: e.matmul(
                            psb[bk][0:M, gi * 128:(gi + 1) * 128], lhsT=lhs, rhs=ident_b,
                            start=True, stop=True),
                            reads=["cst_b"] + [("sq", n) for n in range(NTC)], writes=[("ps", bk)],
                            inc=(gi == len(grp) - 1))
                    ng = len(grp)
                    vo = vt[:, t0:t0 + ng, :].rearrange("p t (a b) -> p t a b", a=3)[:, :, 0:3:2, :]
                    vi = psb[bk][:, 0:ng * 128].rearrange("p (t a b) -> p t a b", a=2, b=64)
                    if (t0 // 4) % 2 == 0:
                        S.op("act", lambda e, vo=vo, vi=vi: e.copy(out=vo, in_=vi), reads=[("ps", bk)], writes=["vt"])
                    else:
                        S.op("dve", lambda e, vo=vo, vi=vi: e.tensor_copy(out=vo, in_=vi), reads=[("ps", bk)], writes=["vt"])
                if stage <= 0.6:
                    break
                pair_otiles = []
                for u in range(2):
                    h = 2 * c + u
                    rows = slice(u * 64, (u + 1) * 64)
                    vcols = slice(0, 128) if u == 0 else slice(64, 192)
                    orow = slice(0, 64) if u == 0 else slice(64, 128)
                    drow = slice(64, 128) if u == 0 else slice(0, 64)

                    def mkpair(pat, r, j, qtok0, step, kind):
                        tok0, st, M, i0 = tiles[tidx[(pat, r, j)]]
                        ti = tidx[(pat, r, j)]
                        if pat == 2:
                            shift = 0
                        elif kind == "t1":
                            shift = 64
                        else:
                            shift = 128 if i0 == 64 else 192
                        return dict(k=tok_ap(ka_v, rows, tok0, st, M), q=tok_ap(qa_v, rows, qtok0, step, 128), M=M,
                                    ecol=SBASE[pat] + shift, eh=h, v=vt[0:M, ti, vcols],
                                    reads=["qa", "ka"], vreads=["vt"])

                    otiles = []
                    for Q in range(4):
                        blocks = []
                        for b in range(4 * Q, 4 * Q + 4):
                            blocks.append([mkpair(0, 0, b + 1, 128 * b, 1, "t1"), mkpair(0, 0, b, 128 * b, 1, "t0")])
                        otiles.append(dict(blocks=blocks, kind=("p1", Q)))
                    for r in range(4):
                        blocks = []
                        for b in range(4):
                            blocks.append([mkpair(1, r, b + 1, r + 4 * 128 * b, 4, "t1"), mkpair(1, r, b, r + 4 * 128 * b, 4, "t0")])
                        otiles.append(dict(blocks=blocks, kind=("p2", r)))
                    for r0 in range(0, 16, 4):
                        blocks = [[mkpair(2, r, 0, r, 16, "t")] for r in range(r0, r0 + 4)]
                        otiles.append(dict(blocks=blocks, kind=("p3", r0)))

                    def evacA(oi, ob, ot):
                        kind, v = ot["kind"]
                        if kind == "p1":
                            S.op("dve", lambda e, ob=ob, v=v: e.tensor_copy(out=acc[:, v * 512:(v + 1) * 512], in_=psb[ob][:, :]),
                                 reads=[("ps", ob)], writes=["acc"])
                        elif kind == "p2":
                            av = acc.rearrange("p (s r) -> p r s", r=4)[:, v, :]
                            S.op("dve", lambda e, ob=ob, av=av: e.tensor_tensor(out=av, in0=psb[ob][:, :], in1=av, op=ALU.add),
                                 reads=[("ps", ob), "acc"], writes=["acc"])
                        else:
                            av = acc.rearrange("p (s r) -> p r s", r=16)[:, v:v + 4, :]
                            S.op("dve", lambda e, ob=ob, av=av: e.tensor_tensor(
                                out=av, in0=psb[ob][:, :].rearrange("p (r s) -> p r s", r=4), in1=av, op=ALU.add),
                                reads=[("ps", ob), "acc"], writes=["acc"])


                    def make_fin(c=c, orow=orow, drow=drow):
                        def fin():
                            for n in range(NTC):
                                bk = bank()
                                S.op("act", lambda e, bk=bk, n=n: e.activation(out=psb[bk][drow, :], in_=acc[drow, n * TC:(n + 1) * TC], func=AF.Ln),
                                     reads=["acc"], writes=[("ps", bk)])
                                S.op("act", lambda e, bk=bk: e.activation(out=psb[bk][drow, :], in_=psb[bk][drow, :], func=AF.Exp, scale=-1.0),
                                     reads=[("ps", bk)], writes=[("ps", bk)])
                                S.op("dve", lambda e, bk=bk, n=n: e.tensor_tensor(
                                    out=yaT[orow, c, n * TC:(n + 1) * TC], in0=acc[orow, n * TC:(n + 1) * TC], in1=psb[bk][drow, :], op=ALU.mult),
                                    reads=["acc", ("ps", bk)], writes=[("ya", c)])
                        return fin
                    for ot_ in otiles:
                        ot_["evac"] = evacA
                    if u == 0:
                        otiles[-1]["post"] = make_fin()
                    else:
                        pending_fin[0] = make_fin()
                    pair_otiles.extend(otiles)
                attention(pair_otiles, None)
            if pending_fin[0] is not None:
                pending_fin[0]()
                pending_fin[0] = None
            if stage <= 0.6:
                break
            if stage < 2:
                if dbg == 1:
                    S.dma("sp", lambda e: e.dma_start(out=dbg_d[:, :], in_=Rv(0, 16 * KB, BF16)), "dbgc", reads=K_YA, writes=["dbgout"])
                elif dbg == 2:
                    S.dma("sp", lambda e: e.dma_start(out=dbg_d[:, 0:T], in_=qa_v), "dbgc", reads=["qa"], writes=["dbgout"])
                    S.dma("sp", lambda e: e.dma_start(out=dbg_d[:, T:2 * T], in_=ka_v), "dbgc2", reads=["ka"], writes=["dbgout2"])
                elif dbg == 3:
                    S.dma("sp", lambda e: e.dma_start(out=dbg_d[:, :], in_=hT[:, 0:4, :].rearrange("p c t -> p (c t)")), "dbgc", reads=[("h", n) for n in range(4)], writes=["dbgout"])
                break

            s1 = ring_ctr[0] % 3
            ring_ctr[0] += 1
            Wq = wview(s1, 8, 512)
            src3 = w_in_d[l].rearrange("(k p) n -> p k n", p=128)
            Wq5 = ring[s1][:, 0:4096].rearrange("p (k c u e) -> p k c u e", k=8, c=4, u=2)
            for u in range(2):
                for cq in range(4):
                    S.dma("pool", lambda e, u=u, cq=cq, Wq5=Wq5, src3=src3: e.dma_start(
                        out=Wq5[:, :, cq, u, :],
                        in_=src3[:, :, 1536 + u * 256 + cq * 64:1536 + u * 256 + (cq + 1) * 64]),
                        ("wch", s1), writes=[("w", s1)])
            s2 = ring_ctr[0] % 3
            ring_ctr[0] += 1
            Wkv = wview(s2, 8, 256)
            S.dma("pool", lambda e: e.dma_start(out=Wkv[:, :, :], in_=src3[:, :, 2048:2304]), ("wch", s2), writes=[("w", s2)])
            S.alias(K_B, ALLR)
            for cc in range(4):
                qk_proj(l, s1, 512, cc * 128, 26, lambda n, cc=cc: qbT[:, cc, n * TC:(n + 1) * TC],
                        lambda n, cc=cc: [("qb", cc, n)], flush=False)
            qk_proj(l, s2, 256, 0, 27, lambda n: kbT[:, n * TC:(n + 1) * TC], lambda n: ["kb"])
            S.op("dve", lambda e: e.memset(vbt[:, :, 64:128], 1.0), reads=[], writes=["vbt"])
            for t0 in range(0, 16, 4):
                bk = bank()
                for gi in range(4):
                    t = t0 + gi
                    for k in range(NCH):
                        S.op("pe", lambda e, bk=bk, gi=gi, t=t, k=k: e.matmul(
                            psb[bk][:, gi * 128:(gi + 1) * 128], lhsT=hT[:, k, t * 128:(t + 1) * 128], rhs=Wkv[:, k, 128:256],
                            start=(k == 0), stop=(k == NCH - 1)),
                            reads=[("w", s2), ("h", t // 4)], writes=[("ps", bk)], inc=(k == NCH - 1 and gi == 3))
                S.op("act", lambda e, bk=bk, t0=t0: e.copy(
                    out=vbt[:, t0:t0 + 4, 0:64], in_=psb[bk][:, :].rearrange("p (t f) -> p t f", f=128)[:, :, 0:64]),
                    reads=[("ps", bk)], writes=["vbt"])
                S.op("dve", lambda e, bk=bk, t0=t0: e.tensor_copy(
                    out=vbt[:, t0:t0 + 4, 128:192], in_=psb[bk][:, :].rearrange("p (t f) -> p t f", f=128)[:, :, 64:128]),
                    reads=[("ps", bk)], writes=["vbt"])

            b_otiles = []
            for cc in range(4):
                for u in range(2):
                    hq = u * 4 + cc
                    rows = slice(u * 64, (u + 1) * 64)
                    vcols = slice(0, 128) if u == 0 else slice(64, 192)
                    orow = slice(0, 64) if u == 0 else slice(64, 128)
                    drow = slice(64, 128) if u == 0 else slice(0, 64)
                    otiles = []
                    for Q in range(4):
                        blocks = []
                        for b in range(4 * Q, 4 * Q + 4):
                            blk = []
                            for dj, shift in ((1, 0), (0, 128), (-1, 256)):
                                kt = b + dj
                                if kt < 0 or kt > 15:
                                    continue
                                blk.append(dict(k=kbT[rows, kt * 128:(kt + 1) * 128], q=qbT[rows, cc, b * 128:(b + 1) * 128], M=128,
                                                ecol=SBASE[3] + shift, eh=hq, v=vbt[:, kt, vcols],
                                                reads=[("qb", cc, Q), "kb"], vreads=["vbt"]))
                            blocks.append(blk)
                        otiles.append(dict(blocks=blocks, kind=Q, gsz=4))

                    def evacB(oi, ob, ot, cc=cc, hq=hq, orow=orow, drow=drow):
                        Q = ot["kind"]
                        ri = scr_ctr[0] % 2
                        scr_ctr[0] += 1
                        S.op("act", lambda e, ob=ob, ri=ri: e.activation(
                            out=scr[ri][drow, :], in_=psb[ob][drow, :], func=AF.Ln, bias=vd[drow, vb0 + 28 + hq: vb0 + 29 + hq]),
                            reads=[("ps", ob)] + VD, writes=[("scr", ri)])
                        S.op("act", lambda e, ri=ri: e.activation(out=scr[ri][drow, :], in_=scr[ri][drow, :], func=AF.Exp, scale=-1.0),
                            reads=[("scr", ri)], writes=[("scr", ri)])
                        S.op("dve", lambda e, ob=ob, ri=ri: e.tensor_tensor(
                            out=qbT[orow, cc, Q * 512:(Q + 1) * 512], in0=psb[ob][orow, :], in1=scr[ri][drow, :], op=ALU.mult),
                            reads=[("ps", ob), ("scr", ri)], writes=[("qb", cc, Q)])

                    for ot_ in otiles:
                        ot_["evac"] = evacB
                    b_otiles.extend(otiles)
            attention(b_otiles, None)
            if stage < 3:
                if dbg == 4:
                    S.dma("sp", lambda e: e.dma_start(out=dbg_d[:, :], in_=Rv(16 * KB, 16 * KB, BF16)), "dbgc", reads=K_B, writes=["dbgout"])
                break

            S.alias(K_G, ALLR)
            ybT = qbT
            for mb in range(2):
                sA = ring_ctr[0] % 3
                ring_ctr[0] += 1
                WA = wview(sA, 8, 512)
                S.dma("pool", lambda e, WA=WA, mb=mb: e.dma_start(
                    out=WA[:, 0:4, :], in_=w_a_d[l].rearrange("(k p) n -> p k n", p=128)[:, :, mb * 512:(mb + 1) * 512]),
                    ("wch", sA), writes=[("w", sA)])
                for u in range(2):
                    S.dma("pool", lambda e, WA=WA, mb=mb, u=u: e.dma_start(
                        out=WA[u * 64:(u + 1) * 64, 4:8, :],
                        in_=w_b_d[l][u * 256:(u + 1) * 256, mb * 512:(mb + 1) * 512].rearrange("(c e) n -> e c n", e=64)),
                        ("wch", sA), writes=[("w", sA)])
                for mi in range(4):
                    if mi % 2 == 0:
                        sGa = ring_ctr[0] % 3
                        ring_ctr[0] += 1
                        sGb = sGa
                        Wg2 = ring[sGa][:, 0:4096].rearrange("p (w k n) -> p w k n", w=2, k=8)
                        WGa = Wg2[:, 0]
                        WGb = Wg2[:, 1]
                        c0g = mb * 512 + (mi // 2) * 256
                        S.dma("pool", lambda e, WGa=WGa, c0g=c0g: e.dma_start(
                            out=WGa, in_=src3[:, :, 2304 + c0g:2304 + c0g + 256]), ("wch", sGa), writes=[("w", sGa)])
                        S.dma("pool", lambda e, WGb=WGb, c0g=c0g: e.dma_start(
                            out=WGb, in_=src3[:, :, 3328 + c0g:3328 + c0g + 256]), ("wch", sGb), writes=[("w", sGb)])
                    mg_ = mi % 2
                    for n in range(NTC):
                        nsl = slice(n * TC, (n + 1) * TC)
                        bA, bB, bGa, bGb = bank(), bank(), bank(), bank()
                        for k in range(4):
                            S.op("pe", lambda e, k=k, bA=bA, mi=mi, nsl=nsl, WA=WA: e.matmul(
                                psb[bA][:, :], lhsT=WA[:, k, mi * 128:(mi + 1) * 128], rhs=yaT[:, k, nsl], start=(k == 0), stop=(k == 3)),
                                reads=[("w", sA), ("ya", k)], writes=[("ps", bA)], inc=(k == 3))
                        for k in range(4):
                            S.op("pe", lambda e, k=k, bB=bB, mi=mi, nsl=nsl, WA=WA: e.matmul(
                                psb[bB][:, :], lhsT=WA[:, 4 + k, mi * 128:(mi + 1) * 128], rhs=ybT[:, k, nsl], start=(k == 0), stop=(k == 3)),
                                reads=[("w", sA), ("qb", k, n)], writes=[("ps", bB)], inc=(k == 3))
                        for k in range(NCH):
                            S.op("pe", lambda e, k=k, bGa=bGa, mg_=mg_, nsl=nsl, WGa=WGa: e.matmul(
                                psb[bGa][:, :], lhsT=WGa[:, k, mg_ * 128:(mg_ + 1) * 128], rhs=hT[:, k, nsl], start=(k == 0), stop=(k == 7)),
                                reads=[("w", sGa), ("h", n)], writes=[("ps", bGa)], inc=(k == 7))
                        for k in range(NCH):
                            S.op("pe", lambda e, k=k, bGb=bGb, mg_=mg_, nsl=nsl, WGb=WGb: e.matmul(
                                psb[bGb][:, :], lhsT=WGb[:, k, mg_ * 128:(mg_ + 1) * 128], rhs=hT[:, k, nsl], start=(k == 0), stop=(k == 7)),
                                reads=[("w", sGb), ("h", n)], writes=[("ps", bGb)], inc=(k == 7))
                        r1 = scr_ctr[0] % 2
                        r2 = (scr_ctr[0] + 1) % 2
                        scr_ctr[0] += 2
                        S.op("act", lambda e, bGa=bGa, r1=r1: e.activation(out=scr[r1][:, :], in_=psb[bGa][:, :], func=AF.Sigmoid),
                             reads=[("ps", bGa)], writes=[("scr", r1)])
                        S.op("act", lambda e, bGb=bGb, r2=r2: e.activation(out=scr[r2][:, :], in_=psb[bGb][:, :], func=AF.Sigmoid),
                             reads=[("ps", bGb)], writes=[("scr", r2)])
                        S.op("dve", lambda e, bA=bA, r1=r1: e.tensor_tensor(out=scr[r1][:, :], in0=psb[bA][:, :], in1=scr[r1][:, :], op=ALU.mult),
                             reads=[("ps", bA), ("scr", r1)], writes=[("scr", r1)])
                        S.op("dve", lambda e, bB=bB, r2=r2: e.tensor_tensor(out=scr[r2][:, :], in0=psb[bB][:, :], in1=scr[r2][:, :], op=ALU.mult),
                             reads=[("ps", bB), ("scr", r2)], writes=[("scr", r2)])
                        S.op("dve", lambda e, r1=r1, r2=r2, mi=mi, nsl=nsl: e.tensor_tensor(
                            out=merged[:, mi, nsl], in0=scr[r1][:, :], in1=scr[r2][:, :], op=ALU.add),
                            reads=[("scr", r1), ("scr", r2)], writes=[("mg", n)])
                sO = ring_ctr[0] % 3
                ring_ctr[0] += 1
                WO = wview(sO, 4, 1024)
                S.dma("pool", lambda e, WO=WO, mb=mb: e.dma_start(
                    out=WO[:, :, :], in_=w_out_d[l][mb * 512:(mb + 1) * 512, :].rearrange("(k p) n -> p k n", p=128)),
                    ("wch", sO), writes=[("w", sO)])
                if mb == 1 and stage >= 4:
                    preloaded[("gu", l, 0, 0)] = load_gu(l, 0, 0, 256)
                    preloaded[("gu", l, 0, 256)] = load_gu(l, 0, 256, 256)
                for n in range(NTC):
                    for m2 in range(NCH):
                        bk = bank()
                        for k in range(4):
                            S.op("pe", lambda e, k=k, n=n, bk=bk, m2=m2, WO=WO: e.matmul(
                                psb[bk][:, :], lhsT=WO[:, k, m2 * 128:(m2 + 1) * 128], rhs=merged[:, k, n * TC:(n + 1) * TC],
                                start=(k == 0), stop=(k == 3)),
                                reads=[("w", sO), ("mg", n)], writes=[("ps", bk)], inc=(k == 3))
                        S.op("dve", lambda e, bk=bk, m2=m2, n=n: e.tensor_tensor(
                            out=xT[:, m2, n * TC:(n + 1) * TC], in0=psb[bk][:, :], in1=xT[:, m2, n * TC:(n + 1) * TC], op=ALU.add),
                            reads=[("ps", bk), ("x", m2, n)], writes=[("x", m2, n)])
                    if mb == 1 and n >= 1 and stage >= 4:
                        norm_chunk(l, 8, n - 1)
                if mb == 1 and stage >= 4:
                    norm_chunk(l, 8, NTC - 1)
            if stage < 4:
                break

            S.alias(K_F, ALLR)
            wg3 = w_g_d[l].rearrange("(k p) n -> p k n", p=128)
            wu3 = w_u_d[l].rearrange("(k p) n -> p k n", p=128)
            for fb in range(2):
                f0 = fb * 1408
                fi = 0
                for cb, ncols in ((0, 256), (256, 256), (512, 256), (768, 256), (1024, 256), (1280, 128)):
                    if ("gu", l, fb, cb) in preloaded:
                        sG, WG, WU = preloaded.pop(("gu", l, fb, cb))
                    else:
                        sG, WG, WU = load_gu(l, fb, cb, ncols)
                    sU = sG
                    for mi in range(ncols // 128):
                        for half in range(2):
                            ns = (2 * half, 2 * half + 1)
                            bG = {n: bank() for n in ns}
                            bU = {n: bank() for n in ns}
                            for k in range(NCH):
                                for n in ns:
                                    S.op("pe", lambda e, k=k, n=n, bk=bG[n], mi=mi, WG=WG: e.matmul(
                                        psb[bk][:, :], lhsT=WG[:, k, mi * 128:(mi + 1) * 128], rhs=hT[:, k, n * TC:(n + 1) * TC],
                                        start=(k == 0), stop=(k == 7)),
                                        reads=[("w", sG), ("h", n)], writes=[("ps", bG[n])], inc=(k == 7))
                            for k in range(NCH):
                                for n in ns:
                                    S.op("pe", lambda e, k=k, n=n, bk=bU[n], mi=mi, WU=WU: e.matmul(
                                        psb[bk][:, :], lhsT=WU[:, k, mi * 128:(mi + 1) * 128], rhs=hT[:, k, n * TC:(n + 1) * TC],
                                        start=(k == 0), stop=(k == 7)),
                                        reads=[("w", sU), ("h", n)], writes=[("ps", bU[n])], inc=(k == 7))
                            for n in ns:
                                ri = scr_ctr[0] % 2
                                scr_ctr[0] += 1
                                S.op("act", lambda e, bk=bG[n], ri=ri: e.activation(out=scr[ri][:, :], in_=psb[bk][:, :], func=AF.Silu),
                                     reads=[("ps", bG[n])], writes=[("scr", ri)])
                                S.op("dve", lambda e, bk=bU[n], ri=ri, fi=fi, n=n: e.tensor_tensor(
                                    out=hid[:, fi, n * TC:(n + 1) * TC], in0=psb[bk][:, :], in1=scr[ri][:, :], op=ALU.mult),
                                    reads=[("ps", bU[n]), ("scr", ri)], writes=[("hid", n)])
                        fi += 1
                for mq in range(4):
                    sD = ring_ctr[0] % 3
                    ring_ctr[0] += 1
                    WD = wview(sD, 11, 256)
                    S.dma("pool", lambda e, WD=WD, f0=f0, mq=mq: e.dma_start(
                        out=WD[:, :, :], in_=w_d_d[l][f0:f0 + 1408, mq * 256:(mq + 1) * 256].rearrange("(k p) n -> p k n", p=128)),
                        ("wch", sD), writes=[("w", sD)])
                    for ml in range(2):
                        m2 = mq * 2 + ml
                        bks = [bank() for _ in range(NTC)]
                        for k in range(11):
                            for n in range(NTC):
                                S.op("pe", lambda e, k=k, n=n, bk=bks[n], ml=ml, WD=WD: e.matmul(
                                    psb[bk][:, :], lhsT=WD[:, k, ml * 128:(ml + 1) * 128], rhs=hid[:, k, n * TC:(n + 1) * TC],
                                    start=(k == 0), stop=(k == 10)),
                                    reads=[("w", sD), ("hid", n)], writes=[("ps", bks[n])], inc=(k == 10))
                        for n in range(NTC):
                            S.op("dve", lambda e, bk=bks[n], m2=m2, n=n: e.tensor_tensor(
                                out=xT[:, m2, n * TC:(n + 1) * TC], in0=psb[bk][:, :], in1=xT[:, m2, n * TC:(n + 1) * TC], op=ALU.add),
                                reads=[("ps", bks[n]), ("x", m2, n)], writes=[("x", m2, n)])
            if stage < 5:
                break

            sP = ring_ctr[0] % 3
            ring_ctr[0] += 1
            WP = wview(sP, 2, 1024)
            S.dma("pool", lambda e, WP=WP: e.dma_start(out=WP[:, :, :], in_=w_pp_d[l].rearrange("(k p) n -> p k n", p=128)),
                  ("wch", sP), writes=[("w", sP)])
            wpg3 = w_pg_d[l].rearrange("(k p) n -> p k n", p=128)
            WGs = []
            for mb in range(2):
                sG = ring_ctr[0] % 3
                ring_ctr[0] += 1
                WG = wview(sG, 8, 512)
                S.dma("pool", lambda e, WG=WG, mb=mb: e.dma_start(out=WG[:, :, :], in_=wpg3[:, :, mb * 512:(mb + 1) * 512]),
                      ("wch", sG), writes=[("w", sG)])
                WGs.append((sG, WG))
            rmsnorm_to_hT(l, 16)
            S.alias(K_P, ALLR)
            for half in range(2):
                S.dma("sp", lambda e, half=half: e.dma_start(
                    out=ptok[:, half * 8:(half + 1) * 8, :],
                    in_=p_d[l][half * 1024:(half + 1) * 1024, :].rearrange("(t p) f -> p t f", p=128)),
                    ("pl", half), writes=[("ptok", half)])
            for fc in range(2):
                for n in range(NTC):
                    bk = bank()
                    for i in range(4):
                        t = 4 * n + i
                        S.op("pe", lambda e, bk=bk, i=i, t=t, fc=fc: e.transpose(
                            out=psb[bk][:, i * 128:(i + 1) * 128], in_=ptok[:, t, fc * 128:(fc + 1) * 128], identity=ident_f),
                            reads=[("ptok", t // 8), "cst_f"], writes=[("ps", bk)], inc=(i == 3))
                    S.op("act", lambda e, bk=bk, fc=fc, n=n: e.copy(out=ppT[:, fc, n * TC:(n + 1) * TC], in_=psb[bk][:, :]),
                         reads=[("ps", bk)], writes=[("ppT", n)])
            last_layer = (l == n_layers - 1)
            if last_layer:
                S.alias(K_O, ALLR)

            def after_chunk(n):
                if not last_layer:
                    norm_chunk(l + 1, 0, n)
                else:
                    for t in range(4 * n, 4 * n + 4):
                        out_tile(t)
                    out_done.add(n)

            for n in range(NTC):
                for m2 in range(NCH):
                    mb, mi = divmod(m2, 4)
                    sG, WG = WGs[mb]
                    bG = bank()
                    for k in range(NCH):
                        S.op("pe", lambda e, k=k, n=n, bk=bG, mi=mi, WG=WG: e.matmul(
                            psb[bk][:, :], lhsT=WG[:, k, mi * 128:(mi + 1) * 128], rhs=hT[:, k, n * TC:(n + 1) * TC],
                            start=(k == 0), stop=(k == 7)),
                            reads=[("w", sG), ("h", n)], writes=[("ps", bG)], inc=(k == 7))
                    bP = bank()
                    for k in range(2):
                        S.op("pe", lambda e, k=k, n=n, bk=bP, m2=m2, WP=WP: e.matmul(
                            psb[bk][:, :], lhsT=WP[:, k, m2 * 128:(m2 + 1) * 128], rhs=ppT[:, k, n * TC:(n + 1) * TC],
                            start=(k == 0), stop=(k == 1)),
                            reads=[("w", sP), ("ppT", n)], writes=[("ps", bP)], inc=(k == 1))
                    ri = scr_ctr[0] % 2
                    scr_ctr[0] += 1
                    S.op("act", lambda e, bk=bG, ri=ri: e.activation(out=scr[ri][:, :], in_=psb[bk][:, :], func=AF.Sigmoid),
                         reads=[("ps", bG)], writes=[("scr", ri)])
                    S.op("dve", lambda e, bk=bP, ri=ri: e.tensor_tensor(out=scr[ri][:, :], in0=psb[bk][:, :], in1=scr[ri][:, :], op=ALU.mult),
                         reads=[("ps", bP), ("scr", ri)], writes=[("scr", ri)])
                    S.op("dve", lambda e, ri=ri, m2=m2, n=n: e.tensor_tensor(
                        out=xT[:, m2, n * TC:(n + 1) * TC], in0=scr[ri][:, :], in1=xT[:, m2, n * TC:(n + 1) * TC], op=ALU.add),
                        reads=[("scr", ri), ("x", m2, n)], writes=[("x", m2, n)])
                if n >= 1:
                    after_chunk(n - 1)
            after_chunk(NTC - 1)

        if len(out_done) < NTC:
            S.alias(K_O, ALLR)
            for t in range(16):
                if t // 4 not in out_done:
                    out_tile(t)
        S.wait_all("sp", [("yout", 0), ("yout", 1), "dbgout", "dbgout2"])

        with nc.Block() as block:
            @block.tensor
            def _(e):
                for f in S.prog["pe"]:
                    f(e)

            @block.scalar
            def _(e):
                for f in S.prog["act"]:
                    f(e)

            @block.vector
            def _(e):
                for f in S.prog["dve"]:
                    f(e)

            @block.gpsimd
            def _(e):
                for f in S.prog["pool"]:
                    f(e)

            @block.sync
            def _(e):
                for f in S.prog["sp"]:
                    f(e)
        build.stats = {k: len(v) for k, v in S.prog.items()}
        build.stats["waits"] = S.nwait
        build.stats["sems"] = len(S.sems)
    return nc


def wview_of(rg, k, n):
    return rg[:, 0:k * n].rearrange("p (k n) -> p k n", k=k)


def _pack_vecs(inp):
    v = np.zeros((128, 2 * NVL), np.float32)
    for l in range(2):
        b = l * NVL
        v[:, b + 0:b + 8] = np.asarray(inp["norm_mix_g"][l]).reshape(8, 128).T
        v[:, b + 8:b + 16] = np.asarray(inp["norm_ffn_g"][l]).reshape(8, 128).T
        v[:, b + 16:b + 24] = np.asarray(inp["norm_ple_g"][l]).reshape(8, 128).T
        v[:, b + 24] = np.tile(np.asarray(inp["qnorm_a_g"][l]), 2)
        v[:, b + 25] = np.tile(np.asarray(inp["knorm_a_g"][l]), 2)
        v[:, b + 26] = np.tile(np.asarray(inp["qnorm_b_g"][l]), 2)
        v[:, b + 27] = np.tile(np.asarray(inp["knorm_b_g"][l]), 2)
        v[:, b + 28:b + 36] = np.broadcast_to(np.asarray(inp["sink_b"][l])[None, :], (128, 8))
    return v


_NC_CACHE = {}


def make_in_maps(inp, n_cores=8):
    f = lambda a: np.ascontiguousarray(np.asarray(a, dtype=np.float32))
    shared = {k: f(inp[k]) for k in ("w_in", "w_branch_a", "w_branch_b", "w_out", "w_ffn_gate", "w_ffn_up",
                                     "w_ffn_down", "w_ple_gate", "w_ple_proj", "rel_table")}
    shared["vecs"] = _pack_vecs(inp)
    shared["cst"] = _cst()
    shared["emat"] = _emat()
    x = f(inp["x"])
    p = f(inp["p"])
    maps = []
    for b in range(n_cores):
        m = dict(shared)
        m["x"] = np.ascontiguousarray(x[b])
        m["p"] = np.ascontiguousarray(p[:, b])
        maps.append(m)
    return maps


def kernel(**inputs):
    if "nc" not in _NC_CACHE:
        _NC_CACHE["nc"] = build(2)
    nc = _NC_CACHE["nc"]
    in_maps = make_in_maps(inputs, 8)
    res = run_bass_kernel_spmd(nc, in_maps, core_ids=list(range(8)))
    out = np.stack([np.asarray(r["y"], dtype=np.float32) for r in res.results], axis=0)
    return out
```

```python
import numpy as np
from contextlib import ExitStack
import concourse.bass as bass
import concourse.mybir as mybir
from concourse.bass_utils import run_bass_kernel_spmd

F32 = mybir.dt.float32
BF16 = mybir.dt.bfloat16
AF = mybir.ActivationFunctionType
ALU = mybir.AluOpType

T = 2048
D = 1024
NCH = 8
TC = 512
NTC = 4
DFF = 2816
EPS = 1e-6
NEG = -30000.0
NVL = 36
STRIPS = [(1, 64, 128, 320), (4, 64, 128, 320), (16, 64, 0, 128), (1, 128, 128, 384)]
SBASE = [0, 320, 640, 768]
EW = 1152


class _Rec:
    def __init__(self):
        self.call = None

    def __getattr__(self, name):
        def f(*a, **k):
            self.call = (name, a, k)
            return self
        return f


def _capture(fn):
    r = _Rec()
    fn(r)
    assert r.call is not None
    return r.call


class Sched:
    ENG = ("pe", "act", "dve", "pool", "sp")

    def __init__(self, nc, stack):
        self.nc = nc
        self.stack = stack
        self.prog = {e: [] for e in self.ENG}
        self.sems = {}
        self.count = {}
        self.waited = {e: {} for e in self.ENG}
        self.res = {}
        self.nwait = 0

    def _sem(self, key):
        if key not in self.sems:
            name = "s_" + "_".join(str(k) for k in (key if isinstance(key, tuple) else (key,)))
            self.sems[key] = self.stack.enter_context(self.nc.semaphore(name))
            self.count[key] = 0
        return self.sems[key]

    def _deps(self, reads, writes):
        toks = []
        for r in reads:
            st = self.res.get(r)
            if st and st[0]:
                toks.append(st[0])
        for w in writes:
            st = self.res.get(w)
            if st:
                if st[0]:
                    toks.append(st[0])
                toks.extend(st[1])
        return toks

    def _waits(self, eng, toks, rawkeys):
        need = {}
        for (k, v) in toks:
            if v > need.get(k, 0):
                need[k] = v
        for k, v in need.items():
            if k == eng and (eng == "pe" or k not in rawkeys):
                continue
            if self.waited[eng].get(k, 0) >= v:
                continue
            self.waited[eng][k] = v
            s = self._sem(k)
            self.prog[eng].append(lambda e, s=s, v=v: e.wait_ge(s, v))
            self.nwait += 1

    def _record(self, tok, reads, writes):
        for r in reads:
            st = self.res.setdefault(r, [None, []])
            st[1].append(tok)
        for w in writes:
            self.res[w] = [tok, []]

    def op(self, eng, fn, reads=(), writes=(), inc=True):
        toks = self._deps(reads, writes)
        raw = set()
        for r in reads:
            st = self.res.get(r)
            if st and st[0]:
                raw.add(st[0][0])
        self._waits(eng, toks, raw)
        s = self._sem(eng)
        call = _capture(fn)
        if inc:
            self.count[eng] += 1
            tok = (eng, self.count[eng])
            self.prog[eng].append(lambda e, c=call, s=s: getattr(e, c[0])(*c[1], **c[2]).then_inc(s, 1))
        else:
            tok = (eng, self.count[eng] + 1)
            self.prog[eng].append(lambda e, c=call: getattr(e, c[0])(*c[1], **c[2]))
        self._record(tok, reads, writes)

    def dma(self, eng, fn, chan, reads=(), writes=()):
        toks = [t for t in self._deps(reads, writes) if t[0] != chan]
        self._waits(eng, toks, set(k for k, _ in toks))
        s = self._sem(chan)
        self.count[chan] += 16
        tok = (chan, self.count[chan])
        call = _capture(fn)
        self.prog[eng].append(lambda e, c=call, s=s: getattr(e, c[0])(*c[1], **c[2]).then_inc(s, 16))
        self._record(tok, reads, writes)

    def alias(self, new_keys, old_keys):
        toks = []
        nk = set(new_keys)
        for k in old_keys:
            if k in nk:
                continue
            st = self.res.get(k)
            if st:
                if st[0]:
                    toks.append(st[0])
                toks.extend(st[1])
        red = {}
        for (k, v) in toks:
            if v > red.get(k, 0):
                red[k] = v
        toks = list(red.items())
        for k in new_keys:
            st = self.res.setdefault(k, [None, []])
            st[1].extend(toks)

    def wait_all(self, eng, keys):
        toks = []
        for k in keys:
            st = self.res.get(k)
            if st:
                if st[0]:
                    toks.append(st[0])
                toks.extend(st[1])
        self._waits(eng, toks, set(k for k, _ in toks))


def _t5_bucket_np(rel):
    rel = np.asarray(rel, dtype=np.int64)
    half_b, max_exact = 16, 8
    sign = np.where(rel > 0, half_b, 0)
    n = np.abs(rel)
    nf = np.maximum(n, 1).astype(np.float32)
    large = max_exact + (np.log(nf / np.float32(max_exact)) / np.float32(np.log(1024 / max_exact))
                         * np.float32(half_b - max_exact)).astype(np.int32)
    large = np.minimum(large, half_b - 1)
    return sign + np.where(n < max_exact, n, large)


def _emat():
    E = np.zeros((33, len(STRIPS), 512), np.float32)
    for si, (d, half, c0, width) in enumerate(STRIPS):
        m = np.arange(512)
        rel = m - (width - 1 - c0)
        code = np.where(np.abs(rel) <= half, _t5_bucket_np(rel * d), 32)
        E[code, si, m] = 1.0
    return E.reshape(33, len(STRIPS) * 512)


def _cst():
    c = np.zeros((128, 512), np.float32)
    c[:, 384:512] = np.roll(np.eye(128, dtype=np.float32), 64, axis=1)
    c[:, 0:128] = np.eye(128, dtype=np.float32)
    c[:, 128:256] = 1.0
    c[0:64, 256:320] = 1.0
    c[64:128, 320:384] = 1.0
    return c


def build(n_layers=2, stage=99, dbg=0):
    nc = bass.Bass("TRN2", target_bir_lowering=False)
    dram = {}

    def din(name, shape):
        dram[name] = nc.dram_tensor(name, list(shape), F32, kind="ExternalInput").ap()
        return dram[name]

    x_d = din("x", [T, D])
    p_d = din("p", [2, T, 256])
    w_in_d = din("w_in", [2, D, 4352])
    w_a_d = din("w_branch_a", [2, 512, D])
    w_b_d = din("w_branch_b", [2, 512, D])
    w_out_d = din("w_out", [2, D, D])
    w_g_d = din("w_ffn_gate", [2, D, DFF])
    w_u_d = din("w_ffn_up", [2, D, DFF])
    w_d_d = din("w_ffn_down", [2, DFF, D])
    w_pg_d = din("w_ple_gate", [2, D, D])
    w_pp_d = din("w_ple_proj", [2, 256, D])
    vecs_d = din("vecs", [128, 2 * NVL])
    tab_d = din("rel_table", [32, 16])
    cst_d = din("cst", [128, 512])
    emat_d = din("emat", [33, 4 * 512])
    y_d = nc.dram_tensor("y", [T, D], F32, kind="ExternalOutput").ap()
    dbg_d = None
    if dbg:
        dbg_d = nc.dram_tensor("dbg", [128, 4 * T], BF16, kind="ExternalOutput").ap()

    with ExitStack() as stack:
        def sb(name, shape, dt):
            return stack.enter_context(nc.sbuf_tensor(name, list(shape), dt))

        xT = sb("xT", [128, NCH, T], F32)
        hT = sb("hT", [128, NCH, T], BF16)
        ring = [sb(f"ring{i}", [128, 4096], BF16) for i in range(3)]
        Ebuf = sb("Ebuf", [128, 8, EW], BF16)
        R = sb("R", [128, 26624], BF16)
        scr = [sb(f"scr{i}", [128, 512], F32) for i in range(2)]
        sq_all = sb("sq_all", [128, 4 * 512], BF16)
        sqb = [sq_all[:, i * 512:(i + 1) * 512] for i in range(4)]
        pTs = [sb(f"pT{i}", [128, 512], BF16) for i in range(6)]
        cst_f = sb("cst_f", [128, 256], F32)
        cst_b = sb("cst_b", [128, 384], BF16)
        vecs = sb("vecs_s", [128, 2 * NVL], F32)
        vd = sb("vd", [128, 2 * NVL], F32)
        tab = sb("tab", [33, 16], BF16)
        tabf = sb("tabf", [33, 16], F32)
        psb = [stack.enter_context(nc.psum_tensor(f"ps{i}", [128, 512], F32)) for i in range(8)]

        S = Sched(nc, stack)
        ident_f = cst_f[:, 0:128]
        swap_f = cst_f[:, 128:256]
        ident_b = cst_b[:, 0:128]
        ones_b = cst_b[:, 128:256]
        bones_b = cst_b[:, 256:384]

        def Rv(off_bytes, nbytes, dt):
            a = R[:, off_bytes // 2:(off_bytes + nbytes) // 2]
            return a if dt == BF16 else a.bitcast(dt)

        KB = 1024
        yaT = Rv(0, 16 * KB, BF16).rearrange("p (c t) -> p c t", c=4)
        qa_v = Rv(16 * KB, 4 * KB, BF16)
        ka_v = Rv(20 * KB, 4 * KB, BF16)
        vt = Rv(24 * KB, 53 * 384, BF16).rearrange("p (t f) -> p t f", f=192)
        acc = Rv(44 * KB, 8 * KB, F32)
        qbT = Rv(16 * KB, 16 * KB, BF16).rearrange("p (c t) -> p c t", c=4)
        kbT = Rv(32 * KB, 4 * KB, BF16)
        vbt = Rv(36 * KB, 16 * 384, BF16).rearrange("p (t f) -> p t f", f=192)
        merged = Rv(32 * KB, 16 * KB, BF16).rearrange("p (c t) -> p c t", c=4)
        hid = Rv(0, 44 * KB, BF16).rearrange("p (c t) -> p c t", c=11)
        ptok = Rv(0, 16 * KB, F32).rearrange("p (t f) -> p t f", f=256)
        ppT = Rv(16 * KB, 8 * KB, BF16).rearrange("p (c t) -> p c t", c=2)
        xtmp = Rv(0, 16 * KB, F32).rearrange("p (s f) -> p s f", s=4)
        Emat = Rv(16 * KB, 4 * KB, BF16)
        otile = Rv(0, 8 * KB, F32).rearrange("p (s f) -> p s f", s=2)
        K_SETUP = [("xtmp", i) for i in range(8)] + ["Emat"]
        K_YA = [("ya", i) for i in range(4)]
        K_A = ["qa", "ka", "vt", "acc"]
        K_B = [("qb", i, j) for i in range(4) for j in range(4)] + ["kb", "vbt"]
        K_G = [("mg", n) for n in range(4)]
        K_F = [("hid", n) for n in range(4)]
        K_P = [("ptok", 0), ("ptok", 1)] + [("ppT", n) for n in range(4)]
        K_O = [("ot", 0), ("ot", 1)]
        ALLR = K_SETUP + K_YA + K_A + K_B + K_G + K_F + K_P + K_O

        bank_ctr = [0]

        def bank():
            b = bank_ctr[0] % 8
            bank_ctr[0] += 1
            return b

        ring_ctr = [0]

        def wload(parts):
            s = ring_ctr[0] % 3
            ring_ctr[0] += 1
            for dst_fn, src in parts:
                dst = dst_fn(ring[s])
                S.dma("pool", lambda e, dst=dst, src=src: e.dma_start(out=dst, in_=src),
                      ("wch", s), writes=[("w", s)])
            return s

        def wview(s, k, n):
            return ring[s][:, 0:k * n].rearrange("p (k n) -> p k n", k=k)

        evac_ctr = [0]

        def load_pair_w(l, c):
            s_ = ring_ctr[0] % 3
            ring_ctr[0] += 1
            W_ = wview(s_, 8, 384)
            src_ = w_in_d[l].rearrange("(k p) n -> p k n", p=128)
            for part, base in enumerate((0, 512, 1024)):
                S.dma("pool", lambda e, W_=W_, part=part, base=base, src_=src_: e.dma_start(
                    out=W_[:, :, part * 128:(part + 1) * 128], in_=src_[:, :, base + c * 128: base + (c + 1) * 128]),
                    ("wch", s_), writes=[("w", s_)])
            return s_

        def load_gu(l, fb, cb, ncols):
            s_ = ring_ctr[0] % 3
            ring_ctr[0] += 1
            Wgu = ring[s_][:, 0:16 * ncols].rearrange("p (w k n) -> p w k n", w=2, k=8)
            WG_, WU_ = Wgu[:, 0], Wgu[:, 1]
            f0_ = fb * 1408 + cb
            wg3_ = w_g_d[l].rearrange("(k p) n -> p k n", p=128)
            wu3_ = w_u_d[l].rearrange("(k p) n -> p k n", p=128)
            S.dma("pool", lambda e: e.dma_start(out=WG_, in_=wg3_[:, :, f0_:f0_ + ncols]), ("wch", s_), writes=[("w", s_)])
            S.dma("pool", lambda e: e.dma_start(out=WU_, in_=wu3_[:, :, f0_:f0_ + ncols]), ("wch", s_), writes=[("w", s_)])
            return s_, WG_, WU_

        preloaded = {}
        S.dma("sp", lambda e: e.dma_start(out=cst_f[:, 0:128], in_=cst_d[:, 0:128]), "c_cstf", writes=["cst_f"])
        S.dma("sp", lambda e: e.dma_start(out=cst_f[:, 128:256], in_=cst_d[:, 384:512]), "c_cstf2", writes=["cst_f2"])
        S.dma("pool", lambda e: e.dma_start(out=cst_b[:], in_=cst_d[:, 0:384]), "c_cstb", writes=["cst_b"])
        S.dma("sp", lambda e: e.dma_start(out=vecs[:], in_=vecs_d[:]), "c_vecs", writes=["vecs"])
        S.dma("sp", lambda e: e.dma_start(out=tabf[0:32, :], in_=tab_d[:]), "c_tab", writes=["tabf"])
        S.op("dve", lambda e: e.memset(tabf[32:33, :], NEG), reads=[], writes=["tabf32"])
        S.op("dve", lambda e: e.tensor_copy(out=tab[0:32, :], in_=tabf[0:32, :]), reads=["tabf"], writes=["tab0"])
        S.op("dve", lambda e: e.tensor_copy(out=tab[32:33, :], in_=tabf[32:33, :]), reads=["tabf32"], writes=["tab1"])
        S.dma("pool", lambda e: e.dma_start(out=Emat[0:33, :], in_=emat_d[:]), "c_emat", writes=["Emat"])
        if n_layers >= 1 and stage >= 0.1:
            preloaded[(0, 0)] = load_pair_w(0, 0)
        for l in range(n_layers):
            b0 = l * NVL
            S.op("dve", lambda e, b0=b0: e.tensor_scalar_mul(out=vd[:, b0:b0 + 24], in0=vecs[:, b0:b0 + 24], scalar1=32.0),
                 reads=["vecs"], writes=[("vd", l, 0)])
            S.op("dve", lambda e, b0=b0: e.tensor_copy(out=vd[:, b0 + 24:b0 + 25], in_=vecs[:, b0 + 24:b0 + 25]),
                 reads=["vecs"], writes=[("vd", l, 1)])
            S.op("dve", lambda e, b0=b0: e.tensor_scalar_mul(out=vd[:, b0 + 25:b0 + 26], in0=vecs[:, b0 + 25:b0 + 26], scalar1=8.0),
                 reads=["vecs"], writes=[("vd", l, 2)])
            S.op("dve", lambda e, b0=b0: e.tensor_copy(out=vd[:, b0 + 26:b0 + 27], in_=vecs[:, b0 + 26:b0 + 27]),
                 reads=["vecs"], writes=[("vd", l, 3)])
            S.op("dve", lambda e, b0=b0: e.tensor_scalar_mul(out=vd[:, b0 + 27:b0 + 28], in0=vecs[:, b0 + 27:b0 + 28], scalar1=8.0),
                 reads=["vecs"], writes=[("vd", l, 4)])
            S.op("act", lambda e, b0=b0: e.activation(out=vd[:, b0 + 28:b0 + 36], in_=vecs[:, b0 + 28:b0 + 36], func=AF.Exp),
                 reads=["vecs"], writes=[("vd", l, 5)])
        VD = [("vd", l, i) for l in range(n_layers) for i in range(6)]

        def gen_strip(si_):
            d_, half_, c0_, width_ = STRIPS[si_]
            hs = 8 if si_ == 3 else 0
            for j0 in range(0, width_, 64):
                bk = bank()
                for jj in range(64):
                    jp = j0 + jj
                    e0 = si_ * 512 + (width_ - 1 - jp)
                    S.op("pe", lambda e, bk=bk, jj=jj, hs=hs, e0=e0: e.matmul(
                        psb[bk][:, jj * 8:(jj + 1) * 8], lhsT=Emat[0:33, e0:e0 + 128],
                        rhs=tab[0:33, hs:hs + 8], start=True, stop=True),
                        reads=["Emat", "tab0", "tab1"], writes=[("ps", bk)], inc=(jj == 63))
                S.op("act", lambda e, bk=bk, si_=si_, j0=j0: e.activation(
                    out=Ebuf[:, :, SBASE[si_] + j0: SBASE[si_] + j0 + 64],
                    in_=psb[bk][:, :].rearrange("p (j h) -> p h j", h=8), func=AF.Exp),
                    reads=[("ps", bk)], writes=["Ebuf"])

        sq_ctr = [0]
        scr_ctr = [0]

        def norm_chunk(l, gcol, n, mid=None):
            bk = bank()
            for half in range(2):
                sis = {}
                for c in range(4 * half, 4 * half + 4):
                    si = sq_ctr[0] % 4
                    sq_ctr[0] += 1
                    sis[c] = si
                    xin = xT[:, c, n * TC:(n + 1) * TC]
                    if c % 2 == 0:
                        S.op("act", lambda e, si=si, xin=xin: e.activation(out=sqb[si], in_=xin, func=AF.Square),
                             reads=[("x", c, n)], writes=[("sq", si)])
                    else:
                        S.op("pool", lambda e, si=si, xin=xin: e.tensor_tensor(out=sqb[si], in0=xin, in1=xin, op=ALU.mult),
                             reads=[("x", c, n)], writes=[("sq", si)])
                if half == 0 and mid is not None:
                    mid()
                for c in range(4 * half, 4 * half + 4):
                    si = sis[c]
                    S.op("pe", lambda e, bk=bk, si=si, c=c: e.matmul(psb[bk][:, :], lhsT=ones_b, rhs=sqb[si],
                                                                    start=(c == 0), stop=(c == NCH - 1)),
                         reads=[("sq", si), "cst_b"], writes=[("ps", bk)], inc=True)
            ri = scr_ctr[0] % 2
            scr_ctr[0] += 1
            S.op("act", lambda e, bk=bk, ri=ri: e.activation(out=scr[ri][:, :], in_=psb[bk][:, :], func=AF.Ln, bias=1024.0 * EPS),
                 reads=[("ps", bk)], writes=[("scr", ri)])
            S.op("act", lambda e, ri=ri: e.activation(out=scr[ri][:, :], in_=scr[ri][:, :], func=AF.Exp, scale=-0.5),
                 reads=[("scr", ri)], writes=[("scr", ri)])
            for c in range(NCH):
                eng = "dve"
                S.op(eng, lambda e, c=c, n=n, ri=ri: e.scalar_tensor_tensor(
                    out=hT[:, c, n * TC:(n + 1) * TC], in0=xT[:, c, n * TC:(n + 1) * TC],
                    scalar=vd[:, l * NVL + gcol + c: l * NVL + gcol + c + 1], in1=scr[ri][:, :],
                    op0=ALU.mult, op1=ALU.mult),
                    reads=[("x", c, n), ("scr", ri)] + VD, writes=[("h", n)])

        def rmsnorm_to_hT(l, gcol, chunks=range(NTC)):
            for n in chunks:
                norm_chunk(l, gcol, n)

        qk_pend = []

        def qk_finish(l, bk, si, gcolidx, dst, dkeys):
            b2 = bank()
            S.op("pe", lambda e, b2=b2, si=si: e.matmul(psb[b2][:, :], lhsT=bones_b, rhs=sqb[si], start=True, stop=True),
                 reads=[("sq", si), "cst_b"], writes=[("ps", b2)])
            ri = scr_ctr[0] % 2
            scr_ctr[0] += 1
            S.op("act", lambda e, b2=b2, ri=ri: e.activation(out=scr[ri][:, :], in_=psb[b2][:, :], func=AF.Ln, bias=64.0 * EPS),
                 reads=[("ps", b2)], writes=[("scr", ri)])
            S.op("act", lambda e, ri=ri: e.activation(out=scr[ri][:, :], in_=scr[ri][:, :], func=AF.Exp, scale=-0.5),
                 reads=[("scr", ri)], writes=[("scr", ri)])
            S.op("dve", lambda e, bk=bk, ri=ri, dst=dst: e.scalar_tensor_tensor(
                out=dst, in0=psb[bk][:, :], scalar=vd[:, l * NVL + gcolidx: l * NVL + gcolidx + 1],
                in1=scr[ri][:, :], op0=ALU.mult, op1=ALU.mult),
                reads=[("ps", bk), ("scr", ri)] + VD, writes=dkeys)

        def qk_flush():
            while qk_pend:
                qk_finish(*qk_pend.pop(0))

        def qk_proj(l, slot, wk, wcol, gcolidx, dst_fn, dst_keys_fn, flush=True):
            W = wview(slot, 8, wk)
            for n in range(NTC):
                bk = bank()
                for k in range(NCH):
                    S.op("pe", lambda e, k=k, n=n, bk=bk: e.matmul(
                        psb[bk][:, :], lhsT=W[:, k, wcol:wcol + 128], rhs=hT[:, k, n * TC:(n + 1) * TC],
                        start=(k == 0), stop=(k == NCH - 1)),
                        reads=[("w", slot), ("h", n)], writes=[("ps", bk)], inc=(k == NCH - 1))
                si = sq_ctr[0] % 4
                sq_ctr[0] += 1
                S.op("act", lambda e, si=si, bk=bk: e.activation(out=sqb[si], in_=psb[bk][:, :], func=AF.Square),
                     reads=[("ps", bk)], writes=[("sq", si)])
                qk_pend.append((l, bk, si, gcolidx, dst_fn(n), dst_keys_fn(n)))
                if len(qk_pend) > 1:
                    qk_finish(*qk_pend.pop(0))
            if flush:
                qk_flush()

        def attention(pairs_by_otile, evac_fn, look=5, hook=None):
            groups = []
            for oi, ot in enumerate(pairs_by_otile):
                flat = [(bi, pi, p) for bi, blk in enumerate(ot["blocks"]) for pi, p in enumerate(blk)]
                gsz = ot.get("gsz", 4)
                chunks = [flat[i:i + gsz] for i in range(0, len(flat), gsz)]
                for ci, ch in enumerate(chunks):
                    groups.append((oi, ch, ci == len(chunks) - 1))
            state = {}
            obank = {}
            mul_ctr = [0]

            def stage1(gi):
                oi, ch, _ = groups[gi]
                bk = bank()
                pslot = gi % 6
                for j, (bi, pi, p) in enumerate(ch):
                    M = p["M"]
                    cols = slice(j * 128, (j + 1) * 128)
                    S.op("pe", lambda e, bk=bk, p=p, M=M, cols=cols: e.matmul(
                        psb[bk][0:M, cols], lhsT=p["k"], rhs=p["q"], start=True, stop=True),
                        reads=p["reads"], writes=[("ps", bk)], inc=(j == len(ch) - 1))
                j0 = 0
                while j0 < len(ch):
                    j1 = j0
                    while j1 < len(ch) and ch[j1][2]["M"] == ch[j0][2]["M"]:
                        j1 += 1
                    M = ch[j0][2]["M"]
                    S.op("act", lambda e, bk=bk, pslot=pslot, j0=j0, j1=j1, M=M: e.activation(
                        out=pTs[pslot][0:M, j0 * 128:j1 * 128], in_=psb[bk][0:M, j0 * 128:j1 * 128], func=AF.Exp),
                        reads=[("ps", bk)], writes=[("pT", pslot)])
                    j0 = j1
                ecs = [p["ecol"] for (_, _, p) in ch]
                full = all(p["M"] == 128 for (_, _, p) in ch)
                rep = None
                if full and len(ch) == 4 and ecs[0] == ecs[2] and ecs[1] == ecs[3] and ecs[1] == ecs[0] + 128:
                    rep = (2, 256)
                elif full and len(ch) == 4 and ecs[0] == ecs[1] == ecs[2] == ecs[3]:
                    rep = (4, 128)
                if rep is not None:
                    a_, b_ = rep
                    eh = ch[0][2]["eh"]
                    ec = ecs[0]
                    pv = pTs[pslot][:, 0:512].rearrange("p (a b) -> p a b", a=a_)
                    ev = Ebuf[:, eh, ec:ec + b_].unsqueeze(1).broadcast_to([128, a_, b_])
                    S.op("dve", lambda e, pv=pv, ev=ev: e.tensor_tensor(out=pv, in0=pv, in1=ev, op=ALU.mult),
                         reads=[("pT", pslot), "Ebuf"], writes=[("pT", pslot)])
                j0 = 0 if rep is None else len(ch)
                while j0 < len(ch):
                    j1 = j0 + 1
                    while (j1 < len(ch) and ch[j1][2]["M"] == ch[j0][2]["M"]
                           and ch[j1][2]["ecol"] == ch[j1 - 1][2]["ecol"] + 128):
                        j1 += 1
                    M = ch[j0][2]["M"]
                    ec = ch[j0][2]["ecol"]
                    eh = ch[j0][2]["eh"]
                    eng = "dve"
                    mul_ctr[0] += 1
                    S.op(eng, lambda e, pslot=pslot, j0=j0, j1=j1, M=M, ec=ec, eh=eh: e.tensor_tensor(
                        out=pTs[pslot][0:M, j0 * 128:j1 * 128], in0=pTs[pslot][0:M, j0 * 128:j1 * 128],
                        in1=Ebuf[0:M, eh, ec:ec + (j1 - j0) * 128], op=ALU.mult),
                        reads=[("pT", pslot), "Ebuf"], writes=[("pT", pslot)])
                    j0 = j1
                state[gi] = pslot

            def stage2(gi):
                oi, ch, last = groups[gi]
                ot = pairs_by_otile[oi]
                first_in_bank = oi not in obank
                if first_in_bank:
                    obank[oi] = bank()
                ob = obank[oi]
                pslot = state[gi]
                order = sorted(range(len(ch)), key=lambda j: (ch[j][1], ch[j][0]))
                for oi_, j in enumerate(order):
                    bi, pi, p = ch[j]
                    M = p["M"]
                    nb = len(ot["blocks"][bi])
                    S.op("pe", lambda e, ob=ob, p=p, M=M, bi=bi, pi=pi, nb=nb, pslot=pslot, j=j, st=(first_in_bank and oi_ == 0): e.matmul(
                        psb[ob][:, bi * 128:(bi + 1) * 128], lhsT=p["v"], rhs=pTs[pslot][0:M, j * 128:(j + 1) * 128],
                        start=st, stop=(pi == nb - 1), skip_group_check=True),
                        reads=[("pT", pslot)] + p["vreads"], writes=[("ps", ob)], inc=(oi_ == len(ch) - 1))
                if last:
                    (ot.get("evac") or evac_fn)(oi, ob, ot)
                    if ot.get("post") is not None:
                        ot["post"]()

            G = len(groups)
            for i in range(G + look):
                if i < G:
                    stage1(i)
                if hook is not None and i == min(2, G - 1):
                    hook()
                    hook = None
                if i - look >= 0:
                    stage2(i - look)

        out_done = set()

        def out_tile(t):
            so = t % 2
            b2 = [bank(), bank()]
            for c in range(NCH):
                bk = b2[c // 4]
                S.op("pe", lambda e, bk=bk, c=c, t=t: e.transpose(
                    out=psb[bk][:, (c % 4) * 128:(c % 4 + 1) * 128], in_=xT[:, c, t * 128:(t + 1) * 128], identity=ident_f),
                    reads=[("x", c, t // 4), "cst_f"], writes=[("ps", bk)], inc=(c % 4 == 3))
            S.op("act", lambda e, bk=b2[0], so=so: e.copy(out=otile[:, so, 0:512], in_=psb[bk][:, :]),
                 reads=[("ps", b2[0])], writes=[("ot", so)])
            S.op("dve", lambda e, bk=b2[1], so=so: e.tensor_copy(out=otile[:, so, 512:1024], in_=psb[bk][:, :]),
                 reads=[("ps", b2[1])], writes=[("ot", so)])
            S.dma("sp", lambda e, t=t, so=so: e.dma_start(out=y_d[t * 128:(t + 1) * 128, :], in_=otile[:, so, :]),
                  ("och", so), reads=[("ot", so)], writes=[("yout", so)])

        xtmpB = Rv(24 * KB, 16 * KB, F32).rearrange("p (s f) -> p s f", s=4)
        norm0_done = (n_layers >= 1 and stage >= 0.1)
        for n in range(NTC):
            xt_ = xtmp if n % 2 == 0 else xtmpB
            for i in range(4):
                t = 4 * n + i
                S.dma("sp", lambda e, i=i, t=t, xt_=xt_: e.dma_start(out=xt_[:, i, :], in_=x_d[t * 128:(t + 1) * 128, :]),
                      ("xl", (n % 2) * 4 + i), writes=[("xtmp", (n % 2) * 4 + i)])
            for c in range(NCH):
                bk = bank()
                for i in range(4):
                    S.op("pe", lambda e, bk=bk, i=i, c=c, xt_=xt_: e.transpose(
                        out=psb[bk][:, i * 128:(i + 1) * 128], in_=xt_[:, i, c * 128:(c + 1) * 128], identity=ident_f),
                        reads=[("xtmp", (n % 2) * 4 + i), "cst_f"], writes=[("ps", bk)], inc=(i == 3))
                if c % 2 == 0:
                    S.op("act", lambda e, bk=bk, c=c, n=n: e.copy(out=xT[:, c, n * TC:(n + 1) * TC], in_=psb[bk][:, :]),
                         reads=[("ps", bk)], writes=[("x", c, n)])
                else:
                    S.op("dve", lambda e, bk=bk, c=c, n=n: e.tensor_copy(out=xT[:, c, n * TC:(n + 1) * TC], in_=psb[bk][:, :]),
                         reads=[("ps", bk)], writes=[("x", c, n)])
            if norm0_done:
                norm_chunk(0, 0, n, mid=(lambda n=n: gen_strip(n)))
        if not norm0_done:
            for si_ in range(len(STRIPS)):
                gen_strip(si_)

        for l in range(n_layers):
            vb0 = l * NVL
            if stage < 0.1:
                break
            if l == 0 and not norm0_done:
                for n in range(NTC):
                    norm_chunk(l, 0, n, mid=(lambda n=n: gen_strip(n)))
            if stage < 1 and stage <= 0.2:
                if dbg == 3:
                    S.dma("sp", lambda e: e.dma_start(out=dbg_d[:, :], in_=hT[:, 0:4, :].rearrange("p c t -> p (c t)")), "dbgc", reads=[("h", n) for n in range(4)], writes=["dbgout"])
                break

            def a_tiles():
                tiles = []
                idx = {}
                for j in range(17):
                    s0, s1 = max(128 * j - 64, 0), min(128 * j + 64, T)
                    idx[(0, 0, j)] = len(tiles)
                    tiles.append((s0, 1, s1 - s0, 64 if j == 0 else 0))
                for r in range(4):
                    for j in range(5):
                        s0, s1 = max(128 * j - 64, 0), min(128 * j + 64, 512)
                        idx[(1, r, j)] = len(tiles)
                        tiles.append((r + 4 * s0, 4, s1 - s0, 64 if j == 0 else 0))
                for r in range(16):
                    idx[(2, r, 0)] = len(tiles)
                    tiles.append((r, 16, 128, 0))
                return tiles, idx

            tiles, tidx = a_tiles()
            pending_fin = [None]

            def tok_ap(buf2d, rows, tok0, step, cnt):
                if step == 1:
                    return buf2d[rows, tok0:tok0 + cnt]
                v = buf2d.rearrange("p (s r) -> p r s", r=step)
                return v[rows, tok0 % step, tok0 // step: tok0 // step + cnt]

            for c in range(4):
                src3 = w_in_d[l].rearrange("(k p) n -> p k n", p=128)
                if (l, c) in preloaded:
                    s = preloaded.pop((l, c))
                else:
                    s = load_pair_w(l, c)
                W = wview(s, 8, 384)
                akeys = ["qa", "ka", "vt", "acc"]
                if c == 0:
                    S.alias(K_A + K_YA, ALLR)
                qk_proj(l, s, 384, 0, 24, lambda n: qa_v[:, n * TC:(n + 1) * TC], lambda n: ["qa"], flush=False)
                qk_proj(l, s, 384, 128, 25, lambda n: ka_v[:, n * TC:(n + 1) * TC], lambda n: ["ka"], flush=True)
                if pending_fin[0] is not None:
                    pending_fin[0]()
                    pending_fin[0] = None
                if stage <= 0.4:
                    if dbg == 2:
                        S.dma("sp", lambda e: e.dma_start(out=dbg_d[:, 0:T], in_=qa_v), "dbgc", reads=["qa"], writes=["dbgout"])
                        S.dma("sp", lambda e: e.dma_start(out=dbg_d[:, T:2 * T], in_=ka_v), "dbgc2", reads=["ka"], writes=["dbgout2"])
                    break
                S.op("dve", lambda e: e.memset(vt[:, :, 64:128], 1.0), reads=[], writes=["vt"])
                vT = sq_all
                for n in range(NTC):
                    bk = bank()
                    for k in range(NCH):
                        S.op("pe", lambda e, k=k, n=n, bk=bk, W=W: e.matmul(
                            psb[bk][:, :], lhsT=W[:, k, 256:384], rhs=hT[:, k, n * TC:(n + 1) * TC],
                            start=(k == 0), stop=(k == NCH - 1)),
                            reads=[("w", s), ("h", n)], writes=[("ps", bk)], inc=(k == NCH - 1))
                    S.op("act", lambda e, bk=bk, n=n: e.copy(out=vT[:, n * TC:(n + 1) * TC], in_=psb[bk][:, :]),
                         reads=[("ps", bk)], writes=[("sq", n)])
                for t0 in range(0, len(tiles), 4):
                    grp = tiles[t0:t0 + 4]
                    bk = bank()
                    for gi, (tok0, step, M, i0) in enumerate(grp):
                        lhs = tok_ap(vT, slice(0, 128), tok0, step, M)
                        S.op("pe", lambda e, bk=bk, gi=gi, M=M, lhs=lhs: e.matmul(
                            psb[bk][0:M, gi * 128:(gi + 1) * 128], lhsT=lhs, rhs=ident_b,
                            start=True, stop=True),
                            reads=["cst_b"] + [("sq", n) for n in range(NTC)], writes=[("ps", bk)],
                            inc=(gi == len(grp) - 1))
                    ng = len(grp)
                    vo = vt[:, t0:t0 + ng, :].rearrange("p t (a b) -> p t a b", a=3)[:, :, 0:3:2, :]
                    vi = psb[bk][:, 0:ng * 128].rearrange("p (t a b) -> p t a b", a=2, b=64)
                    if (t0 // 4) % 2 == 0:
                        S.op("act", lambda e, vo=vo, vi=vi: e.copy(out=vo, in_=vi), reads=[("ps", bk)], writes=["vt"])
                    else:
                        S.op("dve", lambda e, vo=vo, vi=vi: e.tensor_copy(out=vo, in_=vi), reads=[("ps", bk)], writes=["vt"])
                if stage <= 0.6:
                    break
                pair_otiles = []
                for u in range(2):
                    h = 2 * c + u
                    rows = slice(u * 64, (u + 1) * 64)
                    vcols = slice(0, 128) if u == 0 else slice(64, 192)
                    orow = slice(0, 64) if u == 0 else slice(64, 128)
                    drow = slice(64, 128) if u == 0 else slice(0, 64)

                    def mkpair(pat, r, j, qtok0, step, kind):
                        tok0, st, M, i0 = tiles[tidx[(pat, r, j)]]
                        ti = tidx[(pat, r, j)]
                        if pat == 2:
                            shift = 0
                        elif kind == "t1":
                            shift = 64
                        else:
                            shift = 128 if i0 == 64 else 192
                        return dict(k=tok_ap(ka_v, rows, tok0, st, M), q=tok_ap(qa_v, rows, qtok0, step, 128), M=M,
                                    ecol=SBASE[pat] + shift, eh=h, v=vt[0:M, ti, vcols],
                                    reads=["qa", "ka"], vreads=["vt"])

                    otiles = []
                    for Q in range(4):
                        blocks = []
                        for b in range(4 * Q, 4 * Q + 4):
                            blocks.append([mkpair(0, 0, b + 1, 128 * b, 1, "t1"), mkpair(0, 0, b, 128 * b, 1, "t0")])
                        otiles.append(dict(blocks=blocks, kind=("p1", Q)))
                    for r in range(4):
                        blocks = []
                        for b in range(4):
                            blocks.append([mkpair(1, r, b + 1, r + 4 * 128 * b, 4, "t1"), mkpair(1, r, b, r + 4 * 128 * b, 4, "t0")])
                        otiles.append(dict(blocks=blocks, kind=("p2", r)))
                    for r0 in range(0, 16, 4):
                        blocks = [[mkpair(2, r, 0, r, 16, "t")] for r in range(r0, r0 + 4)]
                        otiles.append(dict(blocks=blocks, kind=("p3", r0)))

                    def evacA(oi, ob, ot):
                        kind, v = ot["kind"]
                        if kind == "p1":
                            S.op("dve", lambda e, ob=ob, v=v: e.tensor_copy(out=acc[:, v * 512:(v + 1) * 512], in_=psb[ob][:, :]),
                                 reads=[("ps", ob)], writes=["acc"])
                        elif kind == "p2":
                            av = acc.rearrange("p (s r) -> p r s", r=4)[:, v, :]
                            S.op("dve", lambda e, ob=ob, av=av: e.tensor_tensor(out=av, in0=psb[ob][:, :], in1=av, op=ALU.add),
                                 reads=[("ps", ob), "acc"], writes=["acc"])
                        else:
                            av = acc.rearrange("p (s r) -> p r s", r=16)[:, v:v + 4, :]
                            S.op("dve", lambda e, ob=ob, av=av: e.tensor_tensor(
                                out=av, in0=psb[ob][:, :].rearrange("p (r s) -> p r s", r=4), in1=av, op=ALU.add),
                                reads=[("ps", ob), "acc"], writes=["acc"])


                    def make_fin(c=c, orow=orow, drow=drow):
                        def fin():
                            for n in range(NTC):
                                bk = bank()
                                S.op("act", lambda e, bk=bk, n=n: e.activation(out=psb[bk][drow, :], in_=acc[drow, n * TC:(n + 1) * TC], func=AF.Ln),
                                     reads=["acc"], writes=[("ps", bk)])
                                S.op("act", lambda e, bk=bk: e.activation(out=psb[bk][drow, :], in_=psb[bk][drow, :], func=AF.Exp, scale=-1.0),
                                     reads=[("ps", bk)], writes=[("ps", bk)])
                                S.op("dve", lambda e, bk=bk, n=n: e.tensor_tensor(
                                    out=yaT[orow, c, n * TC:(n + 1) * TC], in0=acc[orow, n * TC:(n + 1) * TC], in1=psb[bk][drow, :], op=ALU.mult),
                                    reads=["acc", ("ps", bk)], writes=[("ya", c)])
                        return fin
                    for ot_ in otiles:
                        ot_["evac"] = evacA
                    if u == 0:
                        otiles[-1]["post"] = make_fin()
                    else:
                        pending_fin[0] = make_fin()
                    pair_otiles.extend(otiles)
                attention(pair_otiles, None)
            if pending_fin[0] is not None:
                pending_fin[0]()
                pending_fin[0] = None
            if stage <= 0.6:
                break
            if stage < 2:
                if dbg == 1:
                    S.dma("sp", lambda e: e.dma_start(out=dbg_d[:, :], in_=Rv(0, 16 * KB, BF16)), "dbgc", reads=K_YA, writes=["dbgout"])
                elif dbg == 2:
                    S.dma("sp", lambda e: e.dma_start(out=dbg_d[:, 0:T], in_=qa_v), "dbgc", reads=["qa"], writes=["dbgout"])
                    S.dma("sp", lambda e: e.dma_start(out=dbg_d[:, T:2 * T], in_=ka_v), "dbgc2", reads=["ka"], writes=["dbgout2"])
                elif dbg == 3:
                    S.dma("sp", lambda e: e.dma_start(out=dbg_d[:, :], in_=hT[:, 0:4, :].rearrange("p c t -> p (c t)")), "dbgc", reads=[("h", n) for n in range(4)], writes=["dbgout"])
                break

            s1 = ring_ctr[0] % 3
            ring_ctr[0] += 1
            Wq = wview(s1, 8, 512)
            src3 = w_in_d[l].rearrange("(k p) n -> p k n", p=128)
            Wq5 = ring[s1][:, 0:4096].rearrange("p (k c u e) -> p k c u e", k=8, c=4, u=2)
            for u in range(2):
                for cq in range(4):
                    S.dma("pool", lambda e, u=u, cq=cq, Wq5=Wq5, src3=src3: e.dma_start(
                        out=Wq5[:, :, cq, u, :],
                        in_=src3[:, :, 1536 + u * 256 + cq * 64:1536 + u * 256 + (cq + 1) * 64]),
                        ("wch", s1), writes=[("w", s1)])
            s2 = ring_ctr[0] % 3
            ring_ctr[0] += 1
            Wkv = wview(s2, 8, 256)
            S.dma("pool", lambda e: e.dma_start(out=Wkv[:, :, :], in_=src3[:, :, 2048:2304]), ("wch", s2), writes=[("w", s2)])
            S.alias(K_B, ALLR)
            for cc in range(4):
                qk_proj(l, s1, 512, cc * 128, 26, lambda n, cc=cc: qbT[:, cc, n * TC:(n + 1) * TC],
                        lambda n, cc=cc: [("qb", cc, n)], flush=False)
            qk_proj(l, s2, 256, 0, 27, lambda n: kbT[:, n * TC:(n + 1) * TC], lambda n: ["kb"])
            S.op("dve", lambda e: e.memset(vbt[:, :, 64:128], 1.0), reads=[], writes=["vbt"])
            for t0 in range(0, 16, 4):
                bk = bank()
                for gi in range(4):
                    t = t0 + gi
                    for k in range(NCH):
                        S.op("pe", lambda e, bk=bk, gi=gi, t=t, k=k: e.matmul(
                            psb[bk][:, gi * 128:(gi + 1) * 128], lhsT=hT[:, k, t * 128:(t + 1) * 128], rhs=Wkv[:, k, 128:256],
                            start=(k == 0), stop=(k == NCH - 1)),
                            reads=[("w", s2), ("h", t // 4)], writes=[("ps", bk)], inc=(k == NCH - 1 and gi == 3))
                S.op("act", lambda e, bk=bk, t0=t0: e.copy(
                    out=vbt[:, t0:t0 + 4, 0:64], in_=psb[bk][:, :].rearrange("p (t f) -> p t f", f=128)[:, :, 0:64]),
                    reads=[("ps", bk)], writes=["vbt"])
                S.op("dve", lambda e, bk=bk, t0=t0: e.tensor_copy(
                    out=vbt[:, t0:t0 + 4, 128:192], in_=psb[bk][:, :].rearrange("p (t f) -> p t f", f=128)[:, :, 64:128]),
                    reads=[("ps", bk)], writes=["vbt"])

            b_otiles = []
            for cc in range(4):
                for u in range(2):
                    hq = u * 4 + cc
                    rows = slice(u * 64, (u + 1) * 64)
                    vcols = slice(0, 128) if u == 0 else slice(64, 192)
                    orow = slice(0, 64) if u == 0 else slice(64, 128)
                    drow = slice(64, 128) if u == 0 else slice(0, 64)
                    otiles = []
                    for Q in range(4):
                        blocks = []
                        for b in range(4 * Q, 4 * Q + 4):
                            blk = []
                            for dj, shift in ((1, 0), (0, 128), (-1, 256)):
                                kt = b + dj
                                if kt < 0 or kt > 15:
                                    continue
                                blk.append(dict(k=kbT[rows, kt * 128:(kt + 1) * 128], q=qbT[rows, cc, b * 128:(b + 1) * 128], M=128,
                                                ecol=SBASE[3] + shift, eh=hq, v=vbt[:, kt, vcols],
                                                reads=[("qb", cc, Q), "kb"], vreads=["vbt"]))
                            blocks.append(blk)
                        otiles.append(dict(blocks=blocks, kind=Q, gsz=4))

                    def evacB(oi, ob, ot, cc=cc, hq=hq, orow=orow, drow=drow):
                        Q = ot["kind"]
                        ri = scr_ctr[0] % 2
                        scr_ctr[0] += 1
                        S.op("act", lambda e, ob=ob, ri=ri: e.activation(
                            out=scr[ri][drow, :], in_=psb[ob][drow, :], func=AF.Ln, bias=vd[drow, vb0 + 28 + hq: vb0 + 29 + hq]),
                            reads=[("ps", ob)] + VD, writes=[("scr", ri)])
                        S.op("act", lambda e, ri=ri: e.activation(out=scr[ri][drow, :], in_=scr[ri][drow, :], func=AF.Exp, scale=-1.0),
                            reads=[("scr", ri)], writes=[("scr", ri)])
                        S.op("dve", lambda e, ob=ob, ri=ri: e.tensor_tensor(
                            out=qbT[orow, cc, Q * 512:(Q + 1) * 512], in0=psb[ob][orow, :], in1=scr[ri][drow, :], op=ALU.mult),
                            reads=[("ps", ob), ("scr", ri)], writes=[("qb", cc, Q)])

                    for ot_ in otiles:
                        ot_["evac"] = evacB
                    b_otiles.extend(otiles)
            attention(b_otiles, None)
            if stage < 3:
                if dbg == 4:
                    S.dma("sp", lambda e: e.dma_start(out=dbg_d[:, :], in_=Rv(16 * KB, 16 * KB, BF16)), "dbgc", reads=K_B, writes=["dbgout"])
                break

            S.alias(K_G, ALLR)
            ybT = qbT
            def load_gates(mb, sub):
                s_ = ring_ctr[0] % 3
                ring_ctr[0] += 1
                Wg2 = ring[s_][:, 0:4096].rearrange("p (w k n) -> p w k n", w=2, k=8)
                c0g = mb * 512 + sub * 256
                S.dma("pool", lambda e: e.dma_start(out=Wg2[:, 0], in_=src3[:, :, 2304 + c0g:2304 + c0g + 256]),
                      ("wch", s_), writes=[("w", s_)])
                S.dma("pool", lambda e: e.dma_start(out=Wg2[:, 1], in_=src3[:, :, 3328 + c0g:3328 + c0g + 256]),
                      ("wch", s_), writes=[("w", s_)])
                return s_, Wg2[:, 0], Wg2[:, 1]

            for mb in range(2):
                g_first = load_gates(mb, 0)
                sA = ring_ctr[0] % 3
                ring_ctr[0] += 1
                WA = wview(sA, 8, 512)
                S.dma("pool", lambda e, WA=WA, mb=mb: e.dma_start(
                    out=WA[:, 0:4, :], in_=w_a_d[l].rearrange("(k p) n -> p k n", p=128)[:, :, mb * 512:(mb + 1) * 512]),
                    ("wch", sA), writes=[("w", sA)])
                for u in range(2):
                    S.dma("pool", lambda e, WA=WA, mb=mb, u=u: e.dma_start(
                        out=WA[u * 64:(u + 1) * 64, 4:8, :],
                        in_=w_b_d[l][u * 256:(u + 1) * 256, mb * 512:(mb + 1) * 512].rearrange("(c e) n -> e c n", e=64)),
                        ("wch", sA), writes=[("w", sA)])
                for mi in range(4):
                    if mi == 0:
                        sGa, WGa, WGb = g_first
                        sGb = sGa
                    elif mi == 2:
                        sGa, WGa, WGb = load_gates(mb, 1)
                        sGb = sGa
                    mg_ = mi % 2
                    for n in range(NTC):
                        nsl = slice(n * TC, (n + 1) * TC)
                        bA, bB, bGa, bGb = bank(), bank(), bank(), bank()
                        for k in range(4):
                            S.op("pe", lambda e, k=k, bA=bA, mi=mi, nsl=nsl, WA=WA: e.matmul(
                                psb[bA][:, :], lhsT=WA[:, k, mi * 128:(mi + 1) * 128], rhs=yaT[:, k, nsl], start=(k == 0), stop=(k == 3)),
                                reads=[("w", sA), ("ya", k)], writes=[("ps", bA)], inc=(k == 3))
                        for k in range(4):
                            S.op("pe", lambda e, k=k, bB=bB, mi=mi, nsl=nsl, WA=WA: e.matmul(
                                psb[bB][:, :], lhsT=WA[:, 4 + k, mi * 128:(mi + 1) * 128], rhs=ybT[:, k, nsl], start=(k == 0), stop=(k == 3)),
                                reads=[("w", sA), ("qb", k, n)], writes=[("ps", bB)], inc=(k == 3))
                        for k in range(NCH):
                            S.op("pe", lambda e, k=k, bGa=bGa, mg_=mg_, nsl=nsl, WGa=WGa: e.matmul(
                                psb[bGa][:, :], lhsT=WGa[:, k, mg_ * 128:(mg_ + 1) * 128], rhs=hT[:, k, nsl], start=(k == 0), stop=(k == 7)),
                                reads=[("w", sGa), ("h", n)], writes=[("ps", bGa)], inc=(k == 7))
                        for k in range(NCH):
                            S.op("pe", lambda e, k=k, bGb=bGb, mg_=mg_, nsl=nsl, WGb=WGb: e.matmul(
                                psb[bGb][:, :], lhsT=WGb[:, k, mg_ * 128:(mg_ + 1) * 128], rhs=hT[:, k, nsl], start=(k == 0), stop=(k == 7)),
                                reads=[("w", sGb), ("h", n)], writes=[("ps", bGb)], inc=(k == 7))
                        r1 = scr_ctr[0] % 2
                        r2 = (scr_ctr[0] + 1) % 2
                        scr_ctr[0] += 2
                        S.op("act", lambda e, bGa=bGa, r1=r1: e.activation(out=scr[r1][:, :], in_=psb[bGa][:, :], func=AF.Sigmoid),
                             reads=[("ps", bGa)], writes=[("scr", r1)])
                        S.op("act", lambda e, bGb=bGb, r2=r2: e.activation(out=scr[r2][:, :], in_=psb[bGb][:, :], func=AF.Sigmoid),
                             reads=[("ps", bGb)], writes=[("scr", r2)])
                        S.op("dve", lambda e, bA=bA, r1=r1: e.tensor_tensor(out=scr[r1][:, :], in0=psb[bA][:, :], in1=scr[r1][:, :], op=ALU.mult),
                             reads=[("ps", bA), ("scr", r1)], writes=[("scr", r1)])
                        S.op("dve", lambda e, bB=bB, r2=r2: e.tensor_tensor(out=scr[r2][:, :], in0=psb[bB][:, :], in1=scr[r2][:, :], op=ALU.mult),
                             reads=[("ps", bB), ("scr", r2)], writes=[("scr", r2)])
                        S.op("dve", lambda e, r1=r1, r2=r2, mi=mi, nsl=nsl: e.tensor_tensor(
                            out=merged[:, mi, nsl], in0=scr[r1][:, :], in1=scr[r2][:, :], op=ALU.add),
                            reads=[("scr", r1), ("scr", r2)], writes=[("mg", n)])
                sO = ring_ctr[0] % 3
                ring_ctr[0] += 1
                WO = wview(sO, 4, 1024)
                S.dma("pool", lambda e, WO=WO, mb=mb: e.dma_start(
                    out=WO[:, :, :], in_=w_out_d[l][mb * 512:(mb + 1) * 512, :].rearrange("(k p) n -> p k n", p=128)),
                    ("wch", sO), writes=[("w", sO)])
                if mb == 1 and stage >= 4:
                    preloaded[("gu", l, 0, 0)] = load_gu(l, 0, 0, 256)
                    preloaded[("gu", l, 0, 256)] = load_gu(l, 0, 256, 256)
                for n in range(NTC):
                    for m2 in range(NCH):
                        bk = bank()
                        for k in range(4):
                            S.op("pe", lambda e, k=k, n=n, bk=bk, m2=m2, WO=WO: e.matmul(
                                psb[bk][:, :], lhsT=WO[:, k, m2 * 128:(m2 + 1) * 128], rhs=merged[:, k, n * TC:(n + 1) * TC],
                                start=(k == 0), stop=(k == 3)),
                                reads=[("w", sO), ("mg", n)], writes=[("ps", bk)], inc=(k == 3))
                        S.op("dve", lambda e, bk=bk, m2=m2, n=n: e.tensor_tensor(
                            out=xT[:, m2, n * TC:(n + 1) * TC], in0=psb[bk][:, :], in1=xT[:, m2, n * TC:(n + 1) * TC], op=ALU.add),
                            reads=[("ps", bk), ("x", m2, n)], writes=[("x", m2, n)])
                    if mb == 1 and n >= 1 and stage >= 4:
                        norm_chunk(l, 8, n - 1)
                if mb == 1 and stage >= 4:
                    norm_chunk(l, 8, NTC - 1)
            if stage < 4:
                break

            S.alias(K_F, ALLR)
            wg3 = w_g_d[l].rearrange("(k p) n -> p k n", p=128)
            wu3 = w_u_d[l].rearrange("(k p) n -> p k n", p=128)
            for fb in range(2):
                f0 = fb * 1408
                fi = 0
                for cb, ncols in ((0, 256), (256, 256), (512, 256), (768, 256), (1024, 256), (1280, 128)):
                    if ("gu", l, fb, cb) in preloaded:
                        sG, WG, WU = preloaded.pop(("gu", l, fb, cb))
                    else:
                        sG, WG, WU = load_gu(l, fb, cb, ncols)
                    sU = sG
                    for mi in range(ncols // 128):
                        for half in range(2):
                            ns = (2 * half, 2 * half + 1)
                            bG = {n: bank() for n in ns}
                            bU = {n: bank() for n in ns}
                            for k in range(NCH):
                                for n in ns:
                                    S.op("pe", lambda e, k=k, n=n, bk=bG[n], mi=mi, WG=WG: e.matmul(
                                        psb[bk][:, :], lhsT=WG[:, k, mi * 128:(mi + 1) * 128], rhs=hT[:, k, n * TC:(n + 1) * TC],
                                        start=(k == 0), stop=(k == 7)),
                                        reads=[("w", sG), ("h", n)], writes=[("ps", bG[n])], inc=(k == 7))
                            for k in range(NCH):
                                for n in ns:
                                    S.op("pe", lambda e, k=k, n=n, bk=bU[n], mi=mi, WU=WU: e.matmul(
                                        psb[bk][:, :], lhsT=WU[:, k, mi * 128:(mi + 1) * 128], rhs=hT[:, k, n * TC:(n + 1) * TC],
                                        start=(k == 0), stop=(k == 7)),
                                        reads=[("w", sU), ("h", n)], writes=[("ps", bU[n])], inc=(k == 7))
                            for n in ns:
                                ri = scr_ctr[0] % 2
                                scr_ctr[0] += 1
                                S.op("act", lambda e, bk=bG[n], ri=ri: e.activation(out=scr[ri][:, :], in_=psb[bk][:, :], func=AF.Silu),
                                     reads=[("ps", bG[n])], writes=[("scr", ri)])
                                S.op("dve", lambda e, bk=bU[n], ri=ri, fi=fi, n=n: e.tensor_tensor(
                                    out=hid[:, fi, n * TC:(n + 1) * TC], in0=psb[bk][:, :], in1=scr[ri][:, :], op=ALU.mult),
                                    reads=[("ps", bU[n]), ("scr", ri)], writes=[("hid", n)])
                        fi += 1
                for mq in range(4):
                    sD = ring_ctr[0] % 3
                    ring_ctr[0] += 1
                    WD = wview(sD, 11, 256)
                    S.dma("pool", lambda e, WD=WD, f0=f0, mq=mq: e.dma_start(
                        out=WD[:, :, :], in_=w_d_d[l][f0:f0 + 1408, mq * 256:(mq + 1) * 256].rearrange("(k p) n -> p k n", p=128)),
                        ("wch", sD), writes=[("w", sD)])
                    for ml in range(2):
                        m2 = mq * 2 + ml
                        bks = [bank() for _ in range(NTC)]
                        for k in range(11):
                            for n in range(NTC):
                                S.op("pe", lambda e, k=k, n=n, bk=bks[n], ml=ml, WD=WD: e.matmul(
                                    psb[bk][:, :], lhsT=WD[:, k, ml * 128:(ml + 1) * 128], rhs=hid[:, k, n * TC:(n + 1) * TC],
                                    start=(k == 0), stop=(k == 10)),
                                    reads=[("w", sD), ("hid", n)], writes=[("ps", bks[n])], inc=(k == 10))
                        for n in range(NTC):
                            S.op("dve", lambda e, bk=bks[n], m2=m2, n=n: e.tensor_tensor(
                                out=xT[:, m2, n * TC:(n + 1) * TC], in0=psb[bk][:, :], in1=xT[:, m2, n * TC:(n + 1) * TC], op=ALU.add),
                                reads=[("ps", bks[n]), ("x", m2, n)], writes=[("x", m2, n)])
            if stage < 5:
                break

            sP = ring_ctr[0] % 3
            ring_ctr[0] += 1
            WP = wview(sP, 2, 1024)
            S.dma("pool", lambda e, WP=WP: e.dma_start(out=WP[:, :, :], in_=w_pp_d[l].rearrange("(k p) n -> p k n", p=128)),
                  ("wch", sP), writes=[("w", sP)])
            wpg3 = w_pg_d[l].rearrange("(k p) n -> p k n", p=128)
            WGs = []
            for mb in range(2):
                sG = ring_ctr[0] % 3
                ring_ctr[0] += 1
                WG = wview(sG, 8, 512)
                S.dma("pool", lambda e, WG=WG, mb=mb: e.dma_start(out=WG[:, :, :], in_=wpg3[:, :, mb * 512:(mb + 1) * 512]),
                      ("wch", sG), writes=[("w", sG)])
                WGs.append((sG, WG))
            rmsnorm_to_hT(l, 16)
            S.alias(K_P, ALLR)
            for half in range(2):
                S.dma("sp", lambda e, half=half: e.dma_start(
                    out=ptok[:, half * 8:(half + 1) * 8, :],
                    in_=p_d[l][half * 1024:(half + 1) * 1024, :].rearrange("(t p) f -> p t f", p=128)),
                    ("pl", half), writes=[("ptok", half)])
            for fc in range(2):
                for n in range(NTC):
                    bk = bank()
                    for i in range(4):
                        t = 4 * n + i
                        S.op("pe", lambda e, bk=bk, i=i, t=t, fc=fc: e.transpose(
                            out=psb[bk][:, i * 128:(i + 1) * 128], in_=ptok[:, t, fc * 128:(fc + 1) * 128], identity=ident_f),
                            reads=[("ptok", t // 8), "cst_f"], writes=[("ps", bk)], inc=(i == 3))
                    S.op("act", lambda e, bk=bk, fc=fc, n=n: e.copy(out=ppT[:, fc, n * TC:(n + 1) * TC], in_=psb[bk][:, :]),
                         reads=[("ps", bk)], writes=[("ppT", n)])
            last_layer = (l == n_layers - 1)
            if last_layer:
                S.alias(K_O, ALLR)

            def after_chunk(n):
                if not last_layer:
                    norm_chunk(l + 1, 0, n)
                else:
                    for t in range(4 * n, 4 * n + 4):
                        out_tile(t)
                    out_done.add(n)

            for n in range(NTC):
                for m2 in range(NCH):
                    mb, mi = divmod(m2, 4)
                    sG, WG = WGs[mb]
                    bG = bank()
                    for k in range(NCH):
                        S.op("pe", lambda e, k=k, n=n, bk=bG, mi=mi, WG=WG: e.matmul(
                            psb[bk][:, :], lhsT=WG[:, k, mi * 128:(mi + 1) * 128], rhs=hT[:, k, n * TC:(n + 1) * TC],
                            start=(k == 0), stop=(k == 7)),
                            reads=[("w", sG), ("h", n)], writes=[("ps", bG)], inc=(k == 7))
                    bP = bank()
                    for k in range(2):
                        S.op("pe", lambda e, k=k, n=n, bk=bP, m2=m2, WP=WP: e.matmul(
                            psb[bk][:, :], lhsT=WP[:, k, m2 * 128:(m2 + 1) * 128], rhs=ppT[:, k, n * TC:(n + 1) * TC],
                            start=(k == 0), stop=(k == 1)),
                            reads=[("w", sP), ("ppT", n)], writes=[("ps", bP)], inc=(k == 1))
                    ri = scr_ctr[0] % 2
                    scr_ctr[0] += 1
                    S.op("act", lambda e, bk=bG, ri=ri: e.activation(out=scr[ri][:, :], in_=psb[bk][:, :], func=AF.Sigmoid),
                         reads=[("ps", bG)], writes=[("scr", ri)])
                    S.op("dve", lambda e, bk=bP, ri=ri: e.tensor_tensor(out=scr[ri][:, :], in0=psb[bk][:, :], in1=scr[ri][:, :], op=ALU.mult),
                         reads=[("ps", bP), ("scr", ri)], writes=[("scr", ri)])
                    S.op("dve", lambda e, ri=ri, m2=m2, n=n: e.tensor_tensor(
                        out=xT[:, m2, n * TC:(n + 1) * TC], in0=scr[ri][:, :], in1=xT[:, m2, n * TC:(n + 1) * TC], op=ALU.add),
                        reads=[("scr", ri), ("x", m2, n)], writes=[("x", m2, n)])
                if n >= 1:
                    after_chunk(n - 1)
            after_chunk(NTC - 1)

        if len(out_done) < NTC:
            S.alias(K_O, ALLR)
            for t in range(16):
                if t // 4 not in out_done:
                    out_tile(t)
        S.wait_all("sp", [("yout", 0), ("yout", 1), "dbgout", "dbgout2"])

        with nc.Block() as block:
            @block.tensor
            def _(e):
                for f in S.prog["pe"]:
                    f(e)

            @block.scalar
            def _(e):
                for f in S.prog["act"]:
                    f(e)

            @block.vector
            def _(e):
                for f in S.prog["dve"]:
                    f(e)

            @block.gpsimd
            def _(e):
                for f in S.prog["pool"]:
                    f(e)

            @block.sync
            def _(e):
                for f in S.prog["sp"]:
                    f(e)
        build.stats = {k: len(v) for k, v in S.prog.items()}
        build.stats["waits"] = S.nwait
        build.stats["sems"] = len(S.sems)
    return nc


def wview_of(rg, k, n):
    return rg[:, 0:k * n].rearrange("p (k n) -> p k n", k=k)


def _pack_vecs(inp):
    v = np.zeros((128, 2 * NVL), np.float32)
    for l in range(2):
        b = l * NVL
        v[:, b + 0:b + 8] = np.asarray(inp["norm_mix_g"][l]).reshape(8, 128).T
        v[:, b + 8:b + 16] = np.asarray(inp["norm_ffn_g"][l]).reshape(8, 128).T
        v[:, b + 16:b + 24] = np.asarray(inp["norm_ple_g"][l]).reshape(8, 128).T
        v[:, b + 24] = np.tile(np.asarray(inp["qnorm_a_g"][l]), 2)
        v[:, b + 25] = np.tile(np.asarray(inp["knorm_a_g"][l]), 2)
        v[:, b + 26] = np.tile(np.asarray(inp["qnorm_b_g"][l]), 2)
        v[:, b + 27] = np.tile(np.asarray(inp["knorm_b_g"][l]), 2)
        v[:, b + 28:b + 36] = np.broadcast_to(np.asarray(inp["sink_b"][l])[None, :], (128, 8))
    return v


_NC_CACHE = {}


def make_in_maps(inp, n_cores=8):
    f = lambda a: np.ascontiguousarray(np.asarray(a, dtype=np.float32))
    shared = {k: f(inp[k]) for k in ("w_in", "w_branch_a", "w_branch_b", "w_out", "w_ffn_gate", "w_ffn_up",
                                     "w_ffn_down", "w_ple_gate", "w_ple_proj", "rel_table")}
    shared["vecs"] = _pack_vecs(inp)
    shared["cst"] = _cst()
    shared["emat"] = _emat()
    x = f(inp["x"])
    p = f(inp["p"])
    maps = []
    for b in range(n_cores):
        m = dict(shared)
        m["x"] = np.ascontiguousarray(x[b])
        m["p"] = np.ascontiguousarray(p[:, b])
        maps.append(m)
    return maps


def kernel(**inputs):
    if "nc" not in _NC_CACHE:
        _NC_CACHE["nc"] = build(2)
    nc = _NC_CACHE["nc"]
    in_maps = make_in_maps(inputs, 8)
    res = run_bass_kernel_spmd(nc, in_maps, core_ids=list(range(8)))
    out = np.stack([np.asarray(r["y"], dtype=np.float32) for r in res.results], axis=0)
    return out
```
